# Optimizing a Trainium2 kernel written in Bass

```python
import jax, jax.numpy as jnp
from jax import lax
import numpy as np

D_MODEL = 1024
BATCH = 8
SEQ = 4096
DEPTH = 2
DEC_BATCH = 8
DEC_SEQ = 16
PAST_LEN = 2048

CHUNK = 64
EPS = 1e-6
LRU_WIDTH = D_MODEL
LRU_BLOCKS = 8
LRU_BLOCK = LRU_WIDTH // LRU_BLOCKS
CONV_W = 4
LRU_C = 8.0
M_HEADS = 4
M_HEAD_DIM = D_MODEL // M_HEADS
M_WIDTH = M_HEADS * M_HEAD_DIM
A_HEAD_DIM = 64
A_HEADS = D_MODEL // A_HEAD_DIM
A_KV_HEADS = 2
A_GROUPS = A_HEADS // A_KV_HEADS
A_WIDTH = A_HEADS * A_HEAD_DIM
A_KV_WIDTH = A_KV_HEADS * A_HEAD_DIM
WINDOW = 128
ROPE_THETA = 10000.0
D_FF = 4 * D_MODEL
IN_SIZES = (LRU_WIDTH, LRU_WIDTH,
            M_WIDTH, M_WIDTH, M_WIDTH, M_WIDTH,
            M_HEADS, M_HEADS,
            A_WIDTH, A_KV_WIDTH, A_KV_WIDTH,
            D_MODEL, D_MODEL, D_MODEL)
IN_SPLITS = tuple(sum(IN_SIZES[:i + 1]) for i in range(len(IN_SIZES) - 1))
IN_COLS = sum(IN_SIZES)

kernel_name = "hybrid_lru_mlstm_swa_stream_step"


def rms_norm(x, g):
    xf = x.astype(jnp.float32)
    y = xf * lax.rsqrt(jnp.mean(xf * xf, axis=-1, keepdims=True) + EPS)
    return (y * g.astype(jnp.float32)).astype(x.dtype)


def rope(x, pos):
    half = x.shape[-1] // 2
    inv = ROPE_THETA ** (-jnp.arange(half, dtype=jnp.float32) / half)
    ang = pos.astype(jnp.float32)[:, None] * inv[None, :]
    cos = jnp.cos(ang)[:, None, :]
    sin = jnp.sin(ang)[:, None, :]
    xf = x.astype(jnp.float32)
    x1, x2 = xf[..., :half], xf[..., half:]
    return jnp.concatenate([x1 * cos - x2 * sin, x2 * cos + x1 * sin], axis=-1).astype(x.dtype)


def rg_lru(x, h0, w_a, b_a, w_x, b_x, lam):
    B, S, W = x.shape
    xf = x.astype(jnp.float32)
    xb = xf.reshape(B, S, LRU_BLOCKS, LRU_BLOCK)
    r = jax.nn.sigmoid(jnp.einsum('bsnc,ncd->bsnd', xb, w_a.astype(jnp.float32)).reshape(B, S, W) + b_a)
    i = jax.nn.sigmoid(jnp.einsum('bsnc,ncd->bsnd', xb, w_x.astype(jnp.float32)).reshape(B, S, W) + b_x)
    log_a = -LRU_C * r * jax.nn.softplus(-lam.astype(jnp.float32))
    a = jnp.exp(log_a)
    u = jnp.sqrt(-jnp.expm1(2.0 * log_a)) * (i * xf)

    def combine(left, right):
        a_l, b_l = left
        a_r, b_r = right
        return a_l * a_r, a_r * b_l + b_r

    a_cum, h = lax.associative_scan(combine, (a, u), axis=1)
    h = h + a_cum * h0.astype(jnp.float32)[:, None, :]
    return h.astype(x.dtype), h[:, -1]


def mlstm_chunk(carry, inp):
    C, n, m = carry
    q, k, v, ig, lf = inp
    L = q.shape[2]
    b = jnp.cumsum(lf, axis=-1)
    causal = jnp.tril(jnp.ones((L, L), dtype=bool))
    log_d = jnp.where(causal, b[..., :, None] - b[..., None, :] + ig[..., None, :], -jnp.inf)
    inter = b + m[..., None]
    m_t = jnp.maximum(inter, jnp.max(log_d, axis=-1))
    w = jnp.einsum('bhtd,bhsd->bhts', q, k) * jnp.exp(log_d - m_t[..., None])
    inter_w = jnp.exp(inter - m_t)
    num = jnp.einsum('bhts,bhsd->bhtd', w, v) + inter_w[..., None] * jnp.einsum('bhtd,bhde->bhte', q, C)
    den = jnp.sum(w, axis=-1) + inter_w * jnp.einsum('bhtd,bhd->bht', q, n)
    h = num / jnp.maximum(jnp.abs(den), jnp.exp(-m_t))[..., None]
    b_last = b[..., -1]
    tail = b_last[..., None] - b + ig
    m_new = jnp.maximum(b_last + m, jnp.max(tail, axis=-1))
    wk = jnp.exp(tail - m_new[..., None])
    decay = jnp.exp(b_last + m - m_new)
    C_new = decay[..., None, None] * C + jnp.einsum('bhs,bhsd,bhse->bhde', wk, k, v)
    n_new = decay[..., None] * n + jnp.einsum('bhs,bhsd->bhd', wk, k)
    return (C_new, n_new, m_new), h


def mlstm_seq(q, k, v, ig, lf, C0, n0, m0):
    B, S, H, d = q.shape
    L = min(CHUNK, S)
    nc = S // L

    def blocks(t):
        t = t.astype(jnp.float32).reshape((B, nc, L, H) + t.shape[3:])
        return jnp.moveaxis(jnp.moveaxis(t, 1, 0), 2, 3)

    carry0 = (C0.astype(jnp.float32), n0.astype(jnp.float32), m0.astype(jnp.float32))
    carry, h = lax.scan(mlstm_chunk, carry0, (blocks(q), blocks(k), blocks(v), blocks(ig), blocks(lf)))
    h = jnp.moveaxis(jnp.moveaxis(h, 3, 2), 0, 1).reshape(B, S, H, d)
    return h, carry


def banded_sink_attention(q, k_all, v_all, P, buf_valid, sinks):
    B, S = q.shape[0], q.shape[1]
    Lq = min(CHUNK, S)
    nq = S // Lq
    idx = (jnp.arange(nq) * Lq)[:, None] + jnp.arange(P + Lq)[None, :]
    kb = k_all[:, idx].astype(jnp.float32)
    vb = v_all[:, idx].astype(jnp.float32)
    valid = jnp.logical_or(idx >= P, buf_valid)
    qb = q.reshape(B, nq, Lq, A_KV_HEADS, A_GROUPS, A_HEAD_DIM).astype(jnp.float32)
    s = jnp.einsum('bnqhgd,bnkhd->bnhgqk', qb, kb) * (A_HEAD_DIM ** -0.5)
    s = jnp.where(valid[None, :, None, None, None, :], s, -jnp.inf)
    sink = sinks.astype(jnp.float32).reshape(1, 1, A_KV_HEADS, A_GROUPS, 1, 1)
    mx = jnp.maximum(jnp.max(s, axis=-1, keepdims=True), sink)
    p = jnp.exp(s - mx)
    p = p / (jnp.sum(p, axis=-1, keepdims=True) + jnp.exp(sink - mx))
    o = jnp.einsum('bnhgqk,bnkhd->bnqhgd', p, vb)
    return o.reshape(B, S, A_WIDTH).astype(q.dtype)


def hybrid_layer(x, pos, conv_buf, h0, C0, n0, m0, k_buf, v_buf, buf_valid,
                 norm1_g, w_in, conv_w, conv_b, lru_wa, lru_ba, lru_wx, lru_bx, lru_lam,
                 m_bi, m_bf, m_norm_g, qn_g, kn_g, sinks, w_oa, w_ob, w_oc, b_gate, w_out,
                 norm2_g, w_up, w_down):
    B, S, _ = x.shape
    u = rms_norm(x, norm1_g)
    z = u @ w_in
    (xa, ga, mq, mk, mv, mo, mi, mf, aq, ak, av, g_a, g_b, g_c) = jnp.split(z, IN_SPLITS, axis=-1)

    xpad = jnp.concatenate([conv_buf.astype(xa.dtype), xa], axis=1)
    xc = conv_b + xpad[:, 0:S] * conv_w[0]
    for j in range(1, CONV_W):
        xc = xc + xpad[:, j:j + S] * conv_w[j]
    conv_new = xpad[:, -(CONV_W - 1):]
    h_a, h_last = rg_lru(xc, h0, lru_wa, lru_ba, lru_wx, lru_bx, lru_lam)
    y_a = (h_a * jax.nn.gelu(ga)) @ w_oa

    q_m = mq.reshape(B, S, M_HEADS, M_HEAD_DIM)
    k_m = mk.reshape(B, S, M_HEADS, M_HEAD_DIM) * (M_HEAD_DIM ** -0.5)
    v_m = mv.reshape(B, S, M_HEADS, M_HEAD_DIM)
    ig = (mi + m_bi).astype(jnp.float32)
    lf = jax.nn.log_sigmoid((mf + m_bf).astype(jnp.float32))
    h_m, (C_new, n_new, m_new) = mlstm_seq(q_m, k_m, v_m, ig, lf, C0, n0, m0)
    h_m = rms_norm(h_m, m_norm_g.reshape(M_HEADS, M_HEAD_DIM)).reshape(B, S, M_WIDTH).astype(x.dtype)
    y_b = (h_m * jax.nn.sigmoid(mo)) @ w_ob

    q_c = rope(rms_norm(aq.reshape(B, S, A_HEADS, A_HEAD_DIM), qn_g), pos)
    k_c = rope(rms_norm(ak.reshape(B, S, A_KV_HEADS, A_HEAD_DIM), kn_g), pos)
    v_c = av.reshape(B, S, A_KV_HEADS, A_HEAD_DIM)
    P = k_buf.shape[1]
    k_all = jnp.concatenate([k_buf.astype(k_c.dtype), k_c], axis=1)
    v_all = jnp.concatenate([v_buf.astype(v_c.dtype), v_c], axis=1)
    y_c = banded_sink_attention(q_c, k_all, v_all, P, buf_valid, sinks) @ w_oc
    k_new = k_all[:, -P:]
    v_new = v_all[:, -P:]

    mix = (jax.nn.sigmoid(g_a + b_gate[0]) * y_a
           + jax.nn.sigmoid(g_b + b_gate[1]) * y_b
           + jax.nn.sigmoid(g_c + b_gate[2]) * y_c)
    x = x + mix @ w_out
    x = x + jnp.square(jax.nn.relu(rms_norm(x, norm2_g) @ w_up)) @ w_down
    return x, (conv_new, h_last, C_new, n_new, m_new, k_new, v_new)


def setup_inputs(seed: int = 0) -> dict:
    key = jax.random.key(seed)
    k = jax.random.split(key, 32)
    f32 = jnp.float32

    def nrm(kk, shape, scale):
        return jax.random.normal(kk, shape, f32) * scale

    win = min(WINDOW, PAST_LEN)
    u = jax.random.uniform(k[16], (DEPTH, LRU_WIDTH), f32, 0.9, 0.999)
    s = u ** (1.0 / LRU_C)
    return {
        'x_prompt': nrm(k[0], (BATCH, SEQ, D_MODEL), 1.0),
        'x_sample': nrm(k[1], (DEC_BATCH, DEC_SEQ, D_MODEL), 1.0),
        'state_conv': nrm(k[2], (DEPTH, DEC_BATCH, CONV_W - 1, LRU_WIDTH), 1.0),
        'state_lru': nrm(k[3], (DEPTH, DEC_BATCH, LRU_WIDTH), 0.5),
        'state_mlstm_C': nrm(k[4], (DEPTH, DEC_BATCH, M_HEADS, M_HEAD_DIM, M_HEAD_DIM), 0.05),
        'state_mlstm_n': nrm(k[5], (DEPTH, DEC_BATCH, M_HEADS, M_HEAD_DIM), 0.1),
        'state_mlstm_m': nrm(k[6], (DEPTH, DEC_BATCH, M_HEADS), 1.0),
        'cache_k': nrm(k[7], (DEPTH, DEC_BATCH, win, A_KV_HEADS, A_HEAD_DIM), 1.0),
        'cache_v': nrm(k[8], (DEPTH, DEC_BATCH, win, A_KV_HEADS, A_HEAD_DIM), 1.0),
        'norm1_g': 1.0 + nrm(k[9], (DEPTH, D_MODEL), 0.02),
        'w_in': nrm(k[10], (DEPTH, D_MODEL, IN_COLS), D_MODEL ** -0.5),
        'conv_w': nrm(k[11], (DEPTH, CONV_W, LRU_WIDTH), CONV_W ** -0.5),
        'conv_b': nrm(k[12], (DEPTH, LRU_WIDTH), 0.02),
        'lru_wa': nrm(k[13], (DEPTH, LRU_BLOCKS, LRU_BLOCK, LRU_BLOCK), LRU_BLOCK ** -0.5),
        'lru_ba': nrm(k[14], (DEPTH, LRU_WIDTH), 0.02),
        'lru_wx': nrm(k[15], (DEPTH, LRU_BLOCKS, LRU_BLOCK, LRU_BLOCK), LRU_BLOCK ** -0.5),
        'lru_bx': nrm(k[17], (DEPTH, LRU_WIDTH), 0.02),
        'lru_lam': jnp.log(s) - jnp.log1p(-s),
        'm_bi': nrm(k[18], (DEPTH, M_HEADS), 0.1),
        'm_bf': jnp.linspace(3.0, 6.0, M_HEADS, dtype=f32)[None, :] + nrm(k[19], (DEPTH, M_HEADS), 0.1),
        'm_norm_g': 1.0 + nrm(k[20], (DEPTH, M_WIDTH), 0.02),
        'qn_g': 1.0 + nrm(k[21], (DEPTH, A_HEAD_DIM), 0.02),
        'kn_g': 1.0 + nrm(k[22], (DEPTH, A_HEAD_DIM), 0.02),
        'sinks': nrm(k[23], (DEPTH, A_HEADS), 0.5),
        'w_oa': nrm(k[24], (DEPTH, LRU_WIDTH, D_MODEL), LRU_WIDTH ** -0.5),
        'w_ob': nrm(k[25], (DEPTH, M_WIDTH, D_MODEL), M_WIDTH ** -0.5),
        'w_oc': nrm(k[26], (DEPTH, A_WIDTH, D_MODEL), A_WIDTH ** -0.5),
        'b_gate': nrm(k[27], (DEPTH, 3, D_MODEL), 0.02),
        'w_out': nrm(k[28], (DEPTH, D_MODEL, D_MODEL), D_MODEL ** -0.5),
        'norm2_g': 1.0 + nrm(k[29], (DEPTH, D_MODEL), 0.02),
        'w_up': nrm(k[30], (DEPTH, D_MODEL, D_FF), D_MODEL ** -0.5),
        'w_down': nrm(k[31], (DEPTH, D_FF, D_MODEL), D_FF ** -0.5),
    }


def reference(x_prompt, x_sample, state_conv, state_lru, state_mlstm_C, state_mlstm_n, state_mlstm_m,
              cache_k, cache_v, norm1_g, w_in, conv_w, conv_b, lru_wa, lru_ba, lru_wx, lru_bx, lru_lam,
              m_bi, m_bf, m_norm_g, qn_g, kn_g, sinks, w_oa, w_ob, w_oc, b_gate, w_out, norm2_g, w_up, w_down):
    B, S, _ = x_prompt.shape
    Sd = x_sample.shape[1]
    pos_p = jnp.arange(S, dtype=jnp.int32)
    pos_s = PAST_LEN + jnp.arange(Sd, dtype=jnp.int32)
    zc = jnp.zeros((B, CONV_W - 1, LRU_WIDTH), x_prompt.dtype)
    zh = jnp.zeros((B, LRU_WIDTH), jnp.float32)
    zC = jnp.zeros((B, M_HEADS, M_HEAD_DIM, M_HEAD_DIM), jnp.float32)
    zn = jnp.zeros((B, M_HEADS, M_HEAD_DIM), jnp.float32)
    zm = jnp.zeros((B, M_HEADS), jnp.float32)
    zkv = jnp.zeros((B, WINDOW, A_KV_HEADS, A_HEAD_DIM), x_prompt.dtype)

    y_p, y_s = x_prompt, x_sample
    new_p, new_s = [], []
    for l in range(DEPTH):
        w = (norm1_g[l], w_in[l], conv_w[l], conv_b[l], lru_wa[l], lru_ba[l], lru_wx[l], lru_bx[l],
             lru_lam[l], m_bi[l], m_bf[l], m_norm_g[l], qn_g[l], kn_g[l], sinks[l], w_oa[l], w_ob[l],
             w_oc[l], b_gate[l], w_out[l], norm2_g[l], w_up[l], w_down[l])
        y_p, sp = hybrid_layer(y_p, pos_p, zc, zh, zC, zn, zm, zkv, zkv, False, *w)
        y_s, ss = hybrid_layer(y_s, pos_s, state_conv[l], state_lru[l], state_mlstm_C[l],
                               state_mlstm_n[l], state_mlstm_m[l], cache_k[l], cache_v[l], True, *w)
        new_p.append(sp)
        new_s.append(ss)
    conv_p, lru_p, C_p, n_p, m_p, k_p, v_p = [jnp.stack(t) for t in zip(*new_p)]
    conv_s, lru_s, C_s, n_s, m_s, k_s, v_s = [jnp.stack(t) for t in zip(*new_s)]
    return (y_p, y_s, conv_p, lru_p, C_p, n_p, m_p, k_p, v_p, conv_s, lru_s, C_s, n_s, m_s, k_s, v_s)
```

```python
import numpy as np
from contextlib import ExitStack
import concourse.bass as bass
import concourse.mybir as mybir
from concourse.bass_utils import run_bass_kernel_spmd

F32 = mybir.dt.float32
BF16 = mybir.dt.bfloat16
AF = mybir.ActivationFunctionType
ALU = mybir.AluOpType
AX = mybir.AxisListType

D = 1024
EPS = 1e-6
IN_COLS = 10504
O_XA, O_GA, O_MQ, O_MK, O_MV, O_MO, O_MI, O_MF, O_AQ, O_AK, O_AV, O_G = (
    0, 1024, 2048, 3072, 4096, 5120, 6144, 6148, 6152, 7176, 7304, 7432)
PAST_LEN = 2048


class Sub:
    __slots__ = ("writer", "readers", "dma_readers")

    def __init__(self):
        self.writer = None
        self.readers = {}
        self.dma_readers = {}


class Buf:
    def __init__(self, name, ap, nsub=1):
        self.name = name
        self.ap = ap
        self.subs = [Sub() for _ in range(nsub)]
        self.dma_sem = None
        self.dma_count = 0

    def __getitem__(self, i):
        return (self, i)


def view(buf, ap):
    v = Buf(buf.name + "_v", ap, 0)
    v.subs = buf.subs
    return v


class ChunkView:
    def __init__(self, buf, ap, per):
        self.buf = buf
        self.ap = ap
        self.per = per
        self.subs = buf.subs

    def __getitem__(self, ch):
        return (self.buf, range(ch * self.per, (ch + 1) * self.per))


class Op:
    __slots__ = ("eng", "fn", "deps", "dma_waits", "signal", "is_dma", "buf", "count", "idx")

    def __init__(self, eng, fn):
        self.eng = eng
        self.fn = fn
        self.deps = []
        self.dma_waits = []
        self.signal = False
        self.is_dma = False
        self.buf = None
        self.count = None


def _subs(refs):
    out = []
    for r in refs:
        if isinstance(r, (Buf, ChunkView)):
            out.extend(r.subs)
        else:
            b, i = r
            if isinstance(i, (list, tuple, range)):
                out.extend(b.subs[j] for j in i)
            else:
                out.append(b.subs[i])
    return out


class Sched:
    ENGS = ["pe", "act", "dve", "pool", "sp"]

    def __init__(self, nc):
        self.nc = nc
        self.ops = {e: [] for e in self.ENGS}
        self.dma_bufs = []

    def add(self, eng, fn, reads=(), writes=(), dma_buf=None):
        op = Op(eng, fn)
        op.idx = len(self.ops[eng])
        need = {}
        dneed = {}

        def dep_on(d):
            if d is op:
                return
            if d.is_dma:
                dneed[id(d.buf)] = d.buf
            else:
                if d.eng == "pe" and eng == "pe":
                    return
                cur = need.get(d.eng)
                if cur is None or d.idx > cur.idx:
                    need[d.eng] = d

        rs = _subs(reads)
        ws = _subs(writes)
        for s in rs:
            if s.writer is not None:
                dep_on(s.writer)
        for s in ws:
            if s.writer is not None:
                dep_on(s.writer)
            for r in s.readers.values():
                dep_on(r)
            for b in s.dma_readers.values():
                dneed[id(b)] = b
        for d in need.values():
            op.deps.append(d)
            d.signal = True
        for b in dneed.values():
            op.dma_waits.append((b, b.dma_count))
        if dma_buf is not None:
            op.is_dma = True
            op.buf = dma_buf
            if dma_buf.dma_sem is None:
                dma_buf.dma_sem = "pending"
                self.dma_bufs.append(dma_buf)
            dma_buf.dma_count += 16
        for s in rs:
            if op.is_dma:
                s.dma_readers[id(dma_buf)] = dma_buf
            else:
                s.readers[eng] = op
        for s in ws:
            s.writer = op
            s.readers = {}
            s.dma_readers = {}
        self.ops[eng].append(op)
        return op

    def pe(self, fn, reads=(), writes=()):
        return self.add("pe", fn, reads, writes)

    def act(self, fn, reads=(), writes=()):
        return self.add("act", fn, reads, writes)

    def dve(self, fn, reads=(), writes=()):
        return self.add("dve", fn, reads, writes)

    def pool(self, fn, reads=(), writes=()):
        return self.add("pool", fn, reads, writes)

    def dma(self, eng, fn, buf, reads=(), writes=()):
        return self.add(eng, fn, reads, writes, dma_buf=buf)

    def emit(self):
        nc = self.nc
        with ExitStack() as st:
            esem = {}
            for e in ["pe", "act", "dve", "pool"]:
                esem[e] = st.enter_context(nc.semaphore("es_" + e))
            for b in self.dma_bufs:
                b.dma_sem = st.enter_context(nc.semaphore("ds_" + b.name))
            for e in ["pe", "act", "dve", "pool"]:
                c = 0
                for op in self.ops[e]:
                    if op.is_dma:
                        continue
                    if op.signal:
                        c += 1
                        op.count = c
            stats = {}
            block = st.enter_context(nc.Block())

            def run(e, engobj):
                waited = {}
                nw = 0
                for op in self.ops[e]:
                    for d in op.deps:
                        if waited.get(d.eng, 0) >= d.count:
                            continue
                        waited[d.eng] = d.count
                        engobj.wait_ge(esem[d.eng], d.count)
                        nw += 1
                    for (b, v) in op.dma_waits:
                        if waited.get(id(b), 0) >= v:
                            continue
                        waited[id(b)] = v
                        engobj.wait_ge(b.dma_sem, v)
                        nw += 1
                    inst = op.fn(engobj)
                    if op.is_dma:
                        inst.then_inc(op.buf.dma_sem, 16)
                    elif op.signal:
                        inst.then_inc(esem[e], 1)
                if e == "sp":
                    for ee in ["pe", "act", "dve", "pool"]:
                        last = 0
                        for op in self.ops[ee]:
                            if op.count:
                                last = op.count
                        if last:
                            engobj.wait_ge(esem[ee], last)
                    for b in self.dma_bufs:
                        engobj.wait_ge(b.dma_sem, b.dma_count)
                stats[e] = (len(self.ops[e]), nw)

            @block.tensor
            def _(eng):
                run("pe", eng)

            @block.scalar
            def _(eng):
                run("act", eng)

            @block.vector
            def _(eng):
                run("dve", eng)

            @block.gpsimd
            def _(eng):
                run("pool", eng)

            @block.sync
            def _(eng):
                run("sp", eng)

            return stats


def build(NT, T, DEPTH, SAMPLE, TS=16):
    nc = bass.Bass("TRN2", target_bir_lowering=False)
    S = Sched(nc)
    SP = NT * T
    NPOS = SP + TS

    def din(name, shape):
        return nc.dram_tensor(name, list(shape), F32, kind="ExternalInput").ap()

    def dout(name, shape):
        return nc.dram_tensor(name, list(shape), F32, kind="ExternalOutput").ap()

    x_p = din("x_p", [SP, D])
    x_s = din("x_s", [TS, D])
    st_conv = din("st_conv", [DEPTH, 3, D])
    st_lru = din("st_lru", [DEPTH, D])
    st_C = din("st_C", [DEPTH, 4, 256, 256])
    st_n = din("st_n", [DEPTH, 4, 256])
    st_m = din("st_m", [DEPTH, 4])
    c_k = din("c_k", [DEPTH, 128, 2, 64])
    c_v = din("c_v", [DEPTH, 128, 2, 64])
    rope_c = din("rope_c", [NPOS, 32])
    rope_s = din("rope_s", [NPOS, 32])
    W = {}
    for nm, shp in [("norm1_g", [DEPTH, D]), ("w_in", [DEPTH, D, IN_COLS]), ("conv_w", [DEPTH, 4, D]),
                    ("conv_b", [DEPTH, D]), ("lru_wa", [DEPTH, 8, 128, 128]), ("lru_ba", [DEPTH, D]),
                    ("lru_wx", [DEPTH, 8, 128, 128]), ("lru_bx", [DEPTH, D]), ("lru_lam", [DEPTH, D]),
                    ("m_bi", [DEPTH, 4]), ("m_bf", [DEPTH, 4]), ("m_norm_g", [DEPTH, D]), ("qn_g", [DEPTH, 64]),
                    ("kn_g", [DEPTH, 64]), ("sinks", [DEPTH, 16]), ("w_oa", [DEPTH, D, D]), ("w_ob", [DEPTH, D, D]),
                    ("w_oc", [DEPTH, D, D]), ("b_gate", [DEPTH, 3, D]), ("w_out", [DEPTH, D, D]),
                    ("norm2_g", [DEPTH, D]), ("w_up", [DEPTH, D, 4096]), ("w_down", [DEPTH, 4096, D])]:
        W[nm] = din(nm, shp)
    O = {}
    O["y_p"] = dout("y_p", [SP, D])
    O["y_s"] = dout("y_s", [TS, D])
    for g in ["p", "s"]:
        O["conv_" + g] = dout("conv_" + g, [DEPTH, 3, D])
        O["lru_" + g] = dout("lru_" + g, [DEPTH, D])
        O["C_" + g] = dout("C_" + g, [DEPTH, 4, 256, 256])
        O["n_" + g] = dout("n_" + g, [DEPTH, 4, 256])
        O["m_" + g] = dout("m_" + g, [DEPTH, 4])
        O["k_" + g] = dout("k_" + g, [DEPTH, 128, 2, 64])
        O["v_" + g] = dout("v_" + g, [DEPTH, 128, 2, 64])

    cnt = [0]

    def sb(shape, dt=F32, nsub=1, name=None):
        cnt[0] += 1
        nm = (name or "t") + "_%d" % cnt[0]
        t = nc.alloc_sbuf_tensor(nm, list(shape), dt)
        return Buf(nm, t.ap(), nsub)

    banks = []
    for i in range(8):
        t = nc.alloc_psum_tensor("bank%d" % i, [128, 512], F32)
        banks.append(Buf("bank%d" % i, t.ap(), 5))
    mm_ring = [0]
    NMM = 2

    def ps_mm():
        b = banks[mm_ring[0] % NMM]
        mm_ring[0] += 1
        return b

    def aux(role):
        return banks[NMM + role]

    ev = [0]

    def evac(fn_act, fn_dve, reads, writes):
        ev[0] += 1
        if ev[0] % 2 == 0:
            return S.act(fn_act, reads, writes)
        return S.dve(fn_dve, reads, writes)

    def copy_any(out_ap, in_ap, reads, writes, scale=None):
        if scale is None:
            return evac(lambda e: e.activation(out=out_ap, in_=in_ap, func=AF.Copy),
                        lambda e: e.tensor_copy(out=out_ap, in_=in_ap), reads, writes)
        return evac(lambda e: e.activation(out=out_ap, in_=in_ap, func=AF.Copy, scale=scale),
                    lambda e: e.tensor_scalar(out=out_ap, in0=in_ap, scalar1=scale, scalar2=None, op0=ALU.mult),
                    reads, writes)

    ident = sb([128, 128], F32, name="ident")
    identb = sb([128, 128], BF16, name="identb")
    ones_bf = sb([128, 128], BF16, name="onesbf")
    ones32 = sb([4, 128], F32, name="ones32")
    cmask = sb([64, 64], F32, name="cmask")
    hmask = sb([4, 4, 8], F32, name="hmask")
    maskrow = sb([4, 512], F32, name="maskrow")
    S.pool(lambda e: e.memset(ident.ap, 0.0), writes=[ident])
    S.pool(lambda e: e.affine_select(out=ident.ap, in_=ident.ap, pattern=[[-1, 128]], compare_op=ALU.not_equal,
                                     fill=1.0, base=0, channel_multiplier=1), reads=[ident], writes=[ident])
    S.dve(lambda e: e.tensor_copy(out=identb.ap, in_=ident.ap), reads=[ident], writes=[identb])
    S.dve(lambda e: e.memset(ones_bf.ap, 1.0), writes=[ones_bf])
    S.dve(lambda e: e.memset(ones32.ap, 1.0), writes=[ones32])
    S.pool(lambda e: e.memset(cmask.ap, 1.0), writes=[cmask])
    S.pool(lambda e: e.affine_select(out=cmask.ap, in_=cmask.ap, pattern=[[1, 64]], compare_op=ALU.is_ge,
                                     fill=0.0, base=0, channel_multiplier=-1), reads=[cmask], writes=[cmask])
    S.pool(lambda e: e.memset(hmask.ap, 1.0), writes=[hmask])
    S.pool(lambda e: e.affine_select(out=hmask.ap, in_=hmask.ap, pattern=[[1, 4], [0, 8]], compare_op=ALU.is_equal,
                                     fill=0.0, base=0, channel_multiplier=-1), reads=[hmask], writes=[hmask])
    S.dve(lambda e: e.memset(maskrow.ap, 1.0), writes=[maskrow])
    S.dve(lambda e: e.memset(maskrow.ap.rearrange("p (c l) -> p c l", l=64)[:, :, 0:1], 0.0), writes=[maskrow])

    VEC = ["norm1_g", "norm2_g", "conv_b", "lru_ba", "lru_bx", "lru_lam", "cw0", "cw1", "cw2", "cw3", "bg0", "bg1", "bg2"]
    NV = len(VEC)
    colv = [sb([128, NV * 8], F32, name="colv") for _ in range(DEPTH)]
    nsp8 = [sb([128, 8], F32, name="nsp8") for _ in range(DEPTH)]
    nsp4 = [sb([128, 8], F32, name="nsp4") for _ in range(DEPTH)]
    hb = [sb([128, 16], F32, name="hb") for _ in range(DEPTH)]
    wa_bf = [sb([128, 8, 128], BF16, name="wa") for _ in range(DEPTH)]
    wx_bf = [sb([128, 8, 128], BF16, name="wx") for _ in range(DEPTH)]
    mg_row1 = sb([64, D], F32, name="mgrow")
    mg_row = [mg_row1 for _ in range(DEPTH)]
    qg_row = [sb([128, 64], F32, name="qgrow") for _ in range(DEPTH)]
    kg_row = [sb([64, 64], F32, name="kgrow") for _ in range(DEPTH)]
    esink = [sb([128, 16], F32, name="esink") for _ in range(DEPTH)]
    bi_col = [sb([4, 1], F32, name="bi") for _ in range(DEPTH)]
    nbf_col = [sb([4, 1], F32, name="nbf") for _ in range(DEPTH)]
    vstage = sb([128, 128], F32, name="vstage")

    def vcol(l, name, c):
        i = VEC.index(name)
        return colv[l].ap[:, i * 8 + c:i * 8 + c + 1]

    for l in range(DEPTH):
        srcs = {"norm1_g": W["norm1_g"][l], "norm2_g": W["norm2_g"][l], "conv_b": W["conv_b"][l],
                "lru_ba": W["lru_ba"][l], "lru_bx": W["lru_bx"][l], "lru_lam": W["lru_lam"][l]}
        for j in range(4):
            srcs["cw%d" % j] = W["conv_w"][l, j]
        for j in range(3):
            srcs["bg%d" % j] = W["b_gate"][l, j]
        for i, nm in enumerate(VEC):
            src = srcs[nm].rearrange("(c p) -> c p", p=128)
            S.dma("sp", lambda e, i=i, src=src: e.dma_start(out=vstage.ap[i * 8:(i + 1) * 8, :], in_=src), vstage,
                  writes=[vstage])
        pb = aux(5)
        S.pe(lambda e, pb=pb: e.transpose(pb.ap[:, 0:NV * 8], vstage.ap[0:NV * 8, :], ident.ap[0:NV * 8, 0:NV * 8]),
             reads=[vstage, ident], writes=[pb])
        S.dve(lambda e, pb=pb, l=l: e.tensor_copy(out=colv[l].ap, in_=pb.ap[:, 0:NV * 8]), reads=[pb], writes=[colv[l]])
        lam = colv[l].ap[:, VEC.index("lru_lam") * 8:VEC.index("lru_lam") * 8 + 8]
        S.act(lambda e, l=l, lam=lam: e.activation(out=nsp8[l].ap, in_=lam, func=AF.Exp, scale=-1.0),
              reads=[colv[l]], writes=[nsp8[l]])
        S.act(lambda e, l=l: e.activation(out=nsp8[l].ap, in_=nsp8[l].ap, func=AF.Ln, bias=1.0),
              reads=[nsp8[l]], writes=[nsp8[l]])
        S.dve(lambda e, l=l: e.tensor_scalar(out=nsp4[l].ap, in0=nsp8[l].ap, scalar1=-4.0, scalar2=None, op0=ALU.mult),
              reads=[nsp8[l]], writes=[nsp4[l]])
        S.dve(lambda e, l=l: e.tensor_scalar(out=nsp8[l].ap, in0=nsp8[l].ap, scalar1=-8.0, scalar2=None, op0=ALU.mult),
              reads=[nsp8[l], nsp4[l]], writes=[nsp8[l]])
        _ib = VEC.index("lru_ba") * 8
        S.dve(lambda e, l=l, _ib=_ib: e.tensor_scalar(out=hb[l].ap, in0=colv[l].ap[:, _ib:_ib + 16], scalar1=0.5, scalar2=None,
                                                      op0=ALU.mult), reads=[colv[l]], writes=[hb[l]])
        S.dma("pool", lambda e, l=l: e.dma_start(out=wa_bf[l].ap, in_=W["lru_wa"][l].rearrange("n c d -> c n d")),
              wa_bf[l], writes=[wa_bf[l]])
        S.dma("pool", lambda e, l=l: e.dma_start(out=wx_bf[l].ap, in_=W["lru_wx"][l].rearrange("n c d -> c n d")),
              wx_bf[l], writes=[wx_bf[l]])
        S.dma("sp", lambda e, l=l: e.dma_start(out=qg_row[l].ap, in_=W["qn_g"][l].partition_broadcast(128)),
              qg_row[l], writes=[qg_row[l]])
        S.dma("sp", lambda e, l=l: e.dma_start(out=kg_row[l].ap, in_=W["kn_g"][l].partition_broadcast(64)),
              kg_row[l], writes=[kg_row[l]])
        S.dma("sp", lambda e, l=l: e.dma_start(out=esink[l].ap, in_=W["sinks"][l].partition_broadcast(128)),
              esink[l], writes=[esink[l]])
        S.act(lambda e, l=l: e.activation(out=esink[l].ap, in_=esink[l].ap, func=AF.Exp), reads=[esink[l]],
              writes=[esink[l]])
        S.dma("sp", lambda e, l=l: e.dma_start(out=bi_col[l].ap, in_=W["m_bi"][l].rearrange("(h o) -> h o", o=1)),
              bi_col[l], writes=[bi_col[l]])
        S.dma("sp", lambda e, l=l: e.dma_start(out=nbf_col[l].ap, in_=W["m_bf"][l].rearrange("(h o) -> h o", o=1)),
              nbf_col[l], writes=[nbf_col[l]])
        S.dve(lambda e, l=l: e.tensor_scalar(out=nbf_col[l].ap, in0=nbf_col[l].ap, scalar1=-1.0, scalar2=None,
                                             op0=ALU.mult), reads=[nbf_col[l]], writes=[nbf_col[l]])

    SCR = {}
    SCRB = {}
    for nm, R_, C_ in [("w_in", D, IN_COLS), ("w_oa", D, D), ("w_ob", D, D), ("w_oc", D, D), ("w_out", D, D),
                       ("w_up", D, 4096), ("w_down", 4096, D)]:
        SCR[nm] = nc.dram_tensor(nm + "_bf", [DEPTH, R_, C_], BF16, kind="Internal").ap()
    WIN_GROUPS = [(0, 2048), (2048, 4096), (O_G, O_G + 1024), (4096, O_AQ), (O_G + 1024, O_G + 2048), (O_AQ, O_G),
                  (O_G + 2048, IN_COLS)]
    for l in range(DEPTH):
        for gi_, (g0, g1) in enumerate(WIN_GROUPS):
            b_ = Buf("w_in_bf%d_%d" % (l, gi_), None, 2)
            SCRB[("w_in", l, gi_)] = b_
            for rb in range(2):
                S.dma("pool", lambda e, l=l, rb=rb, g0=g0, g1=g1: e.dma_start(out=SCR["w_in"][l, rb * 512:(rb + 1) * 512, g0:g1],
                                                                              in_=W["w_in"][l, rb * 512:(rb + 1) * 512, g0:g1]),
                      b_, writes=[b_[rb]])
        for nm in ["w_oa", "w_ob", "w_oc", "w_out", "w_up", "w_down"]:
            R_ = W[nm].shape[1]
            nblk = R_ // 256
            b_ = Buf("%s_bf%d" % (nm, l), None, nblk)
            SCRB[(nm, l)] = b_
            for rb in range(nblk):
                S.dma("pool", lambda e, nm=nm, l=l, rb=rb: e.dma_start(out=SCR[nm][l, rb * 256:(rb + 1) * 256, :],
                                                                       in_=W[nm][l, rb * 256:(rb + 1) * 256, :]),
                      b_, writes=[b_[rb]])

    NCHM = max(T // 64, 1)
    hist = [sb([128, 3, 8], F32, name="hist") for _ in range(DEPTH)]
    hst = [sb([128, 8], F32, name="hst") for _ in range(DEPTH)]
    Cst = [sb([128, 4, 2, 256], F32, nsub=4, name="Cst") for _ in range(DEPTH)]
    nst = [sb([128, 4, 2], F32, name="nst") for _ in range(DEPTH)]
    mst = [sb([4, 1], F32, name="mst") for _ in range(DEPTH)]
    NSLOT = 2 + NCHM
    KTwin = [sb([64, 2, NSLOT * 64], BF16, name="KTwin") for _ in range(DEPTH)]
    vaug = [sb([64, NSLOT, 2, 192], BF16, name="vaug") for _ in range(DEPTH)]

    xT = sb([128, 8, T], F32, nsub=8, name="xT")
    uT = sb([128, 8, T], BF16, nsub=8, name="uT")
    NSLAB = 4
    slabs = [sb([128, 4096], BF16, name="slab") for _ in range(NSLAB)]
    slab_i = [0]
    TB = min(T, 128)
    xin = [sb([128, D], F32, name="xin") for _ in range(2)]
    sqb = [sb([128, T], BF16, name="sqb") for _ in range(2)]
    rstd = sb([128, T], F32, name="rstd")
    xa_w = [sb([128, 3 + T], F32, name="xaw") for _ in range(2)]
    xc3 = [sb([128, T], F32, name="xc") for _ in range(3)]
    xcb = [sb([128, T], BF16, name="xcb") for _ in range(2)]
    rr = [sb([128, T], F32, name="rr") for _ in range(2)]
    ii = [sb([128, T], F32, name="ii") for _ in range(2)]
    aa = [sb([128, T], F32, name="aa") for _ in range(2)]
    sq1 = [sb([128, T], F32, name="sq1") for _ in range(2)]
    hh = [sb([128, T], F32, name="hh") for _ in range(2)]
    gel3 = [sb([128, T], F32, name="gel") for _ in range(3)]
    gel = gel3
    hgT = sb([128, 8, T], BF16, nsub=8, name="hgT")
    sg = [gel[0], gel[1]]
    tmpm = [hh[0], hh[1]]
    mix = sb([128, 8, T], F32, nsub=8, name="mix")
    qT = sb([128, 8, T], BF16, nsub=8, name="qT")
    kT = sb([128, 8, T], BF16, nsub=8, name="kT")
    g64 = [sb([64, 16, T], BF16, nsub=16, name="g64") for _ in range(4)]
    PER = 16 // NCHM

    def cview(g, vw4=False):
        flat = g.ap.rearrange("p h t -> p (h t)")
        if vw4:
            return ChunkView(g, flat.rearrange("p (c h e) -> p c h e", c=NCHM, h=4), PER)
        return ChunkView(g, flat.rearrange("p (c f) -> p c f", c=NCHM), PER)

    ktok = cview(g64[0])
    vw = cview(g64[1], True)
    sgm = cview(g64[2])
    hmtok = cview(g64[3])
    hmT = sb([128, 8, T], BF16, nsub=8, name="hmT")
    _gsrc = [rr[0], rr[1], ii[0], ii[1], aa[0], aa[1], sq1[0], sq1[1]]
    grow = {nm: view(_gsrc[i], _gsrc[i].ap[0:4, :]) for i, nm in enumerate(["ig", "sp", "b", "a", "ea", "cl", "iwt", "tmp"])}
    gsm = {nm: sb([4, NCHM], F32, name="gs_" + nm) for nm in ["amax", "d0", "M", "mnew", "mprev", "iw"]}
    rhsm = sb([4, 4, NCHM], F32, name="rhsm")
    iw_rep = sb([128, 4 * NCHM], F32, name="iwrep")
    colq = sb([64, NCHM, 12], F32, name="colq")
    eab = sb([64, NCHM, 4], BF16, name="eab")
    Cnb = [sb([128, 2, 256], BF16, name="Cnb") for _ in range(2)]
    nb = [sb([128, 2], BF16, name="nb") for _ in range(4)]
    nb4 = nb
    smask = [sb([64, 4, 64], BF16, name="smask") for _ in range(2)]
    den_s = [sb([64, 4], F32, name="dens") for _ in range(2)]
    den_t = [sb([64, 4], F32, name="dent") for _ in range(2)]
    hn = [view(xin[i], xin[i].ap[0:64, :].rearrange("p (h d) -> p h d", h=4)) for i in range(2)]
    ssm = [sb([64, 4], F32, name="ssm") for _ in range(2)]
    NBLK = (T + TB - 1) // TB
    ctab_k = sb([64, NCHM, 32], F32, name="ctabk")
    stab_k = sb([64, NCHM, 32], F32, name="stabk")
    ctab_q = sb([128, NBLK, 32], F32, name="ctabq")
    stab_q = sb([128, NBLK, 32], F32, name="stabq")
    QT_all = g64[0]
    OT2 = hmT
    qsq = sb([128, 512], F32, name="qsq")
    qss = [sb([128, 8], F32, name="qss") for _ in range(2)]
    qn = [sb([128, 8, 64], F32, name="qn") for _ in range(2)]
    qt1 = sb([128, 8, 32], F32, name="qt1")
    qt2 = sb([128, 8, 32], F32, name="qt2")
    qr = [sb([128, 8, 64], BF16, name="qr") for _ in range(2)]

    def _as_cnb(b_):
        return view(b_, b_.ap.rearrange("p a b -> p (a b)").bitcast(BF16).rearrange("p (c e) -> p c e", c=2))

    Cnb4 = [Cnb[0], Cnb[1], _as_cnb(qt1), _as_cnb(qt2)]
    kss = [sb([64, 2], F32, name="kss") for _ in range(2)]
    ksq = sb([64, 128], F32, name="ksq")
    kn = [sb([64, 2, 64], F32, name="kn") for _ in range(2)]
    kt1 = sb([64, 2, 32], F32, name="kt1")
    kt2 = sb([64, 2, 32], F32, name="kt2")
    kr = [sb([64, 128], F32, name="kr") for _ in range(2)]
    krb = [sb([64, 128], BF16, name="krb") for _ in range(2)]
    vf = [sb([64, 128], F32, name="vf") for _ in range(2)]
    pT = [sb([64, 512], BF16, name="pT") for _ in range(6)]
    pT_i = [0]
    dsum = [sb([128, 512], F32, name="dsum") for _ in range(2)]
    hsq = view(dsum[0], dsum[0].ap[0:64, :].bitcast(BF16).rearrange("p (h d) -> p h d", h=4))
    hid_parts = [hgT, qT, kT, hmT]
    rl = [rr[0], rr[1]]
    yout = xin
    kcs = view(qsq, qsq.ap[0:64, 0:256].rearrange("p (s f) -> p s f", s=2))
    kcb = sb([64, 2, 128], BF16, name="kcb")
    tailk = vf[0]

    def pipeline(gens, newest_first):
        gens = list(gens)
        active = []
        i = 0
        while i < len(gens) or active:
            if i < len(gens):
                active.append(gens[i])
                i += 1
            order = list(reversed(active)) if newest_first else list(active)
            for g in order:
                try:
                    next(g)
                except StopIteration:
                    active.remove(g)

    slab_live = [False] * NSLAB

    def release(sl):
        slab_live[slabs.index(sl)] = False

    def load_slab(src_ap, view, srcbuf, hold=False):
        for _ in range(NSLAB + 1):
            i_ = slab_i[0] % NSLAB
            slab_i[0] += 1
            if not slab_live[i_]:
                break
        else:
            raise RuntimeError("all slabs live")
        sl = slabs[i_]
        if hold:
            slab_live[i_] = True
        dst = view(sl.ap)
        S.dma("sp", lambda e: e.dma_start(out=dst, in_=src_ap), sl, reads=[srcbuf], writes=[sl])
        return sl, dst

    def slab_k8(l, wname, c0, ncols, hold=False):
        src = SCR[wname][l][:, c0:c0 + ncols].rearrange("(kc p) n -> p kc n", p=128)
        if wname == "w_in":
            gi_ = [i for i, (g0, g1) in enumerate(WIN_GROUPS) if g0 <= c0 and c0 + ncols <= g1]
            assert len(gi_) == 1, (c0, ncols)
            sb_ = SCRB[("w_in", l, gi_[0])]
        else:
            sb_ = SCRB[(wname, l)]
        return load_slab(src, lambda a: a[:, 0:8 * ncols].rearrange("p (k n) -> p k n", k=8), sb_, hold=hold)

    def qk_proj_gen(l, Tt, L):
        NCH = Tt // L
        for half in range(2):
            sl, v = slab_k8(l, "w_in", O_MQ + half * 512, 512, hold=True)
            for c4 in range(4):
                c = half * 4 + c4
                pb = ps_mm()
                fm_proj(pb, sl, v, c4 * 128, uT, Tt)
                S.dve(lambda e, pb=pb, c=c: e.tensor_copy(out=qT.ap[:, c, 0:Tt], in_=pb.ap[:, 0:Tt]), reads=[pb], writes=[qT[c]])
                yield
            release(sl)
        for half in range(2):
            sl, v = slab_k8(l, "w_in", O_MK + half * 512, 512, hold=True)
            for c4 in range(4):
                c = half * 4 + c4
                pb = ps_mm()
                fm_proj(pb, sl, v, c4 * 128, uT, Tt)
                S.dve(lambda e, pb=pb, c=c: e.tensor_scalar(out=kT.ap[:, c, 0:Tt], in0=pb.ap[:, 0:Tt], scalar1=1.0 / 16.0,
                                                            scalar2=None, op0=ALU.mult), reads=[pb], writes=[kT[c]])
                yield
            for ch in range(NCH):
                pb = ps_mm()
                for k in range(8):
                    S.pe(lambda e, k=k, pb=pb, ch=ch, v=v: e.matmul(pb.ap[0:L, :], uT.ap[:, k, ch * L:(ch + 1) * L], v[:, k, :],
                                                                    start=(k == 0), stop=(k == 7)),
                         reads=[sl, uT[k]], writes=[pb])
                S.dve(lambda e, pb=pb, ch=ch, half=half: e.tensor_scalar(out=ktok.ap[0:L, ch, half * 512:(half + 1) * 512],
                                                                         in0=pb.ap[0:L, :], scalar1=1.0 / 16.0, scalar2=None,
                                                                         op0=ALU.mult), reads=[pb], writes=[ktok[ch]])
                yield
            release(sl)

    def fm_proj(pb, sl, sview, col, act, Tt, KC=8):
        for k in range(KC):
            S.pe(lambda e, k=k: e.matmul(pb.ap[:, 0:Tt], sview[:, k, col:col + 128], act.ap[:, k, 0:Tt],
                                         start=(k == 0), stop=(k == KC - 1)),
                 reads=[sl, act[k]], writes=[pb])

    def norm_to_u(l, gname, Tt):
        pb = aux(0)
        for c in range(8):
            q = sqb[c % 2]
            S.act(lambda e, c=c, q=q: e.activation(out=q.ap[:, 0:Tt], in_=xT.ap[:, c, 0:Tt], func=AF.Square),
                  reads=[xT[c]], writes=[q])
            S.pe(lambda e, c=c, q=q: e.matmul(pb.ap[:, 0:Tt], ones_bf.ap, q.ap[:, 0:Tt], start=(c == 0), stop=(c == 7)),
                 reads=[q, ones_bf], writes=[pb])
        S.act(lambda e: e.activation(out=rstd.ap[:, 0:Tt], in_=pb.ap[:, 0:Tt], func=AF.Ln, scale=1.0 / D, bias=EPS),
              reads=[pb], writes=[rstd])
        S.act(lambda e: e.activation(out=rstd.ap[:, 0:Tt], in_=rstd.ap[:, 0:Tt], func=AF.Exp, scale=-0.5), reads=[rstd], writes=[rstd])
        for c in range(8):
            S.dve(lambda e, c=c: e.scalar_tensor_tensor(out=uT.ap[:, c, 0:Tt], in0=xT.ap[:, c, 0:Tt],
                                                        scalar=vcol(l, gname, c), in1=rstd.ap[:, 0:Tt],
                                                        op0=ALU.mult, op1=ALU.mult),
                  reads=[xT[c], rstd, colv[l]], writes=[uT[c]])

    def merge_branch(l, br, featT, Tt, K64=False):
        wname = ["w_oa", "w_ob", "w_oc"][br]
        for half in range(2):
            slg, vg = slab_k8(l, "w_in", O_G + br * 1024 + half * 512, 512)
            if not K64:
                slo, vo = slab_k8(l, wname, half * 512, 512)
            for c4 in range(4):
                c = half * 4 + c4
                pg = ps_mm()
                fm_proj(pg, slg, vg, c4 * 128, uT, Tt)
                s_ = sg[c % 2]
                S.act(lambda e, pg=pg, s_=s_, c=c: e.activation(out=s_.ap[:, 0:Tt], in_=pg.ap[:, 0:Tt], func=AF.Sigmoid,
                                                                bias=vcol(l, "bg%d" % br, c)),
                      reads=[pg, colv[l]], writes=[s_])
                py = ps_mm()
                if not K64:
                    fm_proj(py, slo, vo, c4 * 128, featT, Tt)
                else:
                    if c4 % 2 == 0:
                        src = SCR["w_oc"][l][:, c * 128:c * 128 + 256].rearrange("(h d) n -> d h n", d=64)
                        slo, vo = load_slab(src, lambda a: a[0:64, :].rearrange("p (h n) -> p h n", h=16), SCRB[("w_oc", l)])
                    off = (c4 % 2) * 128
                    for h in range(16):
                        S.pe(lambda e, h=h, py=py, vo=vo, off=off: e.matmul(py.ap[:, 0:Tt], vo[:, h, off:off + 128],
                                                                            featT.ap[:, h, 0:Tt], start=(h == 0),
                                                                            stop=(h == 15)),
                             reads=[slo, featT[h]], writes=[py])
                if br == 0:
                    S.dve(lambda e, py=py, s_=s_, c=c: e.tensor_tensor(out=mix.ap[:, c, 0:Tt], in0=py.ap[:, 0:Tt],
                                                                       in1=s_.ap[:, 0:Tt], op=ALU.mult),
                          reads=[py, s_], writes=[mix[c]])
                else:
                    t_ = tmpm[c % 2]
                    S.dve(lambda e, py=py, s_=s_, t_=t_: e.tensor_tensor(out=t_.ap[:, 0:Tt], in0=py.ap[:, 0:Tt],
                                                                         in1=s_.ap[:, 0:Tt], op=ALU.mult),
                          reads=[py, s_], writes=[t_])
                    S.pool(lambda e, t_=t_, c=c: e.tensor_tensor(out=mix.ap[:, c, 0:Tt], in0=mix.ap[:, c, 0:Tt],
                                                                 in1=t_.ap[:, 0:Tt], op=ALU.add),
                           reads=[t_, mix[c]], writes=[mix[c]])

    def lru_phase(l, Tt):
        sl_ = {}

        def body(c):
            half, c4 = divmod(c, 4)
            if c4 == 0:
                if "a" in sl_:
                    release(sl_["a"][0])
                    release(sl_["g"][0])
                sl_["a"] = slab_k8(l, "w_in", O_XA + half * 512, 512, hold=True)
                sl_["g"] = slab_k8(l, "w_in", O_GA + half * 512, 512, hold=True)
            sla, va = sl_["a"]
            slg, vg = sl_["g"]
            j = c % 2
            j3 = c % 3
            pa = ps_mm()
            fm_proj(pa, sla, va, c4 * 128, uT, Tt)
            xw = xa_w[j]
            S.dve(lambda e: e.tensor_copy(out=xw.ap[:, 0:3], in_=hist[l].ap[:, :, c]), reads=[hist[l]], writes=[xw])
            S.act(lambda e: e.activation(out=xw.ap[:, 3:3 + Tt], in_=pa.ap[:, 0:Tt], func=AF.Copy), reads=[pa], writes=[xw])
            S.dve(lambda e: e.tensor_copy(out=hist[l].ap[:, :, c], in_=xw.ap[:, Tt:Tt + 3]), reads=[xw], writes=[hist[l]])
            x_ = xc3[j3]
            S.dve(lambda e: e.tensor_scalar(out=x_.ap[:, 0:Tt], in0=xw.ap[:, 0:Tt], scalar1=vcol(l, "cw0", c),
                                            scalar2=vcol(l, "conv_b", c), op0=ALU.mult, op1=ALU.add),
                  reads=[xw, colv[l]], writes=[x_])
            for jj in range(1, 4):
                S.dve(lambda e, jj=jj: e.scalar_tensor_tensor(out=x_.ap[:, 0:Tt], in0=xw.ap[:, jj:jj + Tt],
                                                              scalar=vcol(l, "cw%d" % jj, c), in1=x_.ap[:, 0:Tt],
                                                              op0=ALU.mult, op1=ALU.add), reads=[xw, x_, colv[l]], writes=[x_])
            xb_ = xcb[j]
            S.dve(lambda e: e.tensor_copy(out=xb_.ap[:, 0:Tt], in_=x_.ap[:, 0:Tt]), reads=[x_], writes=[xb_])
            pg = ps_mm()
            fm_proj(pg, slg, vg, c4 * 128, uT, Tt)
            g_ = gel3[j3]
            S.act(lambda e: e.activation(out=g_.ap[:, 0:Tt], in_=pg.ap[:, 0:Tt], func=AF.Gelu_apprx_tanh), reads=[pg], writes=[g_])
            yield
            pr = aux(1)
            S.pe(lambda e: e.matmul(pr.ap[:, 0:Tt], wa_bf[l].ap[:, c, :], xb_.ap[:, 0:Tt], start=True, stop=True),
                 reads=[wa_bf[l], xb_], writes=[pr])
            pi = aux(2)
            S.pe(lambda e: e.matmul(pi.ap[:, 0:Tt], wx_bf[l].ap[:, c, :], xb_.ap[:, 0:Tt], start=True, stop=True),
                 reads=[wx_bf[l], xb_], writes=[pi])
            r_, i_, a_, q_, h_ = rr[j], ii[j], aa[j], sq1[j], hh[j]
            S.act(lambda e: e.activation(out=r_.ap[:, 0:Tt], in_=pr.ap[:, 0:Tt], func=AF.Tanh, scale=0.5, bias=hb[l].ap[:, c:c + 1]),
                  reads=[pr, hb[l]], writes=[r_])
            S.act(lambda e: e.activation(out=i_.ap[:, 0:Tt], in_=pi.ap[:, 0:Tt], func=AF.Tanh, scale=0.5,
                                         bias=hb[l].ap[:, 8 + c:9 + c]), reads=[pi, hb[l]], writes=[i_])
            yield
            S.act(lambda e: e.activation(out=a_.ap[:, 0:Tt], in_=r_.ap[:, 0:Tt], func=AF.Exp, scale=nsp4[l].ap[:, c:c + 1],
                                         bias=nsp4[l].ap[:, c:c + 1]), reads=[r_, nsp4[l]], writes=[a_])
            S.act(lambda e: e.activation(out=q_.ap[:, 0:Tt], in_=r_.ap[:, 0:Tt], func=AF.Exp, scale=nsp8[l].ap[:, c:c + 1],
                                         bias=nsp8[l].ap[:, c:c + 1]), reads=[r_, nsp8[l]], writes=[q_])
            S.act(lambda e: e.activation(out=q_.ap[:, 0:Tt], in_=q_.ap[:, 0:Tt], func=AF.Ln, scale=-1.0, bias=1.0),
                  reads=[q_], writes=[q_])
            S.act(lambda e: e.activation(out=q_.ap[:, 0:Tt], in_=q_.ap[:, 0:Tt], func=AF.Exp, scale=0.5), reads=[q_], writes=[q_])
            S.dve(lambda e: e.scalar_tensor_tensor(out=i_.ap[:, 0:Tt], in0=i_.ap[:, 0:Tt], scalar=1.0, in1=x_.ap[:, 0:Tt],
                                                   op0=ALU.add, op1=ALU.mult), reads=[i_, x_], writes=[i_])
            S.dve(lambda e: e.scalar_tensor_tensor(out=i_.ap[:, 0:Tt], in0=i_.ap[:, 0:Tt], scalar=0.5, in1=q_.ap[:, 0:Tt],
                                                   op0=ALU.mult, op1=ALU.mult), reads=[i_, q_], writes=[i_])
            S.dve(lambda e: e.tensor_tensor_scan(out=h_.ap[:, 0:Tt], data0=a_.ap[:, 0:Tt], data1=i_.ap[:, 0:Tt],
                                                 initial=hst[l].ap[:, c:c + 1], op0=ALU.mult, op1=ALU.add),
                  reads=[a_, i_, hst[l]], writes=[h_])
            S.pool(lambda e: e.tensor_copy(out=hst[l].ap[:, c:c + 1], in_=h_.ap[:, Tt - 1:Tt]), reads=[h_], writes=[hst[l]])
            S.pool(lambda e: e.tensor_tensor(out=hgT.ap[:, c, 0:Tt], in0=h_.ap[:, 0:Tt], in1=g_.ap[:, 0:Tt], op=ALU.mult),
                   reads=[g_, h_], writes=[hgT[c]])
            yield

        gens = [body(c) for c in range(8)]
        filler = qk_proj_gen(l, Tt, FL[0])
        for rnd in range(8 + 2):
            sa = rnd if rnd < 8 else None
            sb1 = rnd - 1 if 0 <= rnd - 1 < 8 else None
            sb2 = rnd - 2 if 0 <= rnd - 2 < 8 else None
            order = [sa, sb1, sb2] if rnd % 2 == 0 else [sb2, sa, sb1]
            for gi in order:
                if gi is not None:
                    next(gens[gi])
            for _ in range(3):
                next(filler, None)
        release(sl_["a"][0])
        release(sl_["g"][0])
        for _ in filler:
            pass
        merge_branch(l, 0, hgT, Tt)

    def mlstm_phase(l, Tt, L):
        NCH = Tt // L
        S.dma("sp", lambda e: e.dma_start(out=mg_row1.ap, in_=W["m_norm_g"][l].partition_broadcast(64)), mg_row1,
              writes=[mg_row1])
        slg, vg = slab_k8(l, "w_in", O_MI, 8)
        pi = aux(0)
        pf = aux(1)
        for k in range(8):
            S.pe(lambda e, k=k: e.matmul(pi.ap[0:4, 0:Tt], vg[:, k, 0:4], uT.ap[:, k, 0:Tt], start=(k == 0), stop=(k == 7)),
                 reads=[slg, uT[k]], writes=[pi])
        for k in range(8):
            S.pe(lambda e, k=k: e.matmul(pf.ap[0:4, 0:Tt], vg[:, k, 4:8], uT.ap[:, k, 0:Tt], start=(k == 0), stop=(k == 7)),
                 reads=[slg, uT[k]], writes=[pf])
        G = {k: v.ap[:, 0:Tt] for k, v in grow.items()}
        Gs = {k: v.ap[:, 0:NCH] for k, v in gsm.items()}
        S.act(lambda e: e.activation(out=G["ig"], in_=pi.ap[0:4, 0:Tt], func=AF.Identity, bias=bi_col[l].ap),
              reads=[pi, bi_col[l]], writes=[grow["ig"]])
        S.act(lambda e: e.activation(out=G["sp"], in_=pf.ap[0:4, 0:Tt], func=AF.Exp, scale=-1.0, bias=nbf_col[l].ap),
              reads=[pf, nbf_col[l]], writes=[grow["sp"]])
        S.act(lambda e: e.activation(out=G["sp"], in_=G["sp"], func=AF.Ln, bias=1.0), reads=[grow["sp"]], writes=[grow["sp"]])
        S.dve(lambda e: e.tensor_tensor_scan(out=G["b"], data0=maskrow.ap[:, 0:Tt], data1=G["sp"], initial=0.0,
                                             op0=ALU.mult, op1=ALU.subtract), reads=[maskrow, grow["sp"]], writes=[grow["b"]])
        S.dve(lambda e: e.tensor_tensor(out=G["a"], in0=G["ig"], in1=G["b"], op=ALU.subtract),
              reads=[grow["ig"], grow["b"]], writes=[grow["a"]])
        a3 = G["a"].rearrange("p (c l) -> p c l", l=L)
        b3 = G["b"].rearrange("p (c l) -> p c l", l=L)
        S.dve(lambda e: e.tensor_reduce(out=Gs["amax"], in_=a3, axis=AX.X, op=ALU.max), reads=[grow["a"]], writes=[gsm["amax"]])
        S.dve(lambda e: e.memset(Gs["d0"][:, 0:1], 0.0), writes=[gsm["d0"]])
        if NCH > 1:
            S.dve(lambda e: e.tensor_copy(out=Gs["d0"][:, 1:NCH], in_=b3[:, 0:NCH - 1, L - 1]), reads=[grow["b"]],
                  writes=[gsm["d0"]])
        S.dve(lambda e: e.tensor_tensor_scan(out=Gs["M"], data0=Gs["d0"], data1=Gs["amax"], initial=mst[l].ap,
                                             op0=ALU.add, op1=ALU.max), reads=[gsm["d0"], gsm["amax"], mst[l]],
              writes=[gsm["M"]])
        S.dve(lambda e: e.tensor_tensor(out=Gs["mnew"], in0=b3[:, :, L - 1], in1=Gs["M"], op=ALU.add),
              reads=[grow["b"], gsm["M"]], writes=[gsm["mnew"]])
        S.dve(lambda e: e.tensor_copy(out=Gs["mprev"][:, 0:1], in_=mst[l].ap), reads=[mst[l]], writes=[gsm["mprev"]])
        if NCH > 1:
            S.dve(lambda e: e.tensor_copy(out=Gs["mprev"][:, 1:NCH], in_=Gs["mnew"][:, 0:NCH - 1]), reads=[gsm["mnew"]],
                  writes=[gsm["mprev"]])
        S.dve(lambda e: e.tensor_copy(out=mst[l].ap, in_=Gs["mnew"][:, NCH - 1:NCH]), reads=[gsm["mnew"], gsm["mprev"]],
              writes=[mst[l]])
        S.dve(lambda e: e.tensor_tensor(out=Gs["iw"], in0=Gs["mprev"], in1=Gs["M"], op=ALU.subtract),
              reads=[gsm["mprev"], gsm["M"]], writes=[gsm["iw"]])
        S.act(lambda e: e.activation(out=Gs["iw"], in_=Gs["iw"], func=AF.Exp), reads=[gsm["iw"]], writes=[gsm["iw"]])
        Mb = Gs["M"].unsqueeze(2).to_broadcast([4, NCH, L])
        ea3 = G["ea"].rearrange("p (c l) -> p c l", l=L)
        cl3 = G["cl"].rearrange("p (c l) -> p c l", l=L)
        iw3 = G["iwt"].rearrange("p (c l) -> p c l", l=L)
        S.dve(lambda e: e.tensor_tensor(out=ea3, in0=a3, in1=Mb, op=ALU.subtract), reads=[grow["a"], gsm["M"]],
              writes=[grow["ea"]])
        S.act(lambda e: e.activation(out=G["ea"], in_=G["ea"], func=AF.Exp), reads=[grow["ea"]], writes=[grow["ea"]])
        S.dve(lambda e: e.tensor_tensor(out=cl3, in0=b3, in1=Mb, op=ALU.add), reads=[grow["b"], gsm["M"]],
              writes=[grow["cl"]])
        S.act(lambda e: e.activation(out=G["cl"], in_=G["cl"], func=AF.Exp, scale=-1.0), reads=[grow["cl"]],
              writes=[grow["cl"]])
        S.dve(lambda e: e.tensor_copy(out=iw3, in_=Gs["iw"].unsqueeze(2).to_broadcast([4, NCH, L])), reads=[gsm["iw"]],
              writes=[grow["iwt"]])
        G2 = min(2, NCH)
        TG = G2 * L
        sgt2 = [qsq, dsum[1]]
        for half in range(2):
            sl, v = slab_k8(l, "w_in", O_MO + half * 512, 512)
            for gp in range(NCH // G2):
                pb = ps_mm()
                for k in range(8):
                    S.pe(lambda e, k=k, pb=pb, gp=gp, v=v: e.matmul(pb.ap[0:TG, :], uT.ap[:, k, gp * TG:(gp + 1) * TG], v[:, k, :],
                                                                    start=(k == 0), stop=(k == 7)),
                         reads=[sl, uT[k]], writes=[pb])
                for gi in range(G2):
                    ch = gp * G2 + gi
                    t_ = sgt2[gi]
                    S.act(lambda e, pb=pb, gi=gi, t_=t_: e.activation(out=t_.ap[0:L, :], in_=pb.ap[gi * L:(gi + 1) * L, :], func=AF.Exp,
                                                                      scale=-1.0), reads=[pb], writes=[t_])
                    S.act(lambda e, t_=t_: e.activation(out=t_.ap[0:L, :], in_=t_.ap[0:L, :], func=AF.Ln, bias=1.0), reads=[t_], writes=[t_])
                    S.act(lambda e, t_=t_: e.activation(out=t_.ap[0:L, :], in_=t_.ap[0:L, :], func=AF.Exp, scale=-1.0), reads=[t_],
                          writes=[t_])
                    S.pool(lambda e, ch=ch, half=half, t_=t_: e.tensor_tensor(out=sgm.ap[0:L, ch, half * 512:(half + 1) * 512],
                                                                              in0=t_.ap[0:L, :],
                                                                              in1=mg_row[l].ap[0:L, half * 512:(half + 1) * 512],
                                                                              op=ALU.mult),
                           reads=[t_, mg_row[l]], writes=[sgm[ch]])
        pc = aux(2)
        pcv = pc.ap[0:L, 0:NCH * 12].rearrange("p (c q) -> p c q", q=12)
        for ch in range(NCH):
            for qi, nm in enumerate(["ea", "cl", "iwt"]):
                S.pe(lambda e, ch=ch, qi=qi, nm=nm: e.transpose(pcv[:, ch, qi * 4:qi * 4 + 4], G[nm][:, ch * L:(ch + 1) * L],
                                                                 ident.ap[0:4, 0:4]),
                     reads=[grow[nm], ident], writes=[pc])
        S.dve(lambda e: e.tensor_copy(out=colq.ap[0:L, 0:NCH, :], in_=pcv), reads=[pc], writes=[colq])
        S.act(lambda e: e.activation(out=eab.ap[0:L, 0:NCH, :], in_=colq.ap[0:L, 0:NCH, 0:4], func=AF.Copy), reads=[colq],
              writes=[eab])
        S.dve(lambda e: e.tensor_tensor(out=rhsm.ap[:, :, 0:NCH], in0=Gs["iw"].unsqueeze(1).to_broadcast([4, 4, NCH]),
                                        in1=hmask.ap[:, :, 0:NCH], op=ALU.mult), reads=[gsm["iw"], hmask], writes=[rhsm])
        pw = aux(3)
        for h2 in range(4):
            S.pe(lambda e, h2=h2: e.matmul(pw.ap[:, h2 * NCH:(h2 + 1) * NCH], ones32.ap, rhsm.ap[:, h2, 0:NCH],
                                           start=True, stop=True), reads=[ones32, rhsm], writes=[pw])
        S.dve(lambda e: e.tensor_copy(out=iw_rep.ap[:, 0:4 * NCH], in_=pw.ap[:, 0:4 * NCH]), reads=[pw], writes=[iw_rep])

        for half in range(2):
            sl, v = slab_k8(l, "w_in", O_MV + half * 512, 512)
            for gp in range(NCH // G2):
                pb = ps_mm()
                for k in range(8):
                    S.pe(lambda e, k=k, pb=pb, gp=gp, v=v: e.matmul(pb.ap[0:TG, :], uT.ap[:, k, gp * TG:(gp + 1) * TG], v[:, k, :],
                                                                    start=(k == 0), stop=(k == 7)),
                         reads=[sl, uT[k]], writes=[pb])
                for gi in range(G2):
                    ch = gp * G2 + gi
                    copy_any(vw.ap[0:L, ch, half * 2:half * 2 + 2, :],
                             pb.ap[gi * L:(gi + 1) * L, :].rearrange("p (h e) -> p h e", h=2), [pb], [vw[ch]])
        for ch in range(NCH):
            S.dve(lambda e, ch=ch: e.tensor_tensor(out=vw.ap[0:L, ch], in0=vw.ap[0:L, ch],
                                                   in1=colq.ap[0:L, ch, 0:4].unsqueeze(2).to_broadcast([L, 4, 256]), op=ALU.mult),
                  reads=[vw[ch], colq], writes=[vw[ch]])
        def state_copies(h, chn):
            iwn = iw_rep.ap[:, h * NCH + chn:h * NCH + chn + 1]
            S.act(lambda e: e.activation(out=Cnb4[h].ap, in_=Cst[l].ap[:, h, :, :], func=AF.Copy, scale=iwn),
                  reads=[Cst[l][h], iw_rep], writes=[Cnb4[h]])
            S.act(lambda e: e.activation(out=nb4[h].ap, in_=nst[l].ap[:, h, :], func=AF.Copy, scale=iwn),
                  reads=[nst[l], iw_rep], writes=[nb4[h]])

        for h in range(4):
            state_copies(h, 0)

        def cbody(ch):
            cs = slice(ch * L, (ch + 1) * L)
            j = ch % 2
            ps_s = aux(0)
            sv = ps_s.ap[0:L, 0:4 * L].rearrange("p (h t) -> p h t", h=4)
            for h in range(4):
                for dc in range(2):
                    S.pe(lambda e, h=h, dc=dc, sv=sv, cs=cs: e.matmul(sv[:, h, :], kT.ap[:, 2 * h + dc, cs],
                                                                      qT.ap[:, 2 * h + dc, cs], start=(dc == 0), stop=(dc == 1)),
                         reads=[kT[2 * h + dc], qT[2 * h + dc]], writes=[ps_s])
            sm = smask[j]
            S.dve(lambda e, sm=sm, sv=sv: e.tensor_tensor(out=sm.ap[0:L, :, 0:L], in0=sv,
                                                          in1=cmask.ap[0:L, 0:L].unsqueeze(1).to_broadcast([L, 4, L]),
                                                          op=ALU.mult), reads=[ps_s, cmask], writes=[sm])
            yield
            po = [aux(1), aux(2)]
            pd = aux(3)
            for h in range(4):
                cb, nb_ = Cnb4[h], nb4[h]
                iwc = iw_rep.ap[:, h * NCH + ch:h * NCH + ch + 1]
                pov = po[h // 2].ap[0:L, (h % 2) * 256:(h % 2 + 1) * 256]
                S.pe(lambda e, pov=pov, sm=sm, h=h, ch=ch: e.matmul(pov, sm.ap[0:L, h, 0:L], vw.ap[0:L, ch, h, :],
                                                                    start=True, stop=False),
                     reads=[sm, vw[ch]], writes=[po[h // 2]])
                for dc in range(2):
                    S.pe(lambda e, pov=pov, h=h, dc=dc, cs=cs, cb=cb: e.matmul(pov, qT.ap[:, 2 * h + dc, cs], cb.ap[:, dc, :],
                                                                               start=False, stop=(dc == 1)),
                         reads=[qT[2 * h + dc], cb], writes=[po[h // 2]])
                pdv = pd.ap[0:L, h:h + 1]
                S.pe(lambda e, pdv=pdv, sm=sm, h=h, ch=ch: e.matmul(pdv, sm.ap[0:L, h, 0:L], eab.ap[0:L, ch, h:h + 1],
                                                                    start=True, stop=False),
                     reads=[sm, eab], writes=[pd[0]])
                for dc in range(2):
                    S.pe(lambda e, pdv=pdv, h=h, dc=dc, cs=cs, nb_=nb_: e.matmul(pdv, qT.ap[:, 2 * h + dc, cs],
                                                                                 nb_.ap[:, dc:dc + 1], start=False,
                                                                                 stop=(dc == 1)),
                         reads=[qT[2 * h + dc], nb_], writes=[pd[0]])
                pdl = aux(4)
                for dc in range(2):
                    S.pe(lambda e, pdl=pdl, h=h, dc=dc, ch=ch: e.matmul(pdl.ap[:, dc * 256:(dc + 1) * 256],
                                                                        ktok.ap[0:L, ch, h * 256 + dc * 128:h * 256 + dc * 128 + 128],
                                                                        vw.ap[0:L, ch, h, :], start=True, stop=True),
                         reads=[ktok[ch], vw[ch]], writes=[pdl])
                    S.pe(lambda e, pd=pd, h=h, dc=dc, ch=ch: e.matmul(pd.ap[:, 8 + h * 2 + dc:8 + h * 2 + dc + 1],
                                                                      ktok.ap[0:L, ch, h * 256 + dc * 128:h * 256 + dc * 128 + 128],
                                                                      eab.ap[0:L, ch, h:h + 1], start=True, stop=True),
                         reads=[ktok[ch], eab], writes=[pd[1 + h]])
                S.dve(lambda e, pdl=pdl, h=h, iwc=iwc: e.scalar_tensor_tensor(
                    out=Cst[l].ap[:, h, :, :].rearrange("p a b -> p (a b)"), in0=Cst[l].ap[:, h, :, :].rearrange("p a b -> p (a b)"),
                    scalar=iwc, in1=pdl.ap[:, 0:512], op0=ALU.mult, op1=ALU.add),
                      reads=[Cst[l][h], pdl, iw_rep], writes=[Cst[l][h]])
                S.dve(lambda e, pd=pd, h=h, iwc=iwc: e.scalar_tensor_tensor(
                    out=nst[l].ap[:, h, :], in0=nst[l].ap[:, h, :], scalar=iwc, in1=pd.ap[:, 8 + h * 2:8 + h * 2 + 2],
                    op0=ALU.mult, op1=ALU.add), reads=[nst[l], pd[1 + h], iw_rep], writes=[nst[l]])
                if ch + 1 < NCH:
                    state_copies(h, ch + 1)
            yield
            ds_, dt_, hn_, ss_ = den_s[j], den_t[j], hn[j], ssm[j]
            S.act(lambda e, ds_=ds_, pd=pd: e.activation(out=ds_.ap[0:L, :], in_=pd.ap[0:L, 0:4], func=AF.Copy), reads=[pd[0]],
                  writes=[ds_])
            S.dve(lambda e, ds_=ds_, dt_=dt_: e.scalar_tensor_tensor(out=dt_.ap[0:L, :], in0=ds_.ap[0:L, :], scalar=-1.0,
                                                                     in1=ds_.ap[0:L, :], op0=ALU.mult, op1=ALU.max),
                  reads=[ds_], writes=[dt_])
            S.dve(lambda e, dt_=dt_, ch=ch: e.tensor_tensor(out=dt_.ap[0:L, :], in0=dt_.ap[0:L, :], in1=colq.ap[0:L, ch, 4:8],
                                                            op=ALU.max), reads=[dt_, colq], writes=[dt_])
            S.dve(lambda e, dt_=dt_: e.reciprocal(out=dt_.ap[0:L, :], in_=dt_.ap[0:L, :]), reads=[dt_], writes=[dt_])
            for hp in range(2):
                S.dve(lambda e, hp=hp, hn_=hn_, dt_=dt_: e.tensor_tensor(
                    out=hn_.ap[0:L, 2 * hp:2 * hp + 2, :], in0=po[hp].ap[0:L, :].rearrange("p (h d) -> p h d", h=2),
                    in1=dt_.ap[0:L, 2 * hp:2 * hp + 2].unsqueeze(2).to_broadcast([L, 2, 256]), op=ALU.mult),
                      reads=[po[hp], dt_], writes=[hn_])
            S.act(lambda e, hn_=hn_: e.activation(out=hsq.ap[0:L], in_=hn_.ap[0:L], func=AF.Square), reads=[hn_], writes=[hsq])
            S.dve(lambda e, ss_=ss_: e.tensor_reduce(out=ss_.ap[0:L, :], in_=hsq.ap[0:L], axis=AX.X, op=ALU.add), reads=[hsq],
                  writes=[ss_])
            S.act(lambda e, ss_=ss_: e.activation(out=ss_.ap[0:L, :], in_=ss_.ap[0:L, :], func=AF.Sqrt, scale=1.0 / 256, bias=EPS),
                  reads=[ss_], writes=[ss_])
            S.dve(lambda e, ss_=ss_: e.reciprocal(out=ss_.ap[0:L, :], in_=ss_.ap[0:L, :]), reads=[ss_], writes=[ss_])
            S.pool(lambda e, hn_=hn_, ss_=ss_: e.tensor_tensor(out=hn_.ap[0:L], in0=hn_.ap[0:L],
                                                               in1=ss_.ap[0:L, :].unsqueeze(2).to_broadcast([L, 4, 256]),
                                                               op=ALU.mult), reads=[hn_, ss_], writes=[hn_])
            S.pool(lambda e, hn_=hn_, ch=ch: e.tensor_tensor(out=hmtok.ap[0:L, ch, :], in0=hn_.ap[0:L].rearrange("p h d -> p (h d)"),
                                                             in1=sgm.ap[0:L, ch, :], op=ALU.mult),
                   reads=[hn_, sgm[ch]], writes=[hmtok[ch]])
            yield
            pt = aux(5)
            ptv = pt.ap.bitcast(BF16)[:, 0:8 * L].rearrange("p (c t) -> p c t", c=8)
            for c in range(8):
                S.pe(lambda e, c=c, ptv=ptv, ch=ch: e.transpose(ptv[:, c, :], hmtok.ap[0:L, ch, c * 128:(c + 1) * 128],
                                                                 identb.ap[0:L, 0:L]),
                     reads=[hmtok[ch], identb], writes=[pt])
            copy_any(hmT.ap[:, :, cs], ptv, [pt], [hmT])

        pipeline([cbody(ch) for ch in range(NCH)], newest_first=False)
        merge_branch(l, 1, hmT, Tt)

    def attn_phase(l, Tt, L, first_tile, is_sample, last_tile, grp):
        NCH = Tt // L
        Tb = min(Tt, 128)
        NB = Tt // Tb
        slkv, vkv = slab_k8(l, "w_in", O_AK, 256)

        def kbody(ch):
            j = ch % 2
            slot = 2 + ch
            pb = ps_mm()
            for k in range(8):
                S.pe(lambda e, k=k, pb=pb, ch=ch: e.matmul(pb.ap[0:L, 0:256], uT.ap[:, k, ch * L:(ch + 1) * L], vkv[:, k, :],
                                                           start=(k == 0), stop=(k == 7)), reads=[slkv, uT[k]], writes=[pb])
            vf_ = vf[j]
            S.act(lambda e, pb=pb, vf_=vf_: e.activation(out=vf_.ap[0:L, :], in_=pb.ap[0:L, 128:256], func=AF.Copy), reads=[pb],
                  writes=[vf_])
            S.dve(lambda e, vf_=vf_, slot=slot: e.tensor_copy(out=vaug[l].ap[0:L, slot, :, 0:64],
                                                              in_=vf_.ap[0:L, :].rearrange("p (k d) -> p k d", k=2)),
                  reads=[vf_], writes=[vaug[l]])
            S.dve(lambda e, vf_=vf_, slot=slot: e.tensor_copy(out=vaug[l].ap[0:L, slot, :, 128:192],
                                                              in_=vf_.ap[0:L, :].rearrange("p (k d) -> p k d", k=2)),
                  reads=[vf_], writes=[vaug[l]])
            ks_, kn_, kr_, krb_ = kss[j], kn[j], kr[j], krb[j]
            S.act(lambda e, pb=pb: e.activation(out=ksq.ap[0:L, :], in_=pb.ap[0:L, 0:128], func=AF.Square), reads=[pb], writes=[ksq])
            S.dve(lambda e, ks_=ks_: e.tensor_reduce(out=ks_.ap[0:L, :], in_=ksq.ap[0:L, :].rearrange("p (k d) -> p k d", k=2),
                                                     axis=AX.X, op=ALU.add), reads=[ksq], writes=[ks_])
            S.act(lambda e, ks_=ks_: e.activation(out=ks_.ap[0:L, :], in_=ks_.ap[0:L, :], func=AF.Sqrt, scale=1.0 / 64, bias=EPS),
                  reads=[ks_], writes=[ks_])
            S.dve(lambda e, ks_=ks_: e.reciprocal(out=ks_.ap[0:L, :], in_=ks_.ap[0:L, :]), reads=[ks_], writes=[ks_])
            for kv in range(2):
                S.dve(lambda e, kv=kv, pb=pb, ks_=ks_, kn_=kn_: e.scalar_tensor_tensor(
                    out=kn_.ap[0:L, kv, :], in0=pb.ap[0:L, kv * 64:(kv + 1) * 64], scalar=ks_.ap[0:L, kv:kv + 1],
                    in1=kg_row[l].ap[0:L, :], op0=ALU.mult, op1=ALU.mult), reads=[pb, ks_, kg_row[l]], writes=[kn_])
            cosb = ctab_k.ap[0:L, ch, :].unsqueeze(1).to_broadcast([L, 2, 32])
            sinb = stab_k.ap[0:L, ch, :].unsqueeze(1).to_broadcast([L, 2, 32])
            k1 = kn_.ap[0:L, :, 0:32]
            k2 = kn_.ap[0:L, :, 32:64]
            krv = kr_.ap[0:L, :].rearrange("p (k d) -> p k d", k=2)
            S.dve(lambda e, k1=k1, cosb=cosb: e.tensor_tensor(out=kt1.ap[0:L], in0=k1, in1=cosb, op=ALU.mult),
                  reads=[kn_, ctab_k], writes=[kt1])
            S.dve(lambda e, k2=k2, sinb=sinb: e.tensor_tensor(out=kt2.ap[0:L], in0=k2, in1=sinb, op=ALU.mult),
                  reads=[kn_, stab_k], writes=[kt2])
            S.dve(lambda e, krv=krv: e.tensor_tensor(out=krv[:, :, 0:32], in0=kt1.ap[0:L], in1=kt2.ap[0:L], op=ALU.subtract),
                  reads=[kt1, kt2], writes=[kr_])
            S.dve(lambda e, k2=k2, cosb=cosb: e.tensor_tensor(out=kt1.ap[0:L], in0=k2, in1=cosb, op=ALU.mult),
                  reads=[kn_, ctab_k], writes=[kt1])
            S.dve(lambda e, k1=k1, sinb=sinb: e.tensor_tensor(out=kt2.ap[0:L], in0=k1, in1=sinb, op=ALU.mult),
                  reads=[kn_, stab_k], writes=[kt2])
            S.dve(lambda e, krv=krv: e.tensor_tensor(out=krv[:, :, 32:64], in0=kt1.ap[0:L], in1=kt2.ap[0:L], op=ALU.add),
                  reads=[kt1, kt2], writes=[kr_])
            S.act(lambda e, kr_=kr_, krb_=krb_: e.activation(out=krb_.ap[0:L, :], in_=kr_.ap[0:L, :], func=AF.Copy),
                  reads=[kr_], writes=[krb_])
            yield
            pk = aux(0)
            pkv = pk.ap.bitcast(BF16)[0:64, 0:2 * L].rearrange("p (k t) -> p k t", k=2)
            for kv in range(2):
                S.pe(lambda e, kv=kv, pkv=pkv, krb_=krb_: e.transpose(pkv[:, kv, :], krb_.ap[0:L, kv * 64:(kv + 1) * 64],
                                                                      identb.ap[0:L, 0:L]), reads=[krb_, identb], writes=[pk])
            S.dve(lambda e, pkv=pkv, slot=slot: e.tensor_copy(out=KTwin[l].ap[:, :, slot * 64:slot * 64 + L], in_=pkv),
                  reads=[pk], writes=[KTwin[l]])
            if is_sample:
                S.dma("sp", lambda e, kr_=kr_: e.dma_start(out=O["k_s"][l, 128 - L:128].rearrange("r k d -> r (k d)"),
                                                           in_=kr_.ap[0:L, :]), kr_, reads=[kr_])
                S.dma("sp", lambda e, vf_=vf_: e.dma_start(out=O["v_s"][l, 128 - L:128].rearrange("r k d -> r (k d)"),
                                                           in_=vf_.ap[0:L, :]), vf_, reads=[vf_])
            elif last_tile and ch >= NCH - 2:
                r0 = (ch - (NCH - 2)) * 64
                S.dma("sp", lambda e, kr_=kr_, r0=r0: e.dma_start(out=O["k_p"][l, r0:r0 + 64].rearrange("r k d -> r (k d)"),
                                                                  in_=kr_.ap[0:L, :]), kr_, reads=[kr_])
                S.dma("sp", lambda e, vf_=vf_, r0=r0: e.dma_start(out=O["v_p"][l, r0:r0 + 64].rearrange("r k d -> r (k d)"),
                                                                  in_=vf_.ap[0:L, :]), vf_, reads=[vf_])

        pipeline([kbody(ch) for ch in range(NCH)], newest_first=True)
        qsl = {}

        def qbody(half, b):
            if True:
                if b == 0:
                    qsl["q"] = slab_k8(l, "w_in", O_AQ + half * 512, 512)
                slq, vq = qsl["q"]
                j = (half * NB + b) % 2
                pb = ps_mm()
                for k in range(8):
                    S.pe(lambda e, k=k, pb=pb, b=b, vq=vq: e.matmul(pb.ap[0:Tb, :], uT.ap[:, k, b * Tb:(b + 1) * Tb], vq[:, k, :],
                                                                    start=(k == 0), stop=(k == 7)), reads=[slq, uT[k]], writes=[pb])
                qs_, qn_, qr_ = qss[j], qn[j], qr[j]
                S.act(lambda e, pb=pb: e.activation(out=qsq.ap[0:Tb, :], in_=pb.ap[0:Tb, :], func=AF.Square), reads=[pb], writes=[qsq])
                S.dve(lambda e, qs_=qs_: e.tensor_reduce(out=qs_.ap[0:Tb, :], in_=qsq.ap[0:Tb, :].rearrange("p (h d) -> p h d", h=8),
                                                         axis=AX.X, op=ALU.add), reads=[qsq], writes=[qs_])
                S.act(lambda e, qs_=qs_: e.activation(out=qs_.ap[0:Tb, :], in_=qs_.ap[0:Tb, :], func=AF.Sqrt, scale=1.0 / 64, bias=EPS),
                      reads=[qs_], writes=[qs_])
                S.dve(lambda e, qs_=qs_: e.reciprocal(out=qs_.ap[0:Tb, :], in_=qs_.ap[0:Tb, :]), reads=[qs_], writes=[qs_])
                S.dve(lambda e, pb=pb, qs_=qs_, qn_=qn_: e.tensor_tensor(
                    out=qn_.ap[0:Tb], in0=pb.ap[0:Tb, :].rearrange("p (h d) -> p h d", h=8),
                    in1=qs_.ap[0:Tb, :].unsqueeze(2).to_broadcast([Tb, 8, 64]), op=ALU.mult), reads=[pb, qs_], writes=[qn_])
                S.dve(lambda e, qn_=qn_: e.tensor_tensor(out=qn_.ap[0:Tb], in0=qn_.ap[0:Tb],
                                                         in1=qg_row[l].ap[0:Tb, :].unsqueeze(1).to_broadcast([Tb, 8, 64]),
                                                         op=ALU.mult), reads=[qn_, qg_row[l]], writes=[qn_])
                cosb = ctab_q.ap[0:Tb, b, :].unsqueeze(1).to_broadcast([Tb, 8, 32])
                sinb = stab_q.ap[0:Tb, b, :].unsqueeze(1).to_broadcast([Tb, 8, 32])
                q1 = qn_.ap[0:Tb, :, 0:32]
                q2 = qn_.ap[0:Tb, :, 32:64]
                S.dve(lambda e, q1=q1, cosb=cosb: e.tensor_tensor(out=qt1.ap[0:Tb], in0=q1, in1=cosb, op=ALU.mult),
                      reads=[qn_, ctab_q], writes=[qt1])
                S.dve(lambda e, q2=q2, sinb=sinb: e.tensor_tensor(out=qt2.ap[0:Tb], in0=q2, in1=sinb, op=ALU.mult),
                      reads=[qn_, stab_q], writes=[qt2])
                S.dve(lambda e, qr_=qr_: e.tensor_tensor(out=qr_.ap[0:Tb, :, 0:32], in0=qt1.ap[0:Tb], in1=qt2.ap[0:Tb],
                                                         op=ALU.subtract), reads=[qt1, qt2], writes=[qr_])
                S.dve(lambda e, q2=q2, cosb=cosb: e.tensor_tensor(out=qt1.ap[0:Tb], in0=q2, in1=cosb, op=ALU.mult),
                      reads=[qn_, ctab_q], writes=[qt1])
                S.dve(lambda e, q1=q1, sinb=sinb: e.tensor_tensor(out=qt2.ap[0:Tb], in0=q1, in1=sinb, op=ALU.mult),
                      reads=[qn_, stab_q], writes=[qt2])
                S.dve(lambda e, qr_=qr_: e.tensor_tensor(out=qr_.ap[0:Tb, :, 32:64], in0=qt1.ap[0:Tb], in1=qt2.ap[0:Tb],
                                                         op=ALU.add), reads=[qt1, qt2], writes=[qr_])
                yield
                pq = aux(1)
                pqv = pq.ap.bitcast(BF16)[0:64, 0:8 * Tb].rearrange("p (h t) -> p h t", h=8)
                for h in range(8):
                    S.pe(lambda e, h=h, pqv=pqv, qr_=qr_: e.transpose(pqv[:, h, :], qr_.ap[0:Tb, h, :], identb.ap[0:Tb, 0:Tb]),
                         reads=[qr_, identb], writes=[pq])
                copy_any(QT_all.ap[:, half * 8:(half + 1) * 8, b * Tb:(b + 1) * Tb], pqv, [pq],
                         [QT_all[half * 8 + h] for h in range(8)])

        pipeline([qbody(half, b) for half in range(2) for b in range(NB)], newest_first=True)

        def abody(ch, kv):
            slots = [(ch, 64), (ch + 1, 64), (ch + 2, L)]
            if first_tile and not is_sample:
                slots = [(s, n) for (s, n) in slots if s >= 2]
            qs = slice(ch * L, (ch + 1) * L)
            if True:
                ppv = aux(kv)
                pts = []
                for si_, (s, nk) in enumerate(slots):
                    pss = aux(2 + si_)
                    S.pe(lambda e, pss=pss, s=s, nk=nk, kv=kv, qs=qs: e.matmul(
                        pss.ap[0:nk, 0:8 * L].rearrange("p (g t) -> p g t", g=8), KTwin[l].ap[:, kv, s * 64:s * 64 + nk],
                        QT_all.ap[:, kv * 8:(kv + 1) * 8, qs], start=True, stop=True),
                         reads=[KTwin[l]] + [QT_all[kv * 8 + g] for g in range(8)], writes=[pss])
                    p_ = pT[pT_i[0] % 6]
                    pT_i[0] += 1
                    S.act(lambda e, p_=p_, pss=pss, nk=nk: e.activation(out=p_.ap[0:nk, 0:8 * L], in_=pss.ap[0:nk, 0:8 * L],
                                                                        func=AF.Exp, scale=0.125), reads=[pss], writes=[p_])
                    pts.append((p_, s, nk))
                yield
                H4 = 4 * L
                for par in range(2):
                    for i, (p_, s, nk) in enumerate(pts):
                        pv4 = p_.ap[0:nk, 0:8 * L].rearrange("p (g two t) -> p g two t", two=2, t=L)
                        S.pe(lambda e, pv4=pv4, s=s, nk=nk, i=i, par=par: e.matmul(
                            ppv.ap[:, par * H4:(par + 1) * H4].rearrange("p (g t) -> p g t", g=4),
                            vaug[l].ap[0:nk, s, kv, par * 64:par * 64 + 128], pv4[:, :, par, :],
                            start=(i == 0), stop=(i == len(pts) - 1)), reads=[vaug[l], p_], writes=[ppv])
                ds_ = dsum[kv]
                es4 = esink[l].ap[:, kv * 8:(kv + 1) * 8].rearrange("p (g two) -> p g two", two=2)
                S.dve(lambda e: e.tensor_tensor(out=ds_.ap[64:128, 0:H4].rearrange("p (g t) -> p g t", g=4),
                                                in0=ppv.ap[64:128, 0:H4].rearrange("p (g t) -> p g t", g=4),
                                                in1=es4[64:128, :, 0].unsqueeze(2).to_broadcast([64, 4, L]), op=ALU.add),
                      reads=[ppv, esink[l]], writes=[ds_])
                S.act(lambda e: e.activation(out=ds_.ap[0:64, 0:H4], in_=ds_.ap[64:128, 0:H4], func=AF.Ln), reads=[ds_], writes=[ds_])
                S.act(lambda e: e.activation(out=ds_.ap[0:64, 0:H4], in_=ds_.ap[0:64, 0:H4], func=AF.Exp, scale=-1.0), reads=[ds_],
                      writes=[ds_])
                S.dve(lambda e: e.tensor_tensor(out=OT2.ap[0:64, kv * 4:(kv + 1) * 4, qs],
                                                in0=ppv.ap[0:64, 0:H4].rearrange("p (g t) -> p g t", g=4),
                                                in1=ds_.ap[0:64, 0:H4].rearrange("p (g t) -> p g t", g=4), op=ALU.mult),
                      reads=[ppv, ds_], writes=[OT2[kv * 4 + g] for g in range(4)])
                S.dve(lambda e: e.tensor_tensor(out=ds_.ap[0:64, H4:2 * H4].rearrange("p (g t) -> p g t", g=4),
                                                in0=ppv.ap[0:64, H4:2 * H4].rearrange("p (g t) -> p g t", g=4),
                                                in1=es4[0:64, :, 1].unsqueeze(2).to_broadcast([64, 4, L]), op=ALU.add),
                      reads=[ppv, esink[l]], writes=[ds_])
                S.act(lambda e: e.activation(out=ds_.ap[64:128, H4:2 * H4], in_=ds_.ap[0:64, H4:2 * H4], func=AF.Ln), reads=[ds_],
                      writes=[ds_])
                S.act(lambda e: e.activation(out=ds_.ap[64:128, H4:2 * H4], in_=ds_.ap[64:128, H4:2 * H4], func=AF.Exp, scale=-1.0),
                      reads=[ds_], writes=[ds_])
                S.dve(lambda e: e.tensor_tensor(out=OT2.ap[64:128, kv * 4:(kv + 1) * 4, qs],
                                                in0=ppv.ap[64:128, H4:2 * H4].rearrange("p (g t) -> p g t", g=4),
                                                in1=ds_.ap[64:128, H4:2 * H4].rearrange("p (g t) -> p g t", g=4), op=ALU.mult),
                      reads=[ppv, ds_], writes=[OT2[kv * 4 + g] for g in range(4)])

        pipeline([abody(ch, kv) for ch in range(NCH) for kv in range(2)], newest_first=True)
        if not is_sample and not last_tile:
            for i in range(2):
                S.dve(lambda e, i=i: e.tensor_copy(out=KTwin[l].ap[:, :, i * 64:(i + 1) * 64],
                                                   in_=KTwin[l].ap[:, :, (NCH + i) * 64:(NCH + i + 1) * 64]),
                      reads=[KTwin[l]], writes=[KTwin[l]])
                S.pool(lambda e, i=i: e.tensor_copy(out=vaug[l].ap[:, i, :, 0:64], in_=vaug[l].ap[:, NCH + i, :, 0:64]),
                       reads=[vaug[l]], writes=[vaug[l]])
                S.pool(lambda e, i=i: e.tensor_copy(out=vaug[l].ap[:, i, :, 128:192], in_=vaug[l].ap[:, NCH + i, :, 128:192]),
                       reads=[vaug[l]], writes=[vaug[l]])
        merge_branch(l, 2, OT2, Tt)

    def out_and_mlp(l, Tt):
        for c in range(8):
            S.act(lambda e, c=c: e.activation(out=uT.ap[:, c, 0:Tt], in_=mix.ap[:, c, 0:Tt], func=AF.Copy), reads=[mix[c]],
                  writes=[uT[c]])
        for half in range(2):
            sl, v = slab_k8(l, "w_out", half * 512, 512)
            for c4 in range(4):
                c = half * 4 + c4
                pb = ps_mm()
                fm_proj(pb, sl, v, c4 * 128, uT, Tt)
                S.dve(lambda e, pb=pb, c=c: e.tensor_tensor(out=xT.ap[:, c, 0:Tt], in0=pb.ap[:, 0:Tt], in1=xT.ap[:, c, 0:Tt],
                                                            op=ALU.add), reads=[pb, xT[c]], writes=[xT[c]])
        norm_to_u(l, "norm2_g", Tt)
        for s8 in range(8):
            sl, v = slab_k8(l, "w_up", s8 * 512, 512)
            for c4 in range(4):
                hc = s8 * 4 + c4
                pb = ps_mm()
                fm_proj(pb, sl, v, c4 * 128, uT, Tt)
                r_ = rl[hc % 2]
                S.act(lambda e, pb=pb, r_=r_: e.activation(out=r_.ap[:, 0:Tt], in_=pb.ap[:, 0:Tt], func=AF.Relu), reads=[pb],
                      writes=[r_])
                hp_ = hid_parts[hc // 8]
                S.dve(lambda e, r_=r_, hc=hc, hp_=hp_: e.tensor_tensor(out=hp_.ap[:, hc % 8, 0:Tt], in0=r_.ap[:, 0:Tt],
                                                                       in1=r_.ap[:, 0:Tt], op=ALU.mult),
                      reads=[r_], writes=[hp_[hc % 8]])
        for c in range(8):
            src = SCR["w_down"][l][:, c * 128:(c + 1) * 128].rearrange("(kc p) n -> p kc n", p=128)
            sl, v = load_slab(src, lambda a: a.rearrange("p (k n) -> p k n", k=32), SCRB[("w_down", l)])
            pb = ps_mm()
            for k in range(32):
                hp_ = hid_parts[k // 8]
                S.pe(lambda e, k=k, pb=pb, v=v, hp_=hp_: e.matmul(pb.ap[:, 0:Tt], v[:, k, :], hp_.ap[:, k % 8, 0:Tt],
                                                                  start=(k == 0), stop=(k == 31)),
                     reads=[sl, hp_[k % 8]], writes=[pb])
            S.dve(lambda e, pb=pb, c=c: e.tensor_tensor(out=xT.ap[:, c, 0:Tt], in0=pb.ap[:, 0:Tt], in1=xT.ap[:, c, 0:Tt],
                                                        op=ALU.add), reads=[pb, xT[c]], writes=[xT[c]])

    def load_x(src, Tt):
        Tb = min(Tt, 128)
        for b in range(Tt // Tb):
            xi = xin[b % 2]
            S.dma("sp", lambda e, xi=xi, b=b: e.dma_start(out=xi.ap[0:Tb, :], in_=src[b * Tb:(b + 1) * Tb, :]), xi, writes=[xi])
            for g4 in range(2):
                pb = aux(g4)
                for c4 in range(4):
                    c = g4 * 4 + c4
                    S.pe(lambda e, pb=pb, c=c, c4=c4, xi=xi: e.transpose(pb.ap[:, c4 * Tb:(c4 + 1) * Tb],
                                                                         xi.ap[0:Tb, c * 128:(c + 1) * 128], ident.ap[0:Tb, 0:Tb]),
                         reads=[xi, ident], writes=[pb])
                copy_any(xT.ap[:, g4 * 4:(g4 + 1) * 4, b * Tb:(b + 1) * Tb],
                         pb.ap[:, 0:4 * Tb].rearrange("p (c t) -> p c t", c=4), [pb], [xT[g4 * 4 + i] for i in range(4)])

    def store_y(dst, Tt):
        Tb = min(Tt, 128)
        for b in range(Tt // Tb):
            yo = yout[b % 2]
            for g4 in range(2):
                pb = aux(g4)
                for c4 in range(4):
                    c = g4 * 4 + c4
                    S.pe(lambda e, pb=pb, c=c, c4=c4, b=b: e.transpose(pb.ap[0:Tb, c4 * 128:(c4 + 1) * 128],
                                                                       xT.ap[:, c, b * Tb:(b + 1) * Tb], ident.ap),
                         reads=[xT[c], ident], writes=[pb])
                copy_any(yo.ap[0:Tb, g4 * 512:(g4 + 1) * 512], pb.ap[0:Tb, :], [pb], [yo])
            S.dma("sp", lambda e, yo=yo, b=b: e.dma_start(out=dst[b * Tb:(b + 1) * Tb, :], in_=yo.ap[0:Tb, :]), yo, reads=[yo])

    def load_rope(pos0, Tt, L):
        NCH = Tt // L
        Tb = min(Tt, 128)
        NB = Tt // Tb
        for (tab, src) in [(ctab_k, rope_c), (stab_k, rope_s)]:
            S.dma("sp", lambda e, tab=tab, src=src: e.dma_start(
                out=tab.ap[0:L, 0:NCH, :], in_=src[pos0:pos0 + Tt, :].rearrange("(c l) f -> l c f", l=L)), tab, writes=[tab])
        for (tab, src) in [(ctab_q, rope_c), (stab_q, rope_s)]:
            S.dma("sp", lambda e, tab=tab, src=src: e.dma_start(
                out=tab.ap[0:Tb, 0:NB, :], in_=src[pos0:pos0 + Tt, :].rearrange("(c l) f -> l c f", l=Tb)), tab, writes=[tab])

    def cols_to_rows_store(src_ap, n, dsts, rd):
        pb = aux(5)
        S.pe(lambda e: e.transpose(pb.ap[0:n, 0:128], src_ap, ident.ap), reads=rd + [ident], writes=[pb])
        S.dve(lambda e: e.tensor_copy(out=vstage.ap[0:n, :], in_=pb.ap[0:n, 0:128]), reads=[pb], writes=[vstage])
        for (r0, r1, d) in dsts:
            S.dma("sp", lambda e, r0=r0, r1=r1, d=d: e.dma_start(out=d, in_=vstage.ap[r0:r1, :]), vstage, reads=[vstage])

    def rows_load_to_cols(srcs, n, dst_ap, wr):
        for (r0, r1, s_) in srcs:
            S.dma("sp", lambda e, r0=r0, r1=r1, s_=s_: e.dma_start(out=vstage.ap[r0:r1, :], in_=s_), vstage, writes=[vstage])
        pb = aux(5)
        S.pe(lambda e: e.transpose(pb.ap[:, 0:n], vstage.ap[0:n, :], ident.ap[0:n, 0:n]), reads=[vstage, ident], writes=[pb])
        S.dve(lambda e: e.tensor_copy(out=dst_ap, in_=pb.ap[:, 0:n]), reads=[pb], writes=wr)

    def init_states_zero(l):
        S.dve(lambda e: e.memset(hist[l].ap, 0.0), writes=[hist[l]])
        S.dve(lambda e: e.memset(hst[l].ap, 0.0), writes=[hst[l]])
        S.pool(lambda e: e.memset(Cst[l].ap, 0.0), writes=[Cst[l]])
        S.dve(lambda e: e.memset(nst[l].ap, 0.0), writes=[nst[l]])
        S.dve(lambda e: e.memset(mst[l].ap, 0.0), writes=[mst[l]])
        S.pool(lambda e: e.memset(vaug[l].ap, 1.0), writes=[vaug[l]])
        S.pool(lambda e: e.memset(KTwin[l].ap, 0.0), writes=[KTwin[l]])

    def init_states_sample(l):
        for (r0, r1, s_) in [(0, 24, st_conv[l].rearrange("j (c p) -> (j c) p", p=128)),
                             (24, 32, st_lru[l].rearrange("(c p) -> c p", p=128)),
                             (32, 40, st_n[l].rearrange("h (c p) -> (h c) p", p=128))]:
            S.dma("sp", lambda e, r0=r0, r1=r1, s_=s_: e.dma_start(out=vstage.ap[r0:r1, :], in_=s_), vstage, writes=[vstage])
        pb = aux(5)
        S.pe(lambda e: e.transpose(pb.ap[:, 0:40], vstage.ap[0:40, :], ident.ap[0:40, 0:40]), reads=[vstage, ident], writes=[pb])
        S.dve(lambda e: e.tensor_copy(out=hist[l].ap.rearrange("p j c -> p (j c)"), in_=pb.ap[:, 0:24]), reads=[pb], writes=[hist[l]])
        S.dve(lambda e: e.tensor_copy(out=hst[l].ap, in_=pb.ap[:, 24:32]), reads=[pb], writes=[hst[l]])
        S.dve(lambda e: e.tensor_copy(out=nst[l].ap.rearrange("p h c -> p (h c)"), in_=pb.ap[:, 32:40]), reads=[pb], writes=[nst[l]])
        S.dma("sp", lambda e: e.dma_start(out=Cst[l].ap, in_=st_C[l].rearrange("h (c p) e -> p h c e", p=128)), Cst[l],
              writes=[Cst[l]])
        S.dma("sp", lambda e: e.dma_start(out=mst[l].ap, in_=st_m[l].rearrange("(h o) -> h o", o=1)), mst[l], writes=[mst[l]])
        S.pool(lambda e: e.memset(vaug[l].ap, 1.0), writes=[vaug[l]])
        for s2 in range(2):
            S.dma("pool", lambda e, s2=s2: e.dma_start(out=vaug[l].ap[:, s2, :, 0:64], in_=c_v[l, s2 * 64:(s2 + 1) * 64]),
                  vaug[l], writes=[vaug[l]])
        S.dve(lambda e: e.tensor_copy(out=vaug[l].ap[:, 0:2, :, 128:192], in_=vaug[l].ap[:, 0:2, :, 0:64]), reads=[vaug[l]],
              writes=[vaug[l]])
        S.dma("sp", lambda e: e.dma_start(out=kcs.ap, in_=c_k[l].rearrange("(s r) k d -> r s (k d)", s=2)), kcs, writes=[kcs])
        S.dve(lambda e: e.tensor_copy(out=kcb.ap, in_=kcs.ap), reads=[kcs], writes=[kcb])
        pk = aux(4)
        pkv = pk.ap.bitcast(BF16)[0:64, 0:256].rearrange("p (k t) -> p k t", k=2)
        for s in range(2):
            for kv in range(2):
                S.pe(lambda e, s=s, kv=kv: e.transpose(pkv[:, kv, s * 64:(s + 1) * 64], kcb.ap[:, s, kv * 64:(kv + 1) * 64],
                                                       identb.ap[0:64, 0:64]), reads=[kcb, identb], writes=[pk])
        S.dve(lambda e: e.tensor_copy(out=KTwin[l].ap[:, :, 0:128], in_=pkv), reads=[pk], writes=[KTwin[l]])
        for (o_, c_) in [(O["k_s"], c_k), (O["v_s"], c_v)]:
            S.dma("sp", lambda e, o_=o_, c_=c_: e.dma_start(out=tailk.ap[0:64, :], in_=c_[l, TS:TS + 64].rearrange("r k d -> r (k d)")),
                  tailk, writes=[tailk])
            S.dma("sp", lambda e, o_=o_: e.dma_start(out=o_[l, 0:64].rearrange("r k d -> r (k d)"), in_=tailk.ap[0:64, :]),
                  tailk, reads=[tailk])
            n2 = 128 - TS - 64
            S.dma("sp", lambda e, o_=o_, c_=c_: e.dma_start(out=tailk.ap[0:n2, :],
                                                            in_=c_[l, TS + 64:128].rearrange("r k d -> r (k d)")),
                  tailk, writes=[tailk])
            S.dma("sp", lambda e, o_=o_: e.dma_start(out=o_[l, 64:64 + n2].rearrange("r k d -> r (k d)"), in_=tailk.ap[0:n2, :]),
                  tailk, reads=[tailk])

    def store_states(l, g):
        cols_to_rows_store(hist[l].ap.rearrange("p j c -> p (j c)"), 24,
                           [(0, 24, O["conv_" + g][l].rearrange("j (c p) -> (j c) p", p=128))], [hist[l]])
        cols_to_rows_store(hst[l].ap, 8, [(0, 8, O["lru_" + g][l].rearrange("(c p) -> c p", p=128))], [hst[l]])
        cols_to_rows_store(nst[l].ap.rearrange("p h c -> p (h c)"), 8,
                           [(0, 8, O["n_" + g][l].rearrange("h (c p) -> (h c) p", p=128))], [nst[l]])
        S.dma("sp", lambda e: e.dma_start(out=O["C_" + g][l].rearrange("h (c p) e -> p h c e", p=128), in_=Cst[l].ap), Cst[l],
              reads=[Cst[l]])
        S.dma("sp", lambda e: e.dma_start(out=O["m_" + g][l].rearrange("(h o) -> h o", o=1), in_=mst[l].ap), mst[l],
              reads=[mst[l]])

    FL = [64]

    def mark(name):
        PHASES.append((name, len(S.ops["pe"])))

    def run_tile(src, dst, pos_idx, Tt, L, first_tile, last_tile, is_sample, g):
        mark("load")
        FL[0] = L
        load_rope(pos_idx, Tt, L)
        load_x(src, Tt)
        for l in range(DEPTH):
            mark("norm1")
            norm_to_u(l, "norm1_g", Tt)
            mark("lru")
            lru_phase(l, Tt)
            mark("mlstm")
            mlstm_phase(l, Tt, L)
            mark("attn")
            attn_phase(l, Tt, L, first_tile, is_sample, last_tile, g)
            mark("mlp")
            out_and_mlp(l, Tt)
            if last_tile:
                store_states(l, g)
        mark("store")
        store_y(dst, Tt)

    for l in range(DEPTH):
        init_states_zero(l)
    for t in range(NT):
        run_tile(x_p[t * T:(t + 1) * T, :], O["y_p"][t * T:(t + 1) * T, :], t * T, T, 64, t == 0, t == NT - 1, False, "p")
    if SAMPLE:
        for l in range(DEPTH):
            init_states_sample(l)
        run_tile(x_s, O["y_s"], SP, TS, TS, True, True, True, "s")
    stats = S.emit()
    return nc, stats


PHASES = []
CFG = dict(NT=16, T=256, DEPTH=2)
_cache = {}


def rope_tables(npos_list):
    half = 32
    inv = (10000.0 ** (-np.arange(half, dtype=np.float32) / half)).astype(np.float32)
    pos = np.asarray(npos_list, dtype=np.float32)
    ang = pos[:, None] * inv[None, :]
    return np.cos(ang).astype(np.float32), np.sin(ang).astype(np.float32)


def run(inputs, NT, T, DEPTH, n_cores, past_len=PAST_LEN):
    key = (NT, T, DEPTH)
    if key not in _cache:
        _cache[key] = build(NT, T, DEPTH, True)
    nc, stats = _cache[key]
    SPp = NT * T
    TS = 16
    pos = list(range(SPp)) + [past_len + i for i in range(TS)]
    rc, rs = rope_tables(pos)
    wnames = ["norm1_g", "w_in", "conv_w", "conv_b", "lru_wa", "lru_ba", "lru_wx", "lru_bx", "lru_lam", "m_bi", "m_bf",
              "m_norm_g", "qn_g", "kn_g", "sinks", "w_oa", "w_ob", "w_oc", "b_gate", "w_out", "norm2_g", "w_up", "w_down"]
    f = lambda a: np.ascontiguousarray(np.asarray(a, dtype=np.float32))
    wd = {k: f(inputs[k])[:DEPTH] for k in wnames}
    in_maps = []
    for b in range(n_cores):
        m = dict(wd)
        m["x_p"] = f(inputs["x_prompt"][b, :SPp])
        m["x_s"] = f(inputs["x_sample"][b])
        m["st_conv"] = f(inputs["state_conv"][:DEPTH, b])
        m["st_lru"] = f(inputs["state_lru"][:DEPTH, b])
        m["st_C"] = f(inputs["state_mlstm_C"][:DEPTH, b])
        m["st_n"] = f(inputs["state_mlstm_n"][:DEPTH, b])
        m["st_m"] = f(inputs["state_mlstm_m"][:DEPTH, b])
        m["c_k"] = f(inputs["cache_k"][:DEPTH, b])
        m["c_v"] = f(inputs["cache_v"][:DEPTH, b])
        m["rope_c"] = rc
        m["rope_s"] = rs
        in_maps.append(m)
    res = run_bass_kernel_spmd(nc, in_maps, core_ids=list(range(n_cores)))
    R = res.results
    outs = []
    outs.append(np.stack([np.asarray(R[b]["y_p"]) for b in range(n_cores)]))
    outs.append(np.stack([np.asarray(R[b]["y_s"]) for b in range(n_cores)]))
    for g in ["p", "s"]:
        for nm in ["conv_", "lru_", "C_", "n_", "m_", "k_", "v_"]:
            outs.append(np.stack([np.asarray(R[b][nm + g]) for b in range(n_cores)], axis=1))
    return tuple(o.astype(np.float32) for o in outs)


def kernel(**inputs):
    return run(inputs, CFG["NT"], CFG["T"], CFG["DEPTH"], 8)
```

```python
import numpy as np
from contextlib import ExitStack
import concourse.bass as bass
import concourse.mybir as mybir
from concourse.bass_utils import run_bass_kernel_spmd

F32 = mybir.dt.float32
BF16 = mybir.dt.bfloat16
AF = mybir.ActivationFunctionType
ALU = mybir.AluOpType
AX = mybir.AxisListType

D = 1024
EPS = 1e-6
IN_COLS = 10504
O_XA, O_GA, O_MQ, O_MK, O_MV, O_MO, O_MI, O_MF, O_AQ, O_AK, O_AV, O_G = (
    0, 1024, 2048, 3072, 4096, 5120, 6144, 6148, 6152, 7176, 7304, 7432)
PAST_LEN = 2048


class Sub:
    __slots__ = ("writer", "readers", "dma_readers")

    def __init__(self):
        self.writer = None
        self.readers = {}
        self.dma_readers = {}


class Buf:
    def __init__(self, name, ap, nsub=1):
        self.name = name
        self.ap = ap
        self.subs = [Sub() for _ in range(nsub)]
        self.dma_sem = None
        self.dma_count = 0

    def __getitem__(self, i):
        return (self, i)


def view(buf, ap):
    v = Buf(buf.name + "_v", ap, 0)
    v.subs = buf.subs
    return v


class ChunkView:
    def __init__(self, buf, ap, per):
        self.buf = buf
        self.ap = ap
        self.per = per
        self.subs = buf.subs

    def __getitem__(self, ch):
        return (self.buf, range(ch * self.per, (ch + 1) * self.per))


class Op:
    __slots__ = ("eng", "fn", "deps", "dma_waits", "signal", "is_dma", "buf", "count", "idx")

    def __init__(self, eng, fn):
        self.eng = eng
        self.fn = fn
        self.deps = []
        self.dma_waits = []
        self.signal = False
        self.is_dma = False
        self.buf = None
        self.count = None


def _subs(refs):
    out = []
    for r in refs:
        if isinstance(r, (Buf, ChunkView)):
            out.extend(r.subs)
        else:
            b, i = r
            if isinstance(i, (list, tuple, range)):
                out.extend(b.subs[j] for j in i)
            else:
                out.append(b.subs[i])
    return out


class Sched:
    ENGS = ["pe", "act", "dve", "pool", "sp"]

    def __init__(self, nc):
        self.nc = nc
        self.ops = {e: [] for e in self.ENGS}
        self.dma_bufs = []

    def add(self, eng, fn, reads=(), writes=(), dma_buf=None):
        op = Op(eng, fn)
        op.idx = len(self.ops[eng])
        need = {}
        dneed = {}

        def dep_on(d):
            if d is op:
                return
            if d.is_dma:
                dneed[id(d.buf)] = d.buf
            else:
                if d.eng == "pe" and eng == "pe":
                    return
                cur = need.get(d.eng)
                if cur is None or d.idx > cur.idx:
                    need[d.eng] = d

        rs = _subs(reads)
        ws = _subs(writes)
        for s in rs:
            if s.writer is not None:
                dep_on(s.writer)
        for s in ws:
            if s.writer is not None:
                dep_on(s.writer)
            for r in s.readers.values():
                dep_on(r)
            for b in s.dma_readers.values():
                dneed[id(b)] = b
        for d in need.values():
            op.deps.append(d)
            d.signal = True
        for b in dneed.values():
            op.dma_waits.append((b, b.dma_count))
        if dma_buf is not None:
            op.is_dma = True
            op.buf = dma_buf
            if dma_buf.dma_sem is None:
                dma_buf.dma_sem = "pending"
                self.dma_bufs.append(dma_buf)
            dma_buf.dma_count += 16
        for s in rs:
            if op.is_dma:
                s.dma_readers[id(dma_buf)] = dma_buf
            else:
                s.readers[eng] = op
        for s in ws:
            s.writer = op
            s.readers = {}
            s.dma_readers = {}
        self.ops[eng].append(op)
        return op

    def pe(self, fn, reads=(), writes=()):
        return self.add("pe", fn, reads, writes)

    def act(self, fn, reads=(), writes=()):
        return self.add("act", fn, reads, writes)

    def dve(self, fn, reads=(), writes=()):
        return self.add("dve", fn, reads, writes)

    def pool(self, fn, reads=(), writes=()):
        return self.add("pool", fn, reads, writes)

    def dma(self, eng, fn, buf, reads=(), writes=()):
        return self.add(eng, fn, reads, writes, dma_buf=buf)

    def emit(self):
        nc = self.nc
        with ExitStack() as st:
            esem = {}
            for e in ["pe", "act", "dve", "pool"]:
                esem[e] = st.enter_context(nc.semaphore("es_" + e))
            for b in self.dma_bufs:
                b.dma_sem = st.enter_context(nc.semaphore("ds_" + b.name))
            for e in ["pe", "act", "dve", "pool"]:
                c = 0
                for op in self.ops[e]:
                    if op.is_dma:
                        continue
                    if op.signal:
                        c += 1
                        op.count = c
            stats = {}
            block = st.enter_context(nc.Block())

            def run(e, engobj):
                waited = {}
                nw = 0
                for op in self.ops[e]:
                    for d in op.deps:
                        if waited.get(d.eng, 0) >= d.count:
                            continue
                        waited[d.eng] = d.count
                        engobj.wait_ge(esem[d.eng], d.count)
                        nw += 1
                    for (b, v) in op.dma_waits:
                        if waited.get(id(b), 0) >= v:
                            continue
                        waited[id(b)] = v
                        engobj.wait_ge(b.dma_sem, v)
                        nw += 1
                    inst = op.fn(engobj)
                    if op.is_dma:
                        inst.then_inc(op.buf.dma_sem, 16)
                    elif op.signal:
                        inst.then_inc(esem[e], 1)
                if e == "sp":
                    for ee in ["pe", "act", "dve", "pool"]:
                        last = 0
                        for op in self.ops[ee]:
                            if op.count:
                                last = op.count
                        if last:
                            engobj.wait_ge(esem[ee], last)
                    for b in self.dma_bufs:
                        engobj.wait_ge(b.dma_sem, b.dma_count)
                stats[e] = (len(self.ops[e]), nw)

            @block.tensor
            def _(eng):
                run("pe", eng)

            @block.scalar
            def _(eng):
                run("act", eng)

            @block.vector
            def _(eng):
                run("dve", eng)

            @block.gpsimd
            def _(eng):
                run("pool", eng)

            @block.sync
            def _(eng):
                run("sp", eng)

            return stats


def build(NT, T, DEPTH, SAMPLE, TS=16):
    nc = bass.Bass("TRN2", target_bir_lowering=False)
    S = Sched(nc)
    SP = NT * T
    NPOS = SP + TS

    def din(name, shape):
        return nc.dram_tensor(name, list(shape), F32, kind="ExternalInput").ap()

    def dout(name, shape):
        return nc.dram_tensor(name, list(shape), F32, kind="ExternalOutput").ap()

    x_p = din("x_p", [SP, D])
    x_s = din("x_s", [TS, D])
    st_conv = din("st_conv", [DEPTH, 3, D])
    st_lru = din("st_lru", [DEPTH, D])
    st_C = din("st_C", [DEPTH, 4, 256, 256])
    st_n = din("st_n", [DEPTH, 4, 256])
    st_m = din("st_m", [DEPTH, 4])
    c_k = din("c_k", [DEPTH, 128, 2, 64])
    c_v = din("c_v", [DEPTH, 128, 2, 64])
    rope_c = din("rope_c", [NPOS, 32])
    rope_s = din("rope_s", [NPOS, 32])
    W = {}
    for nm, shp in [("norm1_g", [DEPTH, D]), ("w_in", [DEPTH, D, IN_COLS]), ("conv_w", [DEPTH, 4, D]),
                    ("conv_b", [DEPTH, D]), ("lru_wa", [DEPTH, 8, 128, 128]), ("lru_ba", [DEPTH, D]),
                    ("lru_wx", [DEPTH, 8, 128, 128]), ("lru_bx", [DEPTH, D]), ("lru_lam", [DEPTH, D]),
                    ("m_bi", [DEPTH, 4]), ("m_bf", [DEPTH, 4]), ("m_norm_g", [DEPTH, D]), ("qn_g", [DEPTH, 64]),
                    ("kn_g", [DEPTH, 64]), ("sinks", [DEPTH, 16]), ("w_oa", [DEPTH, D, D]), ("w_ob", [DEPTH, D, D]),
                    ("w_oc", [DEPTH, D, D]), ("b_gate", [DEPTH, 3, D]), ("w_out", [DEPTH, D, D]),
                    ("norm2_g", [DEPTH, D]), ("w_up", [DEPTH, D, 4096]), ("w_down", [DEPTH, 4096, D])]:
        W[nm] = din(nm, shp)
    O = {}
    O["y_p"] = dout("y_p", [SP, D])
    O["y_s"] = dout("y_s", [TS, D])
    for g in ["p", "s"]:
        O["conv_" + g] = dout("conv_" + g, [DEPTH, 3, D])
        O["lru_" + g] = dout("lru_" + g, [DEPTH, D])
        O["C_" + g] = dout("C_" + g, [DEPTH, 4, 256, 256])
        O["n_" + g] = dout("n_" + g, [DEPTH, 4, 256])
        O["m_" + g] = dout("m_" + g, [DEPTH, 4])
        O["k_" + g] = dout("k_" + g, [DEPTH, 128, 2, 64])
        O["v_" + g] = dout("v_" + g, [DEPTH, 128, 2, 64])

    cnt = [0]

    def sb(shape, dt=F32, nsub=1, name=None):
        cnt[0] += 1
        nm = (name or "t") + "_%d" % cnt[0]
        t = nc.alloc_sbuf_tensor(nm, list(shape), dt)
        return Buf(nm, t.ap(), nsub)

    banks = []
    for i in range(8):
        t = nc.alloc_psum_tensor("bank%d" % i, [128, 512], F32)
        banks.append(Buf("bank%d" % i, t.ap(), 5))
    mm_ring = [0]
    NMM = 2

    def ps_mm():
        b = banks[mm_ring[0] % 8]
        mm_ring[0] += 1
        return b

    def aux(role):
        return banks[NMM + role]

    ev = [0]

    def evac(fn_act, fn_dve, reads, writes):
        ev[0] += 1
        if ev[0] % 2 == 0:
            return S.act(fn_act, reads, writes)
        return S.dve(fn_dve, reads, writes)

    def copy_any(out_ap, in_ap, reads, writes, scale=None):
        if scale is None:
            return evac(lambda e: e.activation(out=out_ap, in_=in_ap, func=AF.Copy),
                        lambda e: e.tensor_copy(out=out_ap, in_=in_ap), reads, writes)
        return evac(lambda e: e.activation(out=out_ap, in_=in_ap, func=AF.Copy, scale=scale),
                    lambda e: e.tensor_scalar(out=out_ap, in0=in_ap, scalar1=scale, scalar2=None, op0=ALU.mult),
                    reads, writes)

    ident = sb([128, 128], F32, name="ident")
    identb = sb([128, 128], BF16, name="identb")
    ones_bf = sb([128, 128], BF16, name="onesbf")
    ones32 = sb([4, 128], F32, name="ones32")
    cmask = sb([64, 64], F32, name="cmask")
    hmask = sb([4, 4, 8], F32, name="hmask")
    maskrow = sb([4, 512], F32, name="maskrow")
    S.pool(lambda e: e.memset(ident.ap, 0.0), writes=[ident])
    S.pool(lambda e: e.affine_select(out=ident.ap, in_=ident.ap, pattern=[[-1, 128]], compare_op=ALU.not_equal,
                                     fill=1.0, base=0, channel_multiplier=1), reads=[ident], writes=[ident])
    S.dve(lambda e: e.tensor_copy(out=identb.ap, in_=ident.ap), reads=[ident], writes=[identb])
    S.dve(lambda e: e.memset(ones_bf.ap, 1.0), writes=[ones_bf])
    S.dve(lambda e: e.memset(ones32.ap, 1.0), writes=[ones32])
    S.pool(lambda e: e.memset(cmask.ap, 1.0), writes=[cmask])
    S.pool(lambda e: e.affine_select(out=cmask.ap, in_=cmask.ap, pattern=[[1, 64]], compare_op=ALU.is_ge,
                                     fill=0.0, base=0, channel_multiplier=-1), reads=[cmask], writes=[cmask])
    S.pool(lambda e: e.memset(hmask.ap, 1.0), writes=[hmask])
    S.pool(lambda e: e.affine_select(out=hmask.ap, in_=hmask.ap, pattern=[[1, 4], [0, 8]], compare_op=ALU.is_equal,
                                     fill=0.0, base=0, channel_multiplier=-1), reads=[hmask], writes=[hmask])
    S.dve(lambda e: e.memset(maskrow.ap, 1.0), writes=[maskrow])
    S.dve(lambda e: e.memset(maskrow.ap.rearrange("p (c l) -> p c l", l=64)[:, :, 0:1], 0.0), writes=[maskrow])

    VEC = ["norm1_g", "norm2_g", "conv_b", "lru_ba", "lru_bx", "lru_lam", "cw0", "cw1", "cw2", "cw3", "bg0", "bg1", "bg2"]
    NV = len(VEC)
    colv = [sb([128, NV * 8], F32, name="colv") for _ in range(DEPTH)]
    nsp8 = [sb([128, 8], F32, name="nsp8") for _ in range(DEPTH)]
    nsp4 = [sb([128, 8], F32, name="nsp4") for _ in range(DEPTH)]
    hb = [sb([128, 16], F32, name="hb") for _ in range(DEPTH)]
    wa_bf = [sb([128, 8, 128], BF16, name="wa") for _ in range(DEPTH)]
    wx_bf = [sb([128, 8, 128], BF16, name="wx") for _ in range(DEPTH)]
    mg_row1 = sb([64, D], F32, name="mgrow")
    mg_row = [mg_row1 for _ in range(DEPTH)]
    qg_row = [sb([128, 64], F32, name="qgrow") for _ in range(DEPTH)]
    kg_row = [sb([64, 64], F32, name="kgrow") for _ in range(DEPTH)]
    esink = [sb([128, 16], F32, name="esink") for _ in range(DEPTH)]
    bi_col = [sb([4, 1], F32, name="bi") for _ in range(DEPTH)]
    nbf_col = [sb([4, 1], F32, name="nbf") for _ in range(DEPTH)]
    vstage = sb([128, 128], F32, name="vstage")

    def vcol(l, name, c):
        i = VEC.index(name)
        return colv[l].ap[:, i * 8 + c:i * 8 + c + 1]

    for l in range(DEPTH):
        srcs = {"norm1_g": W["norm1_g"][l], "norm2_g": W["norm2_g"][l], "conv_b": W["conv_b"][l],
                "lru_ba": W["lru_ba"][l], "lru_bx": W["lru_bx"][l], "lru_lam": W["lru_lam"][l]}
        for j in range(4):
            srcs["cw%d" % j] = W["conv_w"][l, j]
        for j in range(3):
            srcs["bg%d" % j] = W["b_gate"][l, j]
        for i, nm in enumerate(VEC):
            src = srcs[nm].rearrange("(c p) -> c p", p=128)
            S.dma("sp", lambda e, i=i, src=src: e.dma_start(out=vstage.ap[i * 8:(i + 1) * 8, :], in_=src), vstage,
                  writes=[vstage])
        pb = aux(5)
        S.pe(lambda e, pb=pb: e.transpose(pb.ap[:, 0:NV * 8], vstage.ap[0:NV * 8, :], ident.ap[0:NV * 8, 0:NV * 8]),
             reads=[vstage, ident], writes=[pb])
        S.dve(lambda e, pb=pb, l=l: e.tensor_copy(out=colv[l].ap, in_=pb.ap[:, 0:NV * 8]), reads=[pb], writes=[colv[l]])
        lam = colv[l].ap[:, VEC.index("lru_lam") * 8:VEC.index("lru_lam") * 8 + 8]
        S.act(lambda e, l=l, lam=lam: e.activation(out=nsp8[l].ap, in_=lam, func=AF.Exp, scale=-1.0),
              reads=[colv[l]], writes=[nsp8[l]])
        S.act(lambda e, l=l: e.activation(out=nsp8[l].ap, in_=nsp8[l].ap, func=AF.Ln, bias=1.0),
              reads=[nsp8[l]], writes=[nsp8[l]])
        S.dve(lambda e, l=l: e.tensor_scalar(out=nsp4[l].ap, in0=nsp8[l].ap, scalar1=-4.0, scalar2=None, op0=ALU.mult),
              reads=[nsp8[l]], writes=[nsp4[l]])
        S.dve(lambda e, l=l: e.tensor_scalar(out=nsp8[l].ap, in0=nsp8[l].ap, scalar1=-8.0, scalar2=None, op0=ALU.mult),
              reads=[nsp8[l], nsp4[l]], writes=[nsp8[l]])
        _ib = VEC.index("lru_ba") * 8
        S.dve(lambda e, l=l, _ib=_ib: e.tensor_scalar(out=hb[l].ap, in0=colv[l].ap[:, _ib:_ib + 16], scalar1=0.5, scalar2=None,
                                                      op0=ALU.mult), reads=[colv[l]], writes=[hb[l]])
        S.dma("pool", lambda e, l=l: e.dma_start(out=wa_bf[l].ap, in_=W["lru_wa"][l].rearrange("n c d -> c n d")),
              wa_bf[l], writes=[wa_bf[l]])
        S.dma("pool", lambda e, l=l: e.dma_start(out=wx_bf[l].ap, in_=W["lru_wx"][l].rearrange("n c d -> c n d")),
              wx_bf[l], writes=[wx_bf[l]])
        S.dma("sp", lambda e, l=l: e.dma_start(out=qg_row[l].ap, in_=W["qn_g"][l].partition_broadcast(128)),
              qg_row[l], writes=[qg_row[l]])
        S.dma("sp", lambda e, l=l: e.dma_start(out=kg_row[l].ap, in_=W["kn_g"][l].partition_broadcast(64)),
              kg_row[l], writes=[kg_row[l]])
        S.dma("sp", lambda e, l=l: e.dma_start(out=esink[l].ap, in_=W["sinks"][l].partition_broadcast(128)),
              esink[l], writes=[esink[l]])
        S.act(lambda e, l=l: e.activation(out=esink[l].ap, in_=esink[l].ap, func=AF.Exp), reads=[esink[l]],
              writes=[esink[l]])
        S.dma("sp", lambda e, l=l: e.dma_start(out=bi_col[l].ap, in_=W["m_bi"][l].rearrange("(h o) -> h o", o=1)),
              bi_col[l], writes=[bi_col[l]])
        S.dma("sp", lambda e, l=l: e.dma_start(out=nbf_col[l].ap, in_=W["m_bf"][l].rearrange("(h o) -> h o", o=1)),
              nbf_col[l], writes=[nbf_col[l]])
        S.dve(lambda e, l=l: e.tensor_scalar(out=nbf_col[l].ap, in0=nbf_col[l].ap, scalar1=-1.0, scalar2=None,
                                             op0=ALU.mult), reads=[nbf_col[l]], writes=[nbf_col[l]])

    SCR = {}
    SCRB = {}
    for nm, R_, C_ in [("w_in", D, IN_COLS), ("w_oa", D, D), ("w_ob", D, D), ("w_oc", D, D), ("w_out", D, D),
                       ("w_up", D, 4096), ("w_down", 4096, D)]:
        SCR[nm] = nc.dram_tensor(nm + "_bf", [DEPTH, R_, C_], BF16, kind="Internal").ap()
    WIN_GROUPS = [(0, 2048), (2048, 4096), (O_G, O_G + 1024), (4096, O_AQ), (O_G + 1024, O_G + 2048), (O_AQ, O_G),
                  (O_G + 2048, IN_COLS)]
    for l in range(DEPTH):
        for gi_, (g0, g1) in enumerate(WIN_GROUPS):
            b_ = Buf("w_in_bf%d_%d" % (l, gi_), None, 2)
            SCRB[("w_in", l, gi_)] = b_
            for rb in range(2):
                S.dma("pool", lambda e, l=l, rb=rb, g0=g0, g1=g1: e.dma_start(out=SCR["w_in"][l, rb * 512:(rb + 1) * 512, g0:g1],
                                                                              in_=W["w_in"][l, rb * 512:(rb + 1) * 512, g0:g1]),
                      b_, writes=[b_[rb]])
        for nm in ["w_oa", "w_ob", "w_oc", "w_out", "w_up", "w_down"]:
            R_ = W[nm].shape[1]
            nblk = R_ // 256
            b_ = Buf("%s_bf%d" % (nm, l), None, nblk)
            SCRB[(nm, l)] = b_
            for rb in range(nblk):
                S.dma("pool", lambda e, nm=nm, l=l, rb=rb: e.dma_start(out=SCR[nm][l, rb * 256:(rb + 1) * 256, :],
                                                                       in_=W[nm][l, rb * 256:(rb + 1) * 256, :]),
                      b_, writes=[b_[rb]])

    NCHM = max(T // 64, 1)
    hist = [sb([128, 3, 8], F32, name="hist") for _ in range(DEPTH)]
    hst = [sb([128, 8], F32, name="hst") for _ in range(DEPTH)]
    Cst = [sb([128, 4, 2, 256], F32, nsub=4, name="Cst") for _ in range(DEPTH)]
    nst = [sb([128, 4, 2], F32, name="nst") for _ in range(DEPTH)]
    mst = [sb([4, 1], F32, name="mst") for _ in range(DEPTH)]
    NSLOT = 2 + NCHM
    KTwin = [sb([64, 2, NSLOT * 64], BF16, name="KTwin") for _ in range(DEPTH)]
    vaug = [sb([64, NSLOT, 2, 192], BF16, name="vaug") for _ in range(DEPTH)]

    xT = sb([128, 8, T], F32, nsub=8, name="xT")
    uT = sb([128, 8, T], BF16, nsub=8, name="uT")
    NSLAB = 4
    slabs = [sb([128, 4096], BF16, name="slab") for _ in range(NSLAB)]
    slab_i = [0]
    TB = min(T, 128)
    xin = [sb([128, D], F32, name="xin") for _ in range(2)]
    sqb = [sb([128, T], BF16, name="sqb") for _ in range(2)]
    rstd = sb([128, T], F32, name="rstd")
    xa_w = [sb([128, 3 + T], F32, name="xaw") for _ in range(2)]
    xc3 = [sb([128, T], F32, name="xc") for _ in range(3)]
    xcb = [sb([128, T], BF16, name="xcb") for _ in range(2)]
    rr = [sb([128, T], F32, name="rr") for _ in range(2)]
    ii = [sb([128, T], F32, name="ii") for _ in range(2)]
    aa = [sb([128, T], F32, name="aa") for _ in range(2)]
    sq1 = [sb([128, T], F32, name="sq1") for _ in range(2)]
    hh = [sb([128, T], F32, name="hh") for _ in range(2)]
    gel3 = [sb([128, T], F32, name="gel") for _ in range(3)]
    gel = gel3
    hgT = sb([128, 8, T], BF16, nsub=8, name="hgT")
    sg = [gel[0], gel[1]]
    tmpm = [hh[0], hh[1]]
    mix = sb([128, 8, T], F32, nsub=8, name="mix")
    qT = sb([128, 8, T], BF16, nsub=8, name="qT")
    kT = sb([128, 8, T], BF16, nsub=8, name="kT")
    g64 = [sb([64, 16, T], BF16, nsub=16, name="g64") for _ in range(4)]
    PER = 16 // NCHM

    def cview(g, vw4=False):
        flat = g.ap.rearrange("p h t -> p (h t)")
        if vw4:
            return ChunkView(g, flat.rearrange("p (c h e) -> p c h e", c=NCHM, h=4), PER)
        return ChunkView(g, flat.rearrange("p (c f) -> p c f", c=NCHM), PER)

    ktok = cview(g64[0])
    vw = cview(g64[1], True)
    sgm = cview(g64[2])
    hmtok = cview(g64[3])
    hmT = sb([128, 8, T], BF16, nsub=8, name="hmT")
    _gsrc = [rr[0], rr[1], ii[0], ii[1], aa[0], aa[1], sq1[0], sq1[1]]
    grow = {nm: view(_gsrc[i], _gsrc[i].ap[0:4, :]) for i, nm in enumerate(["ig", "sp", "b", "a", "ea", "cl", "iwt", "tmp"])}
    gsm = {nm: sb([4, NCHM], F32, name="gs_" + nm) for nm in ["amax", "d0", "M", "mnew", "mprev", "iw"]}
    rhsm = sb([4, 4, NCHM], F32, name="rhsm")
    iw_rep = sb([128, 4 * NCHM], F32, name="iwrep")
    colq = sb([64, NCHM, 12], F32, name="colq")
    eab = sb([64, NCHM, 4], BF16, name="eab")
    Cnb = [sb([128, 2, 256], BF16, name="Cnb") for _ in range(2)]
    nb = [sb([128, 2], BF16, name="nb") for _ in range(4)]
    nb4 = nb
    smask = [sb([64, 4, 64], BF16, name="smask") for _ in range(2)]
    den_s = [sb([64, 4], F32, name="dens") for _ in range(2)]
    den_t = [sb([64, 4], F32, name="dent") for _ in range(2)]
    hn = [view(xin[i], xin[i].ap[0:64, :].rearrange("p (h d) -> p h d", h=4)) for i in range(2)]
    ssm = [sb([64, 4], F32, name="ssm") for _ in range(2)]
    NBLK = (T + TB - 1) // TB
    ctab_k = sb([64, NCHM, 32], F32, name="ctabk")
    stab_k = sb([64, NCHM, 32], F32, name="stabk")
    ctab_q = sb([128, NBLK, 32], F32, name="ctabq")
    stab_q = sb([128, NBLK, 32], F32, name="stabq")
    QT_all = g64[0]
    OT2 = hmT
    qsq = sb([128, 512], F32, name="qsq")
    qss = [sb([128, 8], F32, name="qss") for _ in range(2)]
    qn = [sb([128, 8, 64], F32, name="qn") for _ in range(2)]
    qt1 = sb([128, 8, 32], F32, name="qt1")
    qt2 = sb([128, 8, 32], F32, name="qt2")
    qr = [sb([128, 8, 64], BF16, name="qr") for _ in range(2)]

    def _as_cnb(b_):
        return view(b_, b_.ap.rearrange("p a b -> p (a b)").bitcast(BF16).rearrange("p (c e) -> p c e", c=2))

    Cnb4 = [Cnb[0], Cnb[1], _as_cnb(qt1), _as_cnb(qt2)]
    kss = [sb([64, 2], F32, name="kss") for _ in range(2)]
    ksq = sb([64, 128], F32, name="ksq")
    kn = [sb([64, 2, 64], F32, name="kn") for _ in range(2)]
    kt1 = sb([64, 2, 32], F32, name="kt1")
    kt2 = sb([64, 2, 32], F32, name="kt2")
    kr = [sb([64, 128], F32, name="kr") for _ in range(2)]
    krb = [sb([64, 128], BF16, name="krb") for _ in range(2)]
    vf = [sb([64, 128], F32, name="vf") for _ in range(2)]
    pT = [sb([64, 512], BF16, name="pT") for _ in range(6)]
    pT_i = [0]
    dsum = [sb([128, 512], F32, name="dsum") for _ in range(2)]
    hsq = view(dsum[0], dsum[0].ap[0:64, :].bitcast(BF16).rearrange("p (h d) -> p h d", h=4))
    hid_parts = [hgT, qT, kT, hmT]
    rl = [rr[0], rr[1]]
    yout = xin
    kcs = view(qsq, qsq.ap[0:64, 0:256].rearrange("p (s f) -> p s f", s=2))
    kcb = sb([64, 2, 128], BF16, name="kcb")
    tailk = vf[0]

    def pipeline(gens, newest_first):
        gens = list(gens)
        active = []
        i = 0
        while i < len(gens) or active:
            if i < len(gens):
                active.append(gens[i])
                i += 1
            order = list(reversed(active)) if newest_first else list(active)
            for g in order:
                try:
                    next(g)
                except StopIteration:
                    active.remove(g)

    slab_live = [False] * NSLAB

    def release(sl):
        slab_live[slabs.index(sl)] = False

    def load_slab(src_ap, view, srcbuf, hold=False):
        for _ in range(NSLAB + 1):
            i_ = slab_i[0] % NSLAB
            slab_i[0] += 1
            if not slab_live[i_]:
                break
        else:
            raise RuntimeError("all slabs live")
        sl = slabs[i_]
        if hold:
            slab_live[i_] = True
        dst = view(sl.ap)
        S.dma("sp", lambda e: e.dma_start(out=dst, in_=src_ap), sl, reads=[srcbuf], writes=[sl])
        return sl, dst

    def slab_k8(l, wname, c0, ncols, hold=False):
        src = SCR[wname][l][:, c0:c0 + ncols].rearrange("(kc p) n -> p kc n", p=128)
        if wname == "w_in":
            gi_ = [i for i, (g0, g1) in enumerate(WIN_GROUPS) if g0 <= c0 and c0 + ncols <= g1]
            assert len(gi_) == 1, (c0, ncols)
            sb_ = SCRB[("w_in", l, gi_[0])]
        else:
            sb_ = SCRB[(wname, l)]
        return load_slab(src, lambda a: a[:, 0:8 * ncols].rearrange("p (k n) -> p k n", k=8), sb_, hold=hold)

    def qk_proj_gen(l, Tt, L):
        NCH = Tt // L
        for half in range(2):
            sl, v = slab_k8(l, "w_in", O_MQ + half * 512, 512, hold=True)
            for c4 in range(4):
                c = half * 4 + c4
                pb = ps_mm()
                fm_proj(pb, sl, v, c4 * 128, uT, Tt)
                S.dve(lambda e, pb=pb, c=c: e.tensor_copy(out=qT.ap[:, c, 0:Tt], in_=pb.ap[:, 0:Tt]), reads=[pb], writes=[qT[c]])
                yield
            release(sl)
        for half in range(2):
            sl, v = slab_k8(l, "w_in", O_MK + half * 512, 512, hold=True)
            for c4 in range(4):
                c = half * 4 + c4
                pb = ps_mm()
                fm_proj(pb, sl, v, c4 * 128, uT, Tt)
                S.dve(lambda e, pb=pb, c=c: e.tensor_scalar(out=kT.ap[:, c, 0:Tt], in0=pb.ap[:, 0:Tt], scalar1=1.0 / 16.0,
                                                            scalar2=None, op0=ALU.mult), reads=[pb], writes=[kT[c]])
                yield
            for ch in range(NCH):
                pb = ps_mm()
                for k in range(8):
                    S.pe(lambda e, k=k, pb=pb, ch=ch, v=v: e.matmul(pb.ap[0:L, :], uT.ap[:, k, ch * L:(ch + 1) * L], v[:, k, :],
                                                                    start=(k == 0), stop=(k == 7)),
                         reads=[sl, uT[k]], writes=[pb])
                S.dve(lambda e, pb=pb, ch=ch, half=half: e.tensor_scalar(out=ktok.ap[0:L, ch, half * 512:(half + 1) * 512],
                                                                         in0=pb.ap[0:L, :], scalar1=1.0 / 16.0, scalar2=None,
                                                                         op0=ALU.mult), reads=[pb], writes=[ktok[ch]])
                yield
            release(sl)

    def fm_proj(pb, sl, sview, col, act, Tt, KC=8):
        for k in range(KC):
            S.pe(lambda e, k=k: e.matmul(pb.ap[:, 0:Tt], sview[:, k, col:col + 128], act.ap[:, k, 0:Tt],
                                         start=(k == 0), stop=(k == KC - 1)),
                 reads=[sl, act[k]], writes=[pb])

    def norm_to_u(l, gname, Tt):
        pb = aux(0)
        for c in range(8):
            q = sqb[c % 2]
            S.act(lambda e, c=c, q=q: e.activation(out=q.ap[:, 0:Tt], in_=xT.ap[:, c, 0:Tt], func=AF.Square),
                  reads=[xT[c]], writes=[q])
            S.pe(lambda e, c=c, q=q: e.matmul(pb.ap[:, 0:Tt], ones_bf.ap, q.ap[:, 0:Tt], start=(c == 0), stop=(c == 7)),
                 reads=[q, ones_bf], writes=[pb])
        S.act(lambda e: e.activation(out=rstd.ap[:, 0:Tt], in_=pb.ap[:, 0:Tt], func=AF.Ln, scale=1.0 / D, bias=EPS),
              reads=[pb], writes=[rstd])
        S.act(lambda e: e.activation(out=rstd.ap[:, 0:Tt], in_=rstd.ap[:, 0:Tt], func=AF.Exp, scale=-0.5), reads=[rstd], writes=[rstd])
        for c in range(8):
            S.dve(lambda e, c=c: e.scalar_tensor_tensor(out=uT.ap[:, c, 0:Tt], in0=xT.ap[:, c, 0:Tt],
                                                        scalar=vcol(l, gname, c), in1=rstd.ap[:, 0:Tt],
                                                        op0=ALU.mult, op1=ALU.mult),
                  reads=[xT[c], rstd, colv[l]], writes=[uT[c]])

    def merge_branch(l, br, featT, Tt, K64=False):
        wname = ["w_oa", "w_ob", "w_oc"][br]
        for half in range(2):
            slg, vg = slab_k8(l, "w_in", O_G + br * 1024 + half * 512, 512)
            if not K64:
                slo, vo = slab_k8(l, wname, half * 512, 512)
            for c4 in range(4):
                c = half * 4 + c4
                pg = ps_mm()
                fm_proj(pg, slg, vg, c4 * 128, uT, Tt)
                s_ = sg[c % 2]
                S.act(lambda e, pg=pg, s_=s_, c=c: e.activation(out=s_.ap[:, 0:Tt], in_=pg.ap[:, 0:Tt], func=AF.Sigmoid,
                                                                bias=vcol(l, "bg%d" % br, c)),
                      reads=[pg, colv[l]], writes=[s_])
                py = ps_mm()
                if not K64:
                    fm_proj(py, slo, vo, c4 * 128, featT, Tt)
                else:
                    if c4 % 2 == 0:
                        src = SCR["w_oc"][l][:, c * 128:c * 128 + 256].rearrange("(h d) n -> d h n", d=64)
                        slo, vo = load_slab(src, lambda a: a[0:64, :].rearrange("p (h n) -> p h n", h=16), SCRB[("w_oc", l)])
                    off = (c4 % 2) * 128
                    for h in range(16):
                        S.pe(lambda e, h=h, py=py, vo=vo, off=off: e.matmul(py.ap[:, 0:Tt], vo[:, h, off:off + 128],
                                                                            featT.ap[:, h, 0:Tt], start=(h == 0),
                                                                            stop=(h == 15)),
                             reads=[slo, featT[h]], writes=[py])
                if br == 0:
                    S.dve(lambda e, py=py, s_=s_, c=c: e.tensor_tensor(out=mix.ap[:, c, 0:Tt], in0=py.ap[:, 0:Tt],
                                                                       in1=s_.ap[:, 0:Tt], op=ALU.mult),
                          reads=[py, s_], writes=[mix[c]])
                else:
                    t_ = tmpm[c % 2]
                    S.dve(lambda e, py=py, s_=s_, t_=t_: e.tensor_tensor(out=t_.ap[:, 0:Tt], in0=py.ap[:, 0:Tt],
                                                                         in1=s_.ap[:, 0:Tt], op=ALU.mult),
                          reads=[py, s_], writes=[t_])
                    S.pool(lambda e, t_=t_, c=c: e.tensor_tensor(out=mix.ap[:, c, 0:Tt], in0=mix.ap[:, c, 0:Tt],
                                                                 in1=t_.ap[:, 0:Tt], op=ALU.add),
                           reads=[t_, mix[c]], writes=[mix[c]])

    def lru_phase(l, Tt):
        sl_ = {}

        def body(c):
            half, c4 = divmod(c, 4)
            if c4 == 0:
                if "a" in sl_:
                    release(sl_["a"][0])
                    release(sl_["g"][0])
                sl_["a"] = slab_k8(l, "w_in", O_XA + half * 512, 512, hold=True)
                sl_["g"] = slab_k8(l, "w_in", O_GA + half * 512, 512, hold=True)
            sla, va = sl_["a"]
            slg, vg = sl_["g"]
            j = c % 2
            j3 = c % 3
            pa = ps_mm()
            fm_proj(pa, sla, va, c4 * 128, uT, Tt)
            xw = xa_w[j]
            S.dve(lambda e: e.tensor_copy(out=xw.ap[:, 0:3], in_=hist[l].ap[:, :, c]), reads=[hist[l]], writes=[xw])
            S.act(lambda e: e.activation(out=xw.ap[:, 3:3 + Tt], in_=pa.ap[:, 0:Tt], func=AF.Copy), reads=[pa], writes=[xw])
            S.dve(lambda e: e.tensor_copy(out=hist[l].ap[:, :, c], in_=xw.ap[:, Tt:Tt + 3]), reads=[xw], writes=[hist[l]])
            x_ = xc3[j3]
            S.dve(lambda e: e.tensor_scalar(out=x_.ap[:, 0:Tt], in0=xw.ap[:, 0:Tt], scalar1=vcol(l, "cw0", c),
                                            scalar2=vcol(l, "conv_b", c), op0=ALU.mult, op1=ALU.add),
                  reads=[xw, colv[l]], writes=[x_])
            for jj in range(1, 4):
                S.dve(lambda e, jj=jj: e.scalar_tensor_tensor(out=x_.ap[:, 0:Tt], in0=xw.ap[:, jj:jj + Tt],
                                                              scalar=vcol(l, "cw%d" % jj, c), in1=x_.ap[:, 0:Tt],
                                                              op0=ALU.mult, op1=ALU.add), reads=[xw, x_, colv[l]], writes=[x_])
            xb_ = xcb[j]
            S.dve(lambda e: e.tensor_copy(out=xb_.ap[:, 0:Tt], in_=x_.ap[:, 0:Tt]), reads=[x_], writes=[xb_])
            pg = ps_mm()
            fm_proj(pg, slg, vg, c4 * 128, uT, Tt)
            g_ = gel3[j3]
            S.act(lambda e: e.activation(out=g_.ap[:, 0:Tt], in_=pg.ap[:, 0:Tt], func=AF.Gelu_apprx_tanh), reads=[pg], writes=[g_])
            yield
            pr = aux(1)
            S.pe(lambda e: e.matmul(pr.ap[:, 0:Tt], wa_bf[l].ap[:, c, :], xb_.ap[:, 0:Tt], start=True, stop=True),
                 reads=[wa_bf[l], xb_], writes=[pr])
            pi = aux(2)
            S.pe(lambda e: e.matmul(pi.ap[:, 0:Tt], wx_bf[l].ap[:, c, :], xb_.ap[:, 0:Tt], start=True, stop=True),
                 reads=[wx_bf[l], xb_], writes=[pi])
            r_, i_, a_, q_, h_ = rr[j], ii[j], aa[j], sq1[j], hh[j]
            S.act(lambda e: e.activation(out=r_.ap[:, 0:Tt], in_=pr.ap[:, 0:Tt], func=AF.Tanh, scale=0.5, bias=hb[l].ap[:, c:c + 1]),
                  reads=[pr, hb[l]], writes=[r_])
            S.act(lambda e: e.activation(out=i_.ap[:, 0:Tt], in_=pi.ap[:, 0:Tt], func=AF.Tanh, scale=0.5,
                                         bias=hb[l].ap[:, 8 + c:9 + c]), reads=[pi, hb[l]], writes=[i_])
            yield
            S.act(lambda e: e.activation(out=a_.ap[:, 0:Tt], in_=r_.ap[:, 0:Tt], func=AF.Exp, scale=nsp4[l].ap[:, c:c + 1],
                                         bias=nsp4[l].ap[:, c:c + 1]), reads=[r_, nsp4[l]], writes=[a_])
            S.act(lambda e: e.activation(out=q_.ap[:, 0:Tt], in_=r_.ap[:, 0:Tt], func=AF.Exp, scale=nsp8[l].ap[:, c:c + 1],
                                         bias=nsp8[l].ap[:, c:c + 1]), reads=[r_, nsp8[l]], writes=[q_])
            S.act(lambda e: e.activation(out=q_.ap[:, 0:Tt], in_=q_.ap[:, 0:Tt], func=AF.Ln, scale=-1.0, bias=1.0),
                  reads=[q_], writes=[q_])
            S.act(lambda e: e.activation(out=q_.ap[:, 0:Tt], in_=q_.ap[:, 0:Tt], func=AF.Exp, scale=0.5), reads=[q_], writes=[q_])
            S.dve(lambda e: e.scalar_tensor_tensor(out=i_.ap[:, 0:Tt], in0=i_.ap[:, 0:Tt], scalar=1.0, in1=x_.ap[:, 0:Tt],
                                                   op0=ALU.add, op1=ALU.mult), reads=[i_, x_], writes=[i_])
            S.dve(lambda e: e.scalar_tensor_tensor(out=i_.ap[:, 0:Tt], in0=i_.ap[:, 0:Tt], scalar=0.5, in1=q_.ap[:, 0:Tt],
                                                   op0=ALU.mult, op1=ALU.mult), reads=[i_, q_], writes=[i_])
            S.dve(lambda e: e.tensor_tensor_scan(out=h_.ap[:, 0:Tt], data0=a_.ap[:, 0:Tt], data1=i_.ap[:, 0:Tt],
                                                 initial=hst[l].ap[:, c:c + 1], op0=ALU.mult, op1=ALU.add),
                  reads=[a_, i_, hst[l]], writes=[h_])
            S.pool(lambda e: e.tensor_copy(out=hst[l].ap[:, c:c + 1], in_=h_.ap[:, Tt - 1:Tt]), reads=[h_], writes=[hst[l]])
            S.pool(lambda e: e.tensor_tensor(out=hgT.ap[:, c, 0:Tt], in0=h_.ap[:, 0:Tt], in1=g_.ap[:, 0:Tt], op=ALU.mult),
                   reads=[g_, h_], writes=[hgT[c]])
            yield

        gens = [body(c) for c in range(8)]
        filler = qk_proj_gen(l, Tt, FL[0])
        for rnd in range(8 + 2):
            sa = rnd if rnd < 8 else None
            sb1 = rnd - 1 if 0 <= rnd - 1 < 8 else None
            sb2 = rnd - 2 if 0 <= rnd - 2 < 8 else None
            order = [sa, sb1, sb2] if rnd % 2 == 0 else [sb2, sa, sb1]
            for gi in order:
                if gi is not None:
                    next(gens[gi])
            for _ in range(3):
                next(filler, None)
        release(sl_["a"][0])
        release(sl_["g"][0])
        for _ in filler:
            pass
        merge_branch(l, 0, hgT, Tt)

    def mlstm_phase(l, Tt, L):
        NCH = Tt // L
        S.dma("sp", lambda e: e.dma_start(out=mg_row1.ap, in_=W["m_norm_g"][l].partition_broadcast(64)), mg_row1,
              writes=[mg_row1])
        slg, vg = slab_k8(l, "w_in", O_MI, 8)
        pi = aux(0)
        pf = aux(1)
        for k in range(8):
            S.pe(lambda e, k=k: e.matmul(pi.ap[0:4, 0:Tt], vg[:, k, 0:4], uT.ap[:, k, 0:Tt], start=(k == 0), stop=(k == 7)),
                 reads=[slg, uT[k]], writes=[pi])
        for k in range(8):
            S.pe(lambda e, k=k: e.matmul(pf.ap[0:4, 0:Tt], vg[:, k, 4:8], uT.ap[:, k, 0:Tt], start=(k == 0), stop=(k == 7)),
                 reads=[slg, uT[k]], writes=[pf])
        G = {k: v.ap[:, 0:Tt] for k, v in grow.items()}
        Gs = {k: v.ap[:, 0:NCH] for k, v in gsm.items()}
        S.act(lambda e: e.activation(out=G["ig"], in_=pi.ap[0:4, 0:Tt], func=AF.Identity, bias=bi_col[l].ap),
              reads=[pi, bi_col[l]], writes=[grow["ig"]])
        S.act(lambda e: e.activation(out=G["sp"], in_=pf.ap[0:4, 0:Tt], func=AF.Exp, scale=-1.0, bias=nbf_col[l].ap),
              reads=[pf, nbf_col[l]], writes=[grow["sp"]])
        S.act(lambda e: e.activation(out=G["sp"], in_=G["sp"], func=AF.Ln, bias=1.0), reads=[grow["sp"]], writes=[grow["sp"]])
        S.dve(lambda e: e.tensor_tensor_scan(out=G["b"], data0=maskrow.ap[:, 0:Tt], data1=G["sp"], initial=0.0,
                                             op0=ALU.mult, op1=ALU.subtract), reads=[maskrow, grow["sp"]], writes=[grow["b"]])
        S.dve(lambda e: e.tensor_tensor(out=G["a"], in0=G["ig"], in1=G["b"], op=ALU.subtract),
              reads=[grow["ig"], grow["b"]], writes=[grow["a"]])
        a3 = G["a"].rearrange("p (c l) -> p c l", l=L)
        b3 = G["b"].rearrange("p (c l) -> p c l", l=L)
        S.dve(lambda e: e.tensor_reduce(out=Gs["amax"], in_=a3, axis=AX.X, op=ALU.max), reads=[grow["a"]], writes=[gsm["amax"]])
        S.dve(lambda e: e.memset(Gs["d0"][:, 0:1], 0.0), writes=[gsm["d0"]])
        if NCH > 1:
            S.dve(lambda e: e.tensor_copy(out=Gs["d0"][:, 1:NCH], in_=b3[:, 0:NCH - 1, L - 1]), reads=[grow["b"]],
                  writes=[gsm["d0"]])
        S.dve(lambda e: e.tensor_tensor_scan(out=Gs["M"], data0=Gs["d0"], data1=Gs["amax"], initial=mst[l].ap,
                                             op0=ALU.add, op1=ALU.max), reads=[gsm["d0"], gsm["amax"], mst[l]],
              writes=[gsm["M"]])
        S.dve(lambda e: e.tensor_tensor(out=Gs["mnew"], in0=b3[:, :, L - 1], in1=Gs["M"], op=ALU.add),
              reads=[grow["b"], gsm["M"]], writes=[gsm["mnew"]])
        S.dve(lambda e: e.tensor_copy(out=Gs["mprev"][:, 0:1], in_=mst[l].ap), reads=[mst[l]], writes=[gsm["mprev"]])
        if NCH > 1:
            S.dve(lambda e: e.tensor_copy(out=Gs["mprev"][:, 1:NCH], in_=Gs["mnew"][:, 0:NCH - 1]), reads=[gsm["mnew"]],
                  writes=[gsm["mprev"]])
        S.dve(lambda e: e.tensor_copy(out=mst[l].ap, in_=Gs["mnew"][:, NCH - 1:NCH]), reads=[gsm["mnew"], gsm["mprev"]],
              writes=[mst[l]])
        S.dve(lambda e: e.tensor_tensor(out=Gs["iw"], in0=Gs["mprev"], in1=Gs["M"], op=ALU.subtract),
              reads=[gsm["mprev"], gsm["M"]], writes=[gsm["iw"]])
        S.act(lambda e: e.activation(out=Gs["iw"], in_=Gs["iw"], func=AF.Exp), reads=[gsm["iw"]], writes=[gsm["iw"]])
        Mb = Gs["M"].unsqueeze(2).to_broadcast([4, NCH, L])
        ea3 = G["ea"].rearrange("p (c l) -> p c l", l=L)
        cl3 = G["cl"].rearrange("p (c l) -> p c l", l=L)
        iw3 = G["iwt"].rearrange("p (c l) -> p c l", l=L)
        S.dve(lambda e: e.tensor_tensor(out=ea3, in0=a3, in1=Mb, op=ALU.subtract), reads=[grow["a"], gsm["M"]],
              writes=[grow["ea"]])
        S.act(lambda e: e.activation(out=G["ea"], in_=G["ea"], func=AF.Exp), reads=[grow["ea"]], writes=[grow["ea"]])
        S.dve(lambda e: e.tensor_tensor(out=cl3, in0=b3, in1=Mb, op=ALU.add), reads=[grow["b"], gsm["M"]],
              writes=[grow["cl"]])
        S.act(lambda e: e.activation(out=G["cl"], in_=G["cl"], func=AF.Exp, scale=-1.0), reads=[grow["cl"]],
              writes=[grow["cl"]])
        S.dve(lambda e: e.tensor_copy(out=iw3, in_=Gs["iw"].unsqueeze(2).to_broadcast([4, NCH, L])), reads=[gsm["iw"]],
              writes=[grow["iwt"]])
        G2 = min(2, NCH)
        TG = G2 * L
        sgt2 = [qsq, dsum[1]]
        for half in range(2):
            sl, v = slab_k8(l, "w_in", O_MO + half * 512, 512)
            for gp in range(NCH // G2):
                pb = ps_mm()
                for k in range(8):
                    S.pe(lambda e, k=k, pb=pb, gp=gp, v=v: e.matmul(pb.ap[0:TG, :], uT.ap[:, k, gp * TG:(gp + 1) * TG], v[:, k, :],
                                                                    start=(k == 0), stop=(k == 7)),
                         reads=[sl, uT[k]], writes=[pb])
                for gi in range(G2):
                    ch = gp * G2 + gi
                    t_ = sgt2[gi]
                    S.act(lambda e, pb=pb, gi=gi, t_=t_: e.activation(out=t_.ap[0:L, :], in_=pb.ap[gi * L:(gi + 1) * L, :], func=AF.Exp,
                                                                      scale=-1.0), reads=[pb], writes=[t_])
                    S.act(lambda e, t_=t_: e.activation(out=t_.ap[0:L, :], in_=t_.ap[0:L, :], func=AF.Ln, bias=1.0), reads=[t_], writes=[t_])
                    S.act(lambda e, t_=t_: e.activation(out=t_.ap[0:L, :], in_=t_.ap[0:L, :], func=AF.Exp, scale=-1.0), reads=[t_],
                          writes=[t_])
                    S.pool(lambda e, ch=ch, half=half, t_=t_: e.tensor_tensor(out=sgm.ap[0:L, ch, half * 512:(half + 1) * 512],
                                                                              in0=t_.ap[0:L, :],
                                                                              in1=mg_row[l].ap[0:L, half * 512:(half + 1) * 512],
                                                                              op=ALU.mult),
                           reads=[t_, mg_row[l]], writes=[sgm[ch]])
        pc = aux(2)
        pcv = pc.ap[0:L, 0:NCH * 12].rearrange("p (c q) -> p c q", q=12)
        for ch in range(NCH):
            for qi, nm in enumerate(["ea", "cl", "iwt"]):
                S.pe(lambda e, ch=ch, qi=qi, nm=nm: e.transpose(pcv[:, ch, qi * 4:qi * 4 + 4], G[nm][:, ch * L:(ch + 1) * L],
                                                                 ident.ap[0:4, 0:4]),
                     reads=[grow[nm], ident], writes=[pc])
        S.dve(lambda e: e.tensor_copy(out=colq.ap[0:L, 0:NCH, :], in_=pcv), reads=[pc], writes=[colq])
        S.act(lambda e: e.activation(out=eab.ap[0:L, 0:NCH, :], in_=colq.ap[0:L, 0:NCH, 0:4], func=AF.Copy), reads=[colq],
              writes=[eab])
        S.dve(lambda e: e.tensor_tensor(out=rhsm.ap[:, :, 0:NCH], in0=Gs["iw"].unsqueeze(1).to_broadcast([4, 4, NCH]),
                                        in1=hmask.ap[:, :, 0:NCH], op=ALU.mult), reads=[gsm["iw"], hmask], writes=[rhsm])
        pw = aux(3)
        for h2 in range(4):
            S.pe(lambda e, h2=h2: e.matmul(pw.ap[:, h2 * NCH:(h2 + 1) * NCH], ones32.ap, rhsm.ap[:, h2, 0:NCH],
                                           start=True, stop=True), reads=[ones32, rhsm], writes=[pw])
        S.dve(lambda e: e.tensor_copy(out=iw_rep.ap[:, 0:4 * NCH], in_=pw.ap[:, 0:4 * NCH]), reads=[pw], writes=[iw_rep])

        for half in range(2):
            sl, v = slab_k8(l, "w_in", O_MV + half * 512, 512)
            for gp in range(NCH // G2):
                pb = ps_mm()
                for k in range(8):
                    S.pe(lambda e, k=k, pb=pb, gp=gp, v=v: e.matmul(pb.ap[0:TG, :], uT.ap[:, k, gp * TG:(gp + 1) * TG], v[:, k, :],
                                                                    start=(k == 0), stop=(k == 7)),
                         reads=[sl, uT[k]], writes=[pb])
                for gi in range(G2):
                    ch = gp * G2 + gi
                    copy_any(vw.ap[0:L, ch, half * 2:half * 2 + 2, :],
                             pb.ap[gi * L:(gi + 1) * L, :].rearrange("p (h e) -> p h e", h=2), [pb], [vw[ch]])
        for ch in range(NCH):
            S.dve(lambda e, ch=ch: e.tensor_tensor(out=vw.ap[0:L, ch], in0=vw.ap[0:L, ch],
                                                   in1=colq.ap[0:L, ch, 0:4].unsqueeze(2).to_broadcast([L, 4, 256]), op=ALU.mult),
                  reads=[vw[ch], colq], writes=[vw[ch]])
        def state_copies(h, chn):
            iwn = iw_rep.ap[:, h * NCH + chn:h * NCH + chn + 1]
            S.act(lambda e: e.activation(out=Cnb4[h].ap, in_=Cst[l].ap[:, h, :, :], func=AF.Copy, scale=iwn),
                  reads=[Cst[l][h], iw_rep], writes=[Cnb4[h]])
            S.act(lambda e: e.activation(out=nb4[h].ap, in_=nst[l].ap[:, h, :], func=AF.Copy, scale=iwn),
                  reads=[nst[l], iw_rep], writes=[nb4[h]])

        for h in range(4):
            state_copies(h, 0)

        def cbody(ch):
            cs = slice(ch * L, (ch + 1) * L)
            j = ch % 2
            ps_s = aux(0)
            sv = ps_s.ap[0:L, 0:4 * L].rearrange("p (h t) -> p h t", h=4)
            for h in range(4):
                for dc in range(2):
                    S.pe(lambda e, h=h, dc=dc, sv=sv, cs=cs: e.matmul(sv[:, h, :], kT.ap[:, 2 * h + dc, cs],
                                                                      qT.ap[:, 2 * h + dc, cs], start=(dc == 0), stop=(dc == 1)),
                         reads=[kT[2 * h + dc], qT[2 * h + dc]], writes=[ps_s])
            sm = smask[j]
            S.dve(lambda e, sm=sm, sv=sv: e.tensor_tensor(out=sm.ap[0:L, :, 0:L], in0=sv,
                                                          in1=cmask.ap[0:L, 0:L].unsqueeze(1).to_broadcast([L, 4, L]),
                                                          op=ALU.mult), reads=[ps_s, cmask], writes=[sm])
            yield
            po = [aux(1), aux(2)]
            pd = aux(3)
            for h in range(4):
                cb, nb_ = Cnb4[h], nb4[h]
                iwc = iw_rep.ap[:, h * NCH + ch:h * NCH + ch + 1]
                pov = po[h // 2].ap[0:L, (h % 2) * 256:(h % 2 + 1) * 256]
                S.pe(lambda e, pov=pov, sm=sm, h=h, ch=ch: e.matmul(pov, sm.ap[0:L, h, 0:L], vw.ap[0:L, ch, h, :],
                                                                    start=True, stop=False),
                     reads=[sm, vw[ch]], writes=[po[h // 2]])
                for dc in range(2):
                    S.pe(lambda e, pov=pov, h=h, dc=dc, cs=cs, cb=cb: e.matmul(pov, qT.ap[:, 2 * h + dc, cs], cb.ap[:, dc, :],
                                                                               start=False, stop=(dc == 1)),
                         reads=[qT[2 * h + dc], cb], writes=[po[h // 2]])
                pdv = pd.ap[0:L, h:h + 1]
                S.pe(lambda e, pdv=pdv, sm=sm, h=h, ch=ch: e.matmul(pdv, sm.ap[0:L, h, 0:L], eab.ap[0:L, ch, h:h + 1],
                                                                    start=True, stop=False),
                     reads=[sm, eab], writes=[pd[0]])
                for dc in range(2):
                    S.pe(lambda e, pdv=pdv, h=h, dc=dc, cs=cs, nb_=nb_: e.matmul(pdv, qT.ap[:, 2 * h + dc, cs],
                                                                                 nb_.ap[:, dc:dc + 1], start=False,
                                                                                 stop=(dc == 1)),
                         reads=[qT[2 * h + dc], nb_], writes=[pd[0]])
                pdl = aux(4)
                for dc in range(2):
                    S.pe(lambda e, pdl=pdl, h=h, dc=dc, ch=ch: e.matmul(pdl.ap[:, dc * 256:(dc + 1) * 256],
                                                                        ktok.ap[0:L, ch, h * 256 + dc * 128:h * 256 + dc * 128 + 128],
                                                                        vw.ap[0:L, ch, h, :], start=True, stop=True),
                         reads=[ktok[ch], vw[ch]], writes=[pdl])
                    S.pe(lambda e, pd=pd, h=h, dc=dc, ch=ch: e.matmul(pd.ap[:, 8 + h * 2 + dc:8 + h * 2 + dc + 1],
                                                                      ktok.ap[0:L, ch, h * 256 + dc * 128:h * 256 + dc * 128 + 128],
                                                                      eab.ap[0:L, ch, h:h + 1], start=True, stop=True),
                         reads=[ktok[ch], eab], writes=[pd[1 + h]])
                S.dve(lambda e, pdl=pdl, h=h, iwc=iwc: e.scalar_tensor_tensor(
                    out=Cst[l].ap[:, h, :, :].rearrange("p a b -> p (a b)"), in0=Cst[l].ap[:, h, :, :].rearrange("p a b -> p (a b)"),
                    scalar=iwc, in1=pdl.ap[:, 0:512], op0=ALU.mult, op1=ALU.add),
                      reads=[Cst[l][h], pdl, iw_rep], writes=[Cst[l][h]])
                S.dve(lambda e, pd=pd, h=h, iwc=iwc: e.scalar_tensor_tensor(
                    out=nst[l].ap[:, h, :], in0=nst[l].ap[:, h, :], scalar=iwc, in1=pd.ap[:, 8 + h * 2:8 + h * 2 + 2],
                    op0=ALU.mult, op1=ALU.add), reads=[nst[l], pd[1 + h], iw_rep], writes=[nst[l]])
                if ch + 1 < NCH:
                    state_copies(h, ch + 1)
            yield
            ds_, dt_, hn_, ss_ = den_s[j], den_t[j], hn[j], ssm[j]
            S.act(lambda e, ds_=ds_, pd=pd: e.activation(out=ds_.ap[0:L, :], in_=pd.ap[0:L, 0:4], func=AF.Copy), reads=[pd[0]],
                  writes=[ds_])
            S.dve(lambda e, ds_=ds_, dt_=dt_: e.scalar_tensor_tensor(out=dt_.ap[0:L, :], in0=ds_.ap[0:L, :], scalar=-1.0,
                                                                     in1=ds_.ap[0:L, :], op0=ALU.mult, op1=ALU.max),
                  reads=[ds_], writes=[dt_])
            S.dve(lambda e, dt_=dt_, ch=ch: e.tensor_tensor(out=dt_.ap[0:L, :], in0=dt_.ap[0:L, :], in1=colq.ap[0:L, ch, 4:8],
                                                            op=ALU.max), reads=[dt_, colq], writes=[dt_])
            S.dve(lambda e, dt_=dt_: e.reciprocal(out=dt_.ap[0:L, :], in_=dt_.ap[0:L, :]), reads=[dt_], writes=[dt_])
            for hp in range(2):
                S.dve(lambda e, hp=hp, hn_=hn_, dt_=dt_: e.tensor_tensor(
                    out=hn_.ap[0:L, 2 * hp:2 * hp + 2, :], in0=po[hp].ap[0:L, :].rearrange("p (h d) -> p h d", h=2),
                    in1=dt_.ap[0:L, 2 * hp:2 * hp + 2].unsqueeze(2).to_broadcast([L, 2, 256]), op=ALU.mult),
                      reads=[po[hp], dt_], writes=[hn_])
            S.act(lambda e, hn_=hn_: e.activation(out=hsq.ap[0:L], in_=hn_.ap[0:L], func=AF.Square), reads=[hn_], writes=[hsq])
            S.dve(lambda e, ss_=ss_: e.tensor_reduce(out=ss_.ap[0:L, :], in_=hsq.ap[0:L], axis=AX.X, op=ALU.add), reads=[hsq],
                  writes=[ss_])
            S.act(lambda e, ss_=ss_: e.activation(out=ss_.ap[0:L, :], in_=ss_.ap[0:L, :], func=AF.Sqrt, scale=1.0 / 256, bias=EPS),
                  reads=[ss_], writes=[ss_])
            S.dve(lambda e, ss_=ss_: e.reciprocal(out=ss_.ap[0:L, :], in_=ss_.ap[0:L, :]), reads=[ss_], writes=[ss_])
            S.pool(lambda e, hn_=hn_, ss_=ss_: e.tensor_tensor(out=hn_.ap[0:L], in0=hn_.ap[0:L],
                                                               in1=ss_.ap[0:L, :].unsqueeze(2).to_broadcast([L, 4, 256]),
                                                               op=ALU.mult), reads=[hn_, ss_], writes=[hn_])
            S.pool(lambda e, hn_=hn_, ch=ch: e.tensor_tensor(out=hmtok.ap[0:L, ch, :], in0=hn_.ap[0:L].rearrange("p h d -> p (h d)"),
                                                             in1=sgm.ap[0:L, ch, :], op=ALU.mult),
                   reads=[hn_, sgm[ch]], writes=[hmtok[ch]])
            yield
            pt = aux(5)
            ptv = pt.ap.bitcast(BF16)[:, 0:8 * L].rearrange("p (c t) -> p c t", c=8)
            for c in range(8):
                S.pe(lambda e, c=c, ptv=ptv, ch=ch: e.transpose(ptv[:, c, :], hmtok.ap[0:L, ch, c * 128:(c + 1) * 128],
                                                                 identb.ap[0:L, 0:L]),
                     reads=[hmtok[ch], identb], writes=[pt])
            copy_any(hmT.ap[:, :, cs], ptv, [pt], [hmT])

        pipeline([cbody(ch) for ch in range(NCH)], newest_first=False)
        merge_branch(l, 1, hmT, Tt)

    def attn_phase(l, Tt, L, first_tile, is_sample, last_tile, grp):
        NCH = Tt // L
        Tb = min(Tt, 128)
        NB = Tt // Tb
        slkv, vkv = slab_k8(l, "w_in", O_AK, 256)

        def kbody(ch):
            j = ch % 2
            slot = 2 + ch
            pb = ps_mm()
            for k in range(8):
                S.pe(lambda e, k=k, pb=pb, ch=ch: e.matmul(pb.ap[0:L, 0:256], uT.ap[:, k, ch * L:(ch + 1) * L], vkv[:, k, :],
                                                           start=(k == 0), stop=(k == 7)), reads=[slkv, uT[k]], writes=[pb])
            vf_ = vf[j]
            S.act(lambda e, pb=pb, vf_=vf_: e.activation(out=vf_.ap[0:L, :], in_=pb.ap[0:L, 128:256], func=AF.Copy), reads=[pb],
                  writes=[vf_])
            S.dve(lambda e, vf_=vf_, slot=slot: e.tensor_copy(out=vaug[l].ap[0:L, slot, :, 0:64],
                                                              in_=vf_.ap[0:L, :].rearrange("p (k d) -> p k d", k=2)),
                  reads=[vf_], writes=[vaug[l]])
            S.dve(lambda e, vf_=vf_, slot=slot: e.tensor_copy(out=vaug[l].ap[0:L, slot, :, 128:192],
                                                              in_=vf_.ap[0:L, :].rearrange("p (k d) -> p k d", k=2)),
                  reads=[vf_], writes=[vaug[l]])
            ks_, kn_, kr_, krb_ = kss[j], kn[j], kr[j], krb[j]
            S.act(lambda e, pb=pb: e.activation(out=ksq.ap[0:L, :], in_=pb.ap[0:L, 0:128], func=AF.Square), reads=[pb], writes=[ksq])
            S.dve(lambda e, ks_=ks_: e.tensor_reduce(out=ks_.ap[0:L, :], in_=ksq.ap[0:L, :].rearrange("p (k d) -> p k d", k=2),
                                                     axis=AX.X, op=ALU.add), reads=[ksq], writes=[ks_])
            S.act(lambda e, ks_=ks_: e.activation(out=ks_.ap[0:L, :], in_=ks_.ap[0:L, :], func=AF.Sqrt, scale=1.0 / 64, bias=EPS),
                  reads=[ks_], writes=[ks_])
            S.dve(lambda e, ks_=ks_: e.reciprocal(out=ks_.ap[0:L, :], in_=ks_.ap[0:L, :]), reads=[ks_], writes=[ks_])
            for kv in range(2):
                S.dve(lambda e, kv=kv, pb=pb, ks_=ks_, kn_=kn_: e.scalar_tensor_tensor(
                    out=kn_.ap[0:L, kv, :], in0=pb.ap[0:L, kv * 64:(kv + 1) * 64], scalar=ks_.ap[0:L, kv:kv + 1],
                    in1=kg_row[l].ap[0:L, :], op0=ALU.mult, op1=ALU.mult), reads=[pb, ks_, kg_row[l]], writes=[kn_])
            cosb = ctab_k.ap[0:L, ch, :].unsqueeze(1).to_broadcast([L, 2, 32])
            sinb = stab_k.ap[0:L, ch, :].unsqueeze(1).to_broadcast([L, 2, 32])
            k1 = kn_.ap[0:L, :, 0:32]
            k2 = kn_.ap[0:L, :, 32:64]
            krv = kr_.ap[0:L, :].rearrange("p (k d) -> p k d", k=2)
            S.dve(lambda e, k1=k1, cosb=cosb: e.tensor_tensor(out=kt1.ap[0:L], in0=k1, in1=cosb, op=ALU.mult),
                  reads=[kn_, ctab_k], writes=[kt1])
            S.dve(lambda e, k2=k2, sinb=sinb: e.tensor_tensor(out=kt2.ap[0:L], in0=k2, in1=sinb, op=ALU.mult),
                  reads=[kn_, stab_k], writes=[kt2])
            S.dve(lambda e, krv=krv: e.tensor_tensor(out=krv[:, :, 0:32], in0=kt1.ap[0:L], in1=kt2.ap[0:L], op=ALU.subtract),
                  reads=[kt1, kt2], writes=[kr_])
            S.dve(lambda e, k2=k2, cosb=cosb: e.tensor_tensor(out=kt1.ap[0:L], in0=k2, in1=cosb, op=ALU.mult),
                  reads=[kn_, ctab_k], writes=[kt1])
            S.dve(lambda e, k1=k1, sinb=sinb: e.tensor_tensor(out=kt2.ap[0:L], in0=k1, in1=sinb, op=ALU.mult),
                  reads=[kn_, stab_k], writes=[kt2])
            S.dve(lambda e, krv=krv: e.tensor_tensor(out=krv[:, :, 32:64], in0=kt1.ap[0:L], in1=kt2.ap[0:L], op=ALU.add),
                  reads=[kt1, kt2], writes=[kr_])
            S.act(lambda e, kr_=kr_, krb_=krb_: e.activation(out=krb_.ap[0:L, :], in_=kr_.ap[0:L, :], func=AF.Copy),
                  reads=[kr_], writes=[krb_])
            yield
            pk = aux(0)
            pkv = pk.ap.bitcast(BF16)[0:64, 0:2 * L].rearrange("p (k t) -> p k t", k=2)
            for kv in range(2):
                S.pe(lambda e, kv=kv, pkv=pkv, krb_=krb_: e.transpose(pkv[:, kv, :], krb_.ap[0:L, kv * 64:(kv + 1) * 64],
                                                                      identb.ap[0:L, 0:L]), reads=[krb_, identb], writes=[pk])
            S.dve(lambda e, pkv=pkv, slot=slot: e.tensor_copy(out=KTwin[l].ap[:, :, slot * 64:slot * 64 + L], in_=pkv),
                  reads=[pk], writes=[KTwin[l]])
            if is_sample:
                S.dma("sp", lambda e, kr_=kr_: e.dma_start(out=O["k_s"][l, 128 - L:128].rearrange("r k d -> r (k d)"),
                                                           in_=kr_.ap[0:L, :]), kr_, reads=[kr_])
                S.dma("sp", lambda e, vf_=vf_: e.dma_start(out=O["v_s"][l, 128 - L:128].rearrange("r k d -> r (k d)"),
                                                           in_=vf_.ap[0:L, :]), vf_, reads=[vf_])
            elif last_tile and ch >= NCH - 2:
                r0 = (ch - (NCH - 2)) * 64
                S.dma("sp", lambda e, kr_=kr_, r0=r0: e.dma_start(out=O["k_p"][l, r0:r0 + 64].rearrange("r k d -> r (k d)"),
                                                                  in_=kr_.ap[0:L, :]), kr_, reads=[kr_])
                S.dma("sp", lambda e, vf_=vf_, r0=r0: e.dma_start(out=O["v_p"][l, r0:r0 + 64].rearrange("r k d -> r (k d)"),
                                                                  in_=vf_.ap[0:L, :]), vf_, reads=[vf_])

        pipeline([kbody(ch) for ch in range(NCH)], newest_first=True)
        qsl = {}

        def qbody(half, b):
            if True:
                if b == 0:
                    qsl["q"] = slab_k8(l, "w_in", O_AQ + half * 512, 512)
                slq, vq = qsl["q"]
                j = (half * NB + b) % 2
                pb = ps_mm()
                for k in range(8):
                    S.pe(lambda e, k=k, pb=pb, b=b, vq=vq: e.matmul(pb.ap[0:Tb, :], uT.ap[:, k, b * Tb:(b + 1) * Tb], vq[:, k, :],
                                                                    start=(k == 0), stop=(k == 7)), reads=[slq, uT[k]], writes=[pb])
                qs_, qn_, qr_ = qss[j], qn[j], qr[j]
                S.act(lambda e, pb=pb: e.activation(out=qsq.ap[0:Tb, :], in_=pb.ap[0:Tb, :], func=AF.Square), reads=[pb], writes=[qsq])
                S.dve(lambda e, qs_=qs_: e.tensor_reduce(out=qs_.ap[0:Tb, :], in_=qsq.ap[0:Tb, :].rearrange("p (h d) -> p h d", h=8),
                                                         axis=AX.X, op=ALU.add), reads=[qsq], writes=[qs_])
                S.act(lambda e, qs_=qs_: e.activation(out=qs_.ap[0:Tb, :], in_=qs_.ap[0:Tb, :], func=AF.Sqrt, scale=1.0 / 64, bias=EPS),
                      reads=[qs_], writes=[qs_])
                S.dve(lambda e, qs_=qs_: e.reciprocal(out=qs_.ap[0:Tb, :], in_=qs_.ap[0:Tb, :]), reads=[qs_], writes=[qs_])
                S.dve(lambda e, pb=pb, qs_=qs_, qn_=qn_: e.tensor_tensor(
                    out=qn_.ap[0:Tb], in0=pb.ap[0:Tb, :].rearrange("p (h d) -> p h d", h=8),
                    in1=qs_.ap[0:Tb, :].unsqueeze(2).to_broadcast([Tb, 8, 64]), op=ALU.mult), reads=[pb, qs_], writes=[qn_])
                S.dve(lambda e, qn_=qn_: e.tensor_tensor(out=qn_.ap[0:Tb], in0=qn_.ap[0:Tb],
                                                         in1=qg_row[l].ap[0:Tb, :].unsqueeze(1).to_broadcast([Tb, 8, 64]),
                                                         op=ALU.mult), reads=[qn_, qg_row[l]], writes=[qn_])
                cosb = ctab_q.ap[0:Tb, b, :].unsqueeze(1).to_broadcast([Tb, 8, 32])
                sinb = stab_q.ap[0:Tb, b, :].unsqueeze(1).to_broadcast([Tb, 8, 32])
                q1 = qn_.ap[0:Tb, :, 0:32]
                q2 = qn_.ap[0:Tb, :, 32:64]
                S.dve(lambda e, q1=q1, cosb=cosb: e.tensor_tensor(out=qt1.ap[0:Tb], in0=q1, in1=cosb, op=ALU.mult),
                      reads=[qn_, ctab_q], writes=[qt1])
                S.dve(lambda e, q2=q2, sinb=sinb: e.tensor_tensor(out=qt2.ap[0:Tb], in0=q2, in1=sinb, op=ALU.mult),
                      reads=[qn_, stab_q], writes=[qt2])
                S.dve(lambda e, qr_=qr_: e.tensor_tensor(out=qr_.ap[0:Tb, :, 0:32], in0=qt1.ap[0:Tb], in1=qt2.ap[0:Tb],
                                                         op=ALU.subtract), reads=[qt1, qt2], writes=[qr_])
                S.dve(lambda e, q2=q2, cosb=cosb: e.tensor_tensor(out=qt1.ap[0:Tb], in0=q2, in1=cosb, op=ALU.mult),
                      reads=[qn_, ctab_q], writes=[qt1])
                S.dve(lambda e, q1=q1, sinb=sinb: e.tensor_tensor(out=qt2.ap[0:Tb], in0=q1, in1=sinb, op=ALU.mult),
                      reads=[qn_, stab_q], writes=[qt2])
                S.dve(lambda e, qr_=qr_: e.tensor_tensor(out=qr_.ap[0:Tb, :, 32:64], in0=qt1.ap[0:Tb], in1=qt2.ap[0:Tb],
                                                         op=ALU.add), reads=[qt1, qt2], writes=[qr_])
                yield
                pq = aux(1)
                pqv = pq.ap.bitcast(BF16)[0:64, 0:8 * Tb].rearrange("p (h t) -> p h t", h=8)
                for h in range(8):
                    S.pe(lambda e, h=h, pqv=pqv, qr_=qr_: e.transpose(pqv[:, h, :], qr_.ap[0:Tb, h, :], identb.ap[0:Tb, 0:Tb]),
                         reads=[qr_, identb], writes=[pq])
                copy_any(QT_all.ap[:, half * 8:(half + 1) * 8, b * Tb:(b + 1) * Tb], pqv, [pq],
                         [QT_all[half * 8 + h] for h in range(8)])

        pipeline([qbody(half, b) for half in range(2) for b in range(NB)], newest_first=True)

        def abody(ch, kv):
            slots = [(ch, 64), (ch + 1, 64), (ch + 2, L)]
            if first_tile and not is_sample:
                slots = [(s, n) for (s, n) in slots if s >= 2]
            qs = slice(ch * L, (ch + 1) * L)
            if True:
                ppv = aux(kv)
                pts = []
                for si_, (s, nk) in enumerate(slots):
                    pss = aux(2 + si_)
                    S.pe(lambda e, pss=pss, s=s, nk=nk, kv=kv, qs=qs: e.matmul(
                        pss.ap[0:nk, 0:8 * L].rearrange("p (g t) -> p g t", g=8), KTwin[l].ap[:, kv, s * 64:s * 64 + nk],
                        QT_all.ap[:, kv * 8:(kv + 1) * 8, qs], start=True, stop=True),
                         reads=[KTwin[l]] + [QT_all[kv * 8 + g] for g in range(8)], writes=[pss])
                    p_ = pT[pT_i[0] % 6]
                    pT_i[0] += 1
                    S.act(lambda e, p_=p_, pss=pss, nk=nk: e.activation(out=p_.ap[0:nk, 0:8 * L], in_=pss.ap[0:nk, 0:8 * L],
                                                                        func=AF.Exp, scale=0.125), reads=[pss], writes=[p_])
                    pts.append((p_, s, nk))
                yield
                H4 = 4 * L
                for par in range(2):
                    for i, (p_, s, nk) in enumerate(pts):
                        pv4 = p_.ap[0:nk, 0:8 * L].rearrange("p (g two t) -> p g two t", two=2, t=L)
                        S.pe(lambda e, pv4=pv4, s=s, nk=nk, i=i, par=par: e.matmul(
                            ppv.ap[:, par * H4:(par + 1) * H4].rearrange("p (g t) -> p g t", g=4),
                            vaug[l].ap[0:nk, s, kv, par * 64:par * 64 + 128], pv4[:, :, par, :],
                            start=(i == 0), stop=(i == len(pts) - 1)), reads=[vaug[l], p_], writes=[ppv])
                ds_ = dsum[kv]
                es4 = esink[l].ap[:, kv * 8:(kv + 1) * 8].rearrange("p (g two) -> p g two", two=2)
                S.dve(lambda e: e.tensor_tensor(out=ds_.ap[64:128, 0:H4].rearrange("p (g t) -> p g t", g=4),
                                                in0=ppv.ap[64:128, 0:H4].rearrange("p (g t) -> p g t", g=4),
                                                in1=es4[64:128, :, 0].unsqueeze(2).to_broadcast([64, 4, L]), op=ALU.add),
                      reads=[ppv, esink[l]], writes=[ds_])
                S.act(lambda e: e.activation(out=ds_.ap[0:64, 0:H4], in_=ds_.ap[64:128, 0:H4], func=AF.Ln), reads=[ds_], writes=[ds_])
                S.act(lambda e: e.activation(out=ds_.ap[0:64, 0:H4], in_=ds_.ap[0:64, 0:H4], func=AF.Exp, scale=-1.0), reads=[ds_],
                      writes=[ds_])
                S.dve(lambda e: e.tensor_tensor(out=OT2.ap[0:64, kv * 4:(kv + 1) * 4, qs],
                                                in0=ppv.ap[0:64, 0:H4].rearrange("p (g t) -> p g t", g=4),
                                                in1=ds_.ap[0:64, 0:H4].rearrange("p (g t) -> p g t", g=4), op=ALU.mult),
                      reads=[ppv, ds_], writes=[OT2[kv * 4 + g] for g in range(4)])
                S.dve(lambda e: e.tensor_tensor(out=ds_.ap[0:64, H4:2 * H4].rearrange("p (g t) -> p g t", g=4),
                                                in0=ppv.ap[0:64, H4:2 * H4].rearrange("p (g t) -> p g t", g=4),
                                                in1=es4[0:64, :, 1].unsqueeze(2).to_broadcast([64, 4, L]), op=ALU.add),
                      reads=[ppv, esink[l]], writes=[ds_])
                S.act(lambda e: e.activation(out=ds_.ap[64:128, H4:2 * H4], in_=ds_.ap[0:64, H4:2 * H4], func=AF.Ln), reads=[ds_],
                      writes=[ds_])
                S.act(lambda e: e.activation(out=ds_.ap[64:128, H4:2 * H4], in_=ds_.ap[64:128, H4:2 * H4], func=AF.Exp, scale=-1.0),
                      reads=[ds_], writes=[ds_])
                S.dve(lambda e: e.tensor_tensor(out=OT2.ap[64:128, kv * 4:(kv + 1) * 4, qs],
                                                in0=ppv.ap[64:128, H4:2 * H4].rearrange("p (g t) -> p g t", g=4),
                                                in1=ds_.ap[64:128, H4:2 * H4].rearrange("p (g t) -> p g t", g=4), op=ALU.mult),
                      reads=[ppv, ds_], writes=[OT2[kv * 4 + g] for g in range(4)])

        pipeline([abody(ch, kv) for ch in range(NCH) for kv in range(2)], newest_first=True)
        if not is_sample and not last_tile:
            for i in range(2):
                S.dve(lambda e, i=i: e.tensor_copy(out=KTwin[l].ap[:, :, i * 64:(i + 1) * 64],
                                                   in_=KTwin[l].ap[:, :, (NCH + i) * 64:(NCH + i + 1) * 64]),
                      reads=[KTwin[l]], writes=[KTwin[l]])
                S.pool(lambda e, i=i: e.tensor_copy(out=vaug[l].ap[:, i, :, 0:64], in_=vaug[l].ap[:, NCH + i, :, 0:64]),
                       reads=[vaug[l]], writes=[vaug[l]])
                S.pool(lambda e, i=i: e.tensor_copy(out=vaug[l].ap[:, i, :, 128:192], in_=vaug[l].ap[:, NCH + i, :, 128:192]),
                       reads=[vaug[l]], writes=[vaug[l]])
        merge_branch(l, 2, OT2, Tt)

    def out_and_mlp(l, Tt):
        for c in range(8):
            S.act(lambda e, c=c: e.activation(out=uT.ap[:, c, 0:Tt], in_=mix.ap[:, c, 0:Tt], func=AF.Copy), reads=[mix[c]],
                  writes=[uT[c]])
        for half in range(2):
            sl, v = slab_k8(l, "w_out", half * 512, 512)
            for c4 in range(4):
                c = half * 4 + c4
                pb = ps_mm()
                fm_proj(pb, sl, v, c4 * 128, uT, Tt)
                S.dve(lambda e, pb=pb, c=c: e.tensor_tensor(out=xT.ap[:, c, 0:Tt], in0=pb.ap[:, 0:Tt], in1=xT.ap[:, c, 0:Tt],
                                                            op=ALU.add), reads=[pb, xT[c]], writes=[xT[c]])
        norm_to_u(l, "norm2_g", Tt)
        for s8 in range(8):
            sl, v = slab_k8(l, "w_up", s8 * 512, 512)
            for c4 in range(4):
                hc = s8 * 4 + c4
                pb = ps_mm()
                fm_proj(pb, sl, v, c4 * 128, uT, Tt)
                r_ = rl[hc % 2]
                S.act(lambda e, pb=pb, r_=r_: e.activation(out=r_.ap[:, 0:Tt], in_=pb.ap[:, 0:Tt], func=AF.Relu), reads=[pb],
                      writes=[r_])
                hp_ = hid_parts[hc // 8]
                S.dve(lambda e, r_=r_, hc=hc, hp_=hp_: e.tensor_tensor(out=hp_.ap[:, hc % 8, 0:Tt], in0=r_.ap[:, 0:Tt],
                                                                       in1=r_.ap[:, 0:Tt], op=ALU.mult),
                      reads=[r_], writes=[hp_[hc % 8]])
        for c in range(8):
            src = SCR["w_down"][l][:, c * 128:(c + 1) * 128].rearrange("(kc p) n -> p kc n", p=128)
            sl, v = load_slab(src, lambda a: a.rearrange("p (k n) -> p k n", k=32), SCRB[("w_down", l)])
            pb = ps_mm()
            for k in range(32):
                hp_ = hid_parts[k // 8]
                S.pe(lambda e, k=k, pb=pb, v=v, hp_=hp_: e.matmul(pb.ap[:, 0:Tt], v[:, k, :], hp_.ap[:, k % 8, 0:Tt],
                                                                  start=(k == 0), stop=(k == 31)),
                     reads=[sl, hp_[k % 8]], writes=[pb])
            S.dve(lambda e, pb=pb, c=c: e.tensor_tensor(out=xT.ap[:, c, 0:Tt], in0=pb.ap[:, 0:Tt], in1=xT.ap[:, c, 0:Tt],
                                                        op=ALU.add), reads=[pb, xT[c]], writes=[xT[c]])

    def load_x(src, Tt):
        Tb = min(Tt, 128)
        for b in range(Tt // Tb):
            xi = xin[b % 2]
            S.dma("sp", lambda e, xi=xi, b=b: e.dma_start(out=xi.ap[0:Tb, :], in_=src[b * Tb:(b + 1) * Tb, :]), xi, writes=[xi])
            for g4 in range(2):
                pb = aux(g4)
                for c4 in range(4):
                    c = g4 * 4 + c4
                    S.pe(lambda e, pb=pb, c=c, c4=c4, xi=xi: e.transpose(pb.ap[:, c4 * Tb:(c4 + 1) * Tb],
                                                                         xi.ap[0:Tb, c * 128:(c + 1) * 128], ident.ap[0:Tb, 0:Tb]),
                         reads=[xi, ident], writes=[pb])
                copy_any(xT.ap[:, g4 * 4:(g4 + 1) * 4, b * Tb:(b + 1) * Tb],
                         pb.ap[:, 0:4 * Tb].rearrange("p (c t) -> p c t", c=4), [pb], [xT[g4 * 4 + i] for i in range(4)])

    def store_y(dst, Tt):
        Tb = min(Tt, 128)
        for b in range(Tt // Tb):
            yo = yout[b % 2]
            for g4 in range(2):
                pb = aux(g4)
                for c4 in range(4):
                    c = g4 * 4 + c4
                    S.pe(lambda e, pb=pb, c=c, c4=c4, b=b: e.transpose(pb.ap[0:Tb, c4 * 128:(c4 + 1) * 128],
                                                                       xT.ap[:, c, b * Tb:(b + 1) * Tb], ident.ap),
                         reads=[xT[c], ident], writes=[pb])
                copy_any(yo.ap[0:Tb, g4 * 512:(g4 + 1) * 512], pb.ap[0:Tb, :], [pb], [yo])
            S.dma("sp", lambda e, yo=yo, b=b: e.dma_start(out=dst[b * Tb:(b + 1) * Tb, :], in_=yo.ap[0:Tb, :]), yo, reads=[yo])

    def load_rope(pos0, Tt, L):
        NCH = Tt // L
        Tb = min(Tt, 128)
        NB = Tt // Tb
        for (tab, src) in [(ctab_k, rope_c), (stab_k, rope_s)]:
            S.dma("sp", lambda e, tab=tab, src=src: e.dma_start(
                out=tab.ap[0:L, 0:NCH, :], in_=src[pos0:pos0 + Tt, :].rearrange("(c l) f -> l c f", l=L)), tab, writes=[tab])
        for (tab, src) in [(ctab_q, rope_c), (stab_q, rope_s)]:
            S.dma("sp", lambda e, tab=tab, src=src: e.dma_start(
                out=tab.ap[0:Tb, 0:NB, :], in_=src[pos0:pos0 + Tt, :].rearrange("(c l) f -> l c f", l=Tb)), tab, writes=[tab])

    def cols_to_rows_store(src_ap, n, dsts, rd):
        pb = aux(5)
        S.pe(lambda e: e.transpose(pb.ap[0:n, 0:128], src_ap, ident.ap), reads=rd + [ident], writes=[pb])
        S.dve(lambda e: e.tensor_copy(out=vstage.ap[0:n, :], in_=pb.ap[0:n, 0:128]), reads=[pb], writes=[vstage])
        for (r0, r1, d) in dsts:
            S.dma("sp", lambda e, r0=r0, r1=r1, d=d: e.dma_start(out=d, in_=vstage.ap[r0:r1, :]), vstage, reads=[vstage])

    def rows_load_to_cols(srcs, n, dst_ap, wr):
        for (r0, r1, s_) in srcs:
            S.dma("sp", lambda e, r0=r0, r1=r1, s_=s_: e.dma_start(out=vstage.ap[r0:r1, :], in_=s_), vstage, writes=[vstage])
        pb = aux(5)
        S.pe(lambda e: e.transpose(pb.ap[:, 0:n], vstage.ap[0:n, :], ident.ap[0:n, 0:n]), reads=[vstage, ident], writes=[pb])
        S.dve(lambda e: e.tensor_copy(out=dst_ap, in_=pb.ap[:, 0:n]), reads=[pb], writes=wr)

    def init_states_zero(l):
        S.dve(lambda e: e.memset(hist[l].ap, 0.0), writes=[hist[l]])
        S.dve(lambda e: e.memset(hst[l].ap, 0.0), writes=[hst[l]])
        S.pool(lambda e: e.memset(Cst[l].ap, 0.0), writes=[Cst[l]])
        S.dve(lambda e: e.memset(nst[l].ap, 0.0), writes=[nst[l]])
        S.dve(lambda e: e.memset(mst[l].ap, 0.0), writes=[mst[l]])
        S.pool(lambda e: e.memset(vaug[l].ap, 1.0), writes=[vaug[l]])
        S.pool(lambda e: e.memset(KTwin[l].ap, 0.0), writes=[KTwin[l]])

    def init_states_sample(l):
        for (r0, r1, s_) in [(0, 24, st_conv[l].rearrange("j (c p) -> (j c) p", p=128)),
                             (24, 32, st_lru[l].rearrange("(c p) -> c p", p=128)),
                             (32, 40, st_n[l].rearrange("h (c p) -> (h c) p", p=128))]:
            S.dma("sp", lambda e, r0=r0, r1=r1, s_=s_: e.dma_start(out=vstage.ap[r0:r1, :], in_=s_), vstage, writes=[vstage])
        pb = aux(5)
        S.pe(lambda e: e.transpose(pb.ap[:, 0:40], vstage.ap[0:40, :], ident.ap[0:40, 0:40]), reads=[vstage, ident], writes=[pb])
        S.dve(lambda e: e.tensor_copy(out=hist[l].ap.rearrange("p j c -> p (j c)"), in_=pb.ap[:, 0:24]), reads=[pb], writes=[hist[l]])
        S.dve(lambda e: e.tensor_copy(out=hst[l].ap, in_=pb.ap[:, 24:32]), reads=[pb], writes=[hst[l]])
        S.dve(lambda e: e.tensor_copy(out=nst[l].ap.rearrange("p h c -> p (h c)"), in_=pb.ap[:, 32:40]), reads=[pb], writes=[nst[l]])
        S.dma("sp", lambda e: e.dma_start(out=Cst[l].ap, in_=st_C[l].rearrange("h (c p) e -> p h c e", p=128)), Cst[l],
              writes=[Cst[l]])
        S.dma("sp", lambda e: e.dma_start(out=mst[l].ap, in_=st_m[l].rearrange("(h o) -> h o", o=1)), mst[l], writes=[mst[l]])
        S.pool(lambda e: e.memset(vaug[l].ap, 1.0), writes=[vaug[l]])
        for s2 in range(2):
            S.dma("pool", lambda e, s2=s2: e.dma_start(out=vaug[l].ap[:, s2, :, 0:64], in_=c_v[l, s2 * 64:(s2 + 1) * 64]),
                  vaug[l], writes=[vaug[l]])
        S.dve(lambda e: e.tensor_copy(out=vaug[l].ap[:, 0:2, :, 128:192], in_=vaug[l].ap[:, 0:2, :, 0:64]), reads=[vaug[l]],
              writes=[vaug[l]])
        S.dma("sp", lambda e: e.dma_start(out=kcs.ap, in_=c_k[l].rearrange("(s r) k d -> r s (k d)", s=2)), kcs, writes=[kcs])
        S.dve(lambda e: e.tensor_copy(out=kcb.ap, in_=kcs.ap), reads=[kcs], writes=[kcb])
        pk = aux(4)
        pkv = pk.ap.bitcast(BF16)[0:64, 0:256].rearrange("p (k t) -> p k t", k=2)
        for s in range(2):
            for kv in range(2):
                S.pe(lambda e, s=s, kv=kv: e.transpose(pkv[:, kv, s * 64:(s + 1) * 64], kcb.ap[:, s, kv * 64:(kv + 1) * 64],
                                                       identb.ap[0:64, 0:64]), reads=[kcb, identb], writes=[pk])
        S.dve(lambda e: e.tensor_copy(out=KTwin[l].ap[:, :, 0:128], in_=pkv), reads=[pk], writes=[KTwin[l]])
        for (o_, c_) in [(O["k_s"], c_k), (O["v_s"], c_v)]:
            S.dma("sp", lambda e, o_=o_, c_=c_: e.dma_start(out=tailk.ap[0:64, :], in_=c_[l, TS:TS + 64].rearrange("r k d -> r (k d)")),
                  tailk, writes=[tailk])
            S.dma("sp", lambda e, o_=o_: e.dma_start(out=o_[l, 0:64].rearrange("r k d -> r (k d)"), in_=tailk.ap[0:64, :]),
                  tailk, reads=[tailk])
            n2 = 128 - TS - 64
            S.dma("sp", lambda e, o_=o_, c_=c_: e.dma_start(out=tailk.ap[0:n2, :],
                                                            in_=c_[l, TS + 64:128].rearrange("r k d -> r (k d)")),
                  tailk, writes=[tailk])
            S.dma("sp", lambda e, o_=o_: e.dma_start(out=o_[l, 64:64 + n2].rearrange("r k d -> r (k d)"), in_=tailk.ap[0:n2, :]),
                  tailk, reads=[tailk])

    def store_states(l, g):
        cols_to_rows_store(hist[l].ap.rearrange("p j c -> p (j c)"), 24,
                           [(0, 24, O["conv_" + g][l].rearrange("j (c p) -> (j c) p", p=128))], [hist[l]])
        cols_to_rows_store(hst[l].ap, 8, [(0, 8, O["lru_" + g][l].rearrange("(c p) -> c p", p=128))], [hst[l]])
        cols_to_rows_store(nst[l].ap.rearrange("p h c -> p (h c)"), 8,
                           [(0, 8, O["n_" + g][l].rearrange("h (c p) -> (h c) p", p=128))], [nst[l]])
        S.dma("sp", lambda e: e.dma_start(out=O["C_" + g][l].rearrange("h (c p) e -> p h c e", p=128), in_=Cst[l].ap), Cst[l],
              reads=[Cst[l]])
        S.dma("sp", lambda e: e.dma_start(out=O["m_" + g][l].rearrange("(h o) -> h o", o=1), in_=mst[l].ap), mst[l],
              reads=[mst[l]])

    FL = [64]

    def mark(name):
        PHASES.append((name, len(S.ops["pe"])))

    def run_tile(src, dst, pos_idx, Tt, L, first_tile, last_tile, is_sample, g):
        mark("load")
        FL[0] = L
        load_rope(pos_idx, Tt, L)
        load_x(src, Tt)
        for l in range(DEPTH):
            mark("norm1")
            norm_to_u(l, "norm1_g", Tt)
            mark("lru")
            lru_phase(l, Tt)
            mark("mlstm")
            mlstm_phase(l, Tt, L)
            mark("attn")
            attn_phase(l, Tt, L, first_tile, is_sample, last_tile, g)
            mark("mlp")
            out_and_mlp(l, Tt)
            if last_tile:
                store_states(l, g)
        mark("store")
        store_y(dst, Tt)

    for l in range(DEPTH):
        init_states_zero(l)
    for t in range(NT):
        run_tile(x_p[t * T:(t + 1) * T, :], O["y_p"][t * T:(t + 1) * T, :], t * T, T, 64, t == 0, t == NT - 1, False, "p")
    if SAMPLE:
        for l in range(DEPTH):
            init_states_sample(l)
        run_tile(x_s, O["y_s"], SP, TS, TS, True, True, True, "s")
    stats = S.emit()
    return nc, stats


PHASES = []
CFG = dict(NT=16, T=256, DEPTH=2)
_cache = {}


def rope_tables(npos_list):
    half = 32
    inv = (10000.0 ** (-np.arange(half, dtype=np.float32) / half)).astype(np.float32)
    pos = np.asarray(npos_list, dtype=np.float32)
    ang = pos[:, None] * inv[None, :]
    return np.cos(ang).astype(np.float32), np.sin(ang).astype(np.float32)


def run(inputs, NT, T, DEPTH, n_cores, past_len=PAST_LEN):
    key = (NT, T, DEPTH)
    if key not in _cache:
        _cache[key] = build(NT, T, DEPTH, True)
    nc, stats = _cache[key]
    SPp = NT * T
    TS = 16
    pos = list(range(SPp)) + [past_len + i for i in range(TS)]
    rc, rs = rope_tables(pos)
    wnames = ["norm1_g", "w_in", "conv_w", "conv_b", "lru_wa", "lru_ba", "lru_wx", "lru_bx", "lru_lam", "m_bi", "m_bf",
              "m_norm_g", "qn_g", "kn_g", "sinks", "w_oa", "w_ob", "w_oc", "b_gate", "w_out", "norm2_g", "w_up", "w_down"]
    f = lambda a: np.ascontiguousarray(np.asarray(a, dtype=np.float32))
    wd = {k: f(inputs[k])[:DEPTH] for k in wnames}
    in_maps = []
    for b in range(n_cores):
        m = dict(wd)
        m["x_p"] = f(inputs["x_prompt"][b, :SPp])
        m["x_s"] = f(inputs["x_sample"][b])
        m["st_conv"] = f(inputs["state_conv"][:DEPTH, b])
        m["st_lru"] = f(inputs["state_lru"][:DEPTH, b])
        m["st_C"] = f(inputs["state_mlstm_C"][:DEPTH, b])
        m["st_n"] = f(inputs["state_mlstm_n"][:DEPTH, b])
        m["st_m"] = f(inputs["state_mlstm_m"][:DEPTH, b])
        m["c_k"] = f(inputs["cache_k"][:DEPTH, b])
        m["c_v"] = f(inputs["cache_v"][:DEPTH, b])
        m["rope_c"] = rc
        m["rope_s"] = rs
        in_maps.append(m)
    res = run_bass_kernel_spmd(nc, in_maps, core_ids=list(range(n_cores)))
    R = res.results
    outs = []
    outs.append(np.stack([np.asarray(R[b]["y_p"]) for b in range(n_cores)]))
    outs.append(np.stack([np.asarray(R[b]["y_s"]) for b in range(n_cores)]))
    for g in ["p", "s"]:
        for nm in ["conv_", "lru_", "C_", "n_", "m_", "k_", "v_"]:
            outs.append(np.stack([np.asarray(R[b][nm + g]) for b in range(n_cores)], axis=1))
    return tuple(o.astype(np.float32) for o in outs)


def kernel(**inputs):
    return run(inputs, CFG["NT"], CFG["T"], CFG["DEPTH"], 8)
```

```python
import numpy as np
from contextlib import ExitStack
import concourse.bass as bass
import concourse.mybir as mybir
from concourse.bass_utils import run_bass_kernel_spmd

F32 = mybir.dt.float32
BF16 = mybir.dt.bfloat16
AF = mybir.ActivationFunctionType
ALU = mybir.AluOpType
AX = mybir.AxisListType

D = 1024
EPS = 1e-6
IN_COLS = 10504
O_XA, O_GA, O_MQ, O_MK, O_MV, O_MO, O_MI, O_MF, O_AQ, O_AK, O_AV, O_G = (
    0, 1024, 2048, 3072, 4096, 5120, 6144, 6148, 6152, 7176, 7304, 7432)
PAST_LEN = 2048


class Sub:
    __slots__ = ("writer", "readers", "dma_readers")

    def __init__(self):
        self.writer = None
        self.readers = {}
        self.dma_readers = {}


class Buf:
    def __init__(self, name, ap, nsub=1):
        self.name = name
        self.ap = ap
        self.subs = [Sub() for _ in range(nsub)]
        self.dma_sem = None
        self.dma_count = 0

    def __getitem__(self, i):
        return (self, i)


def view(buf, ap):
    v = Buf(buf.name + "_v", ap, 0)
    v.subs = buf.subs
    return v


class ChunkView:
    def __init__(self, buf, ap, per):
        self.buf = buf
        self.ap = ap
        self.per = per
        self.subs = buf.subs

    def __getitem__(self, ch):
        return (self.buf, range(ch * self.per, (ch + 1) * self.per))


class Op:
    __slots__ = ("eng", "fn", "deps", "dma_waits", "signal", "is_dma", "buf", "count", "idx")

    def __init__(self, eng, fn):
        self.eng = eng
        self.fn = fn
        self.deps = []
        self.dma_waits = []
        self.signal = False
        self.is_dma = False
        self.buf = None
        self.count = None


def _subs(refs):
    out = []
    for r in refs:
        if isinstance(r, (Buf, ChunkView)):
            out.extend(r.subs)
        else:
            b, i = r
            if isinstance(i, (list, tuple, range)):
                out.extend(b.subs[j] for j in i)
            else:
                out.append(b.subs[i])
    return out


class Sched:
    ENGS = ["pe", "act", "dve", "pool", "sp"]

    def __init__(self, nc):
        self.nc = nc
        self.ops = {e: [] for e in self.ENGS}
        self.dma_bufs = []

    def add(self, eng, fn, reads=(), writes=(), dma_buf=None):
        op = Op(eng, fn)
        op.idx = len(self.ops[eng])
        need = {}
        dneed = {}

        def dep_on(d):
            if d is op:
                return
            if d.is_dma:
                dneed[id(d.buf)] = d.buf
            else:
                if d.eng == "pe" and eng == "pe":
                    return
                cur = need.get(d.eng)
                if cur is None or d.idx > cur.idx:
                    need[d.eng] = d

        rs = _subs(reads)
        ws = _subs(writes)
        for s in rs:
            if s.writer is not None:
                dep_on(s.writer)
        for s in ws:
            if s.writer is not None:
                dep_on(s.writer)
            for r in s.readers.values():
                dep_on(r)
            for b in s.dma_readers.values():
                dneed[id(b)] = b
        for d in need.values():
            op.deps.append(d)
            d.signal = True
        for b in dneed.values():
            op.dma_waits.append((b, b.dma_count))
        if dma_buf is not None:
            op.is_dma = True
            op.buf = dma_buf
            if dma_buf.dma_sem is None:
                dma_buf.dma_sem = "pending"
                self.dma_bufs.append(dma_buf)
            dma_buf.dma_count += 16
        for s in rs:
            if op.is_dma:
                s.dma_readers[id(dma_buf)] = dma_buf
            else:
                s.readers[eng] = op
        for s in ws:
            s.writer = op
            s.readers = {}
            s.dma_readers = {}
        self.ops[eng].append(op)
        return op

    def pe(self, fn, reads=(), writes=()):
        return self.add("pe", fn, reads, writes)

    def act(self, fn, reads=(), writes=()):
        return self.add("act", fn, reads, writes)

    def dve(self, fn, reads=(), writes=()):
        return self.add("dve", fn, reads, writes)

    def pool(self, fn, reads=(), writes=()):
        return self.add("pool", fn, reads, writes)

    def dma(self, eng, fn, buf, reads=(), writes=()):
        return self.add(eng, fn, reads, writes, dma_buf=buf)

    def emit(self):
        nc = self.nc
        with ExitStack() as st:
            esem = {}
            for e in ["pe", "act", "dve", "pool"]:
                esem[e] = st.enter_context(nc.semaphore("es_" + e))
            for b in self.dma_bufs:
                b.dma_sem = st.enter_context(nc.semaphore("ds_" + b.name))
            for e in ["pe", "act", "dve", "pool"]:
                c = 0
                for op in self.ops[e]:
                    if op.is_dma:
                        continue
                    if op.signal:
                        c += 1
                        op.count = c
            stats = {}
            block = st.enter_context(nc.Block())

            def run(e, engobj):
                waited = {}
                nw = 0
                for op in self.ops[e]:
                    for d in op.deps:
                        if waited.get(d.eng, 0) >= d.count:
                            continue
                        waited[d.eng] = d.count
                        engobj.wait_ge(esem[d.eng], d.count)
                        nw += 1
                    for (b, v) in op.dma_waits:
                        if waited.get(id(b), 0) >= v:
                            continue
                        waited[id(b)] = v
                        engobj.wait_ge(b.dma_sem, v)
                        nw += 1
                    inst = op.fn(engobj)
                    if op.is_dma:
                        inst.then_inc(op.buf.dma_sem, 16)
                    elif op.signal:
                        inst.then_inc(esem[e], 1)
                if e == "sp":
                    for ee in ["pe", "act", "dve", "pool"]:
                        last = 0
                        for op in self.ops[ee]:
                            if op.count:
                                last = op.count
                        if last:
                            engobj.wait_ge(esem[ee], last)
                    for b in self.dma_bufs:
                        engobj.wait_ge(b.dma_sem, b.dma_count)
                stats[e] = (len(self.ops[e]), nw)

            @block.tensor
            def _(eng):
                run("pe", eng)

            @block.scalar
            def _(eng):
                run("act", eng)

            @block.vector
            def _(eng):
                run("dve", eng)

            @block.gpsimd
            def _(eng):
                run("pool", eng)

            @block.sync
            def _(eng):
                run("sp", eng)

            return stats


def build(NT, T, DEPTH, SAMPLE, TS=16):
    nc = bass.Bass("TRN2", target_bir_lowering=False)
    S = Sched(nc)
    SP = NT * T
    NPOS = SP + TS

    def din(name, shape):
        return nc.dram_tensor(name, list(shape), F32, kind="ExternalInput").ap()

    def dout(name, shape):
        return nc.dram_tensor(name, list(shape), F32, kind="ExternalOutput").ap()

    x_p = din("x_p", [SP, D])
    x_s = din("x_s", [TS, D])
    st_conv = din("st_conv", [DEPTH, 3, D])
    st_lru = din("st_lru", [DEPTH, D])
    st_C = din("st_C", [DEPTH, 4, 256, 256])
    st_n = din("st_n", [DEPTH, 4, 256])
    st_m = din("st_m", [DEPTH, 4])
    c_k = din("c_k", [DEPTH, 128, 2, 64])
    c_v = din("c_v", [DEPTH, 128, 2, 64])
    rope_c = din("rope_c", [NPOS, 32])
    rope_s = din("rope_s", [NPOS, 32])
    W = {}
    for nm, shp in [("norm1_g", [DEPTH, D]), ("w_in", [DEPTH, D, IN_COLS]), ("conv_w", [DEPTH, 4, D]),
                    ("conv_b", [DEPTH, D]), ("lru_wa", [DEPTH, 8, 128, 128]), ("lru_ba", [DEPTH, D]),
                    ("lru_wx", [DEPTH, 8, 128, 128]), ("lru_bx", [DEPTH, D]), ("lru_lam", [DEPTH, D]),
                    ("m_bi", [DEPTH, 4]), ("m_bf", [DEPTH, 4]), ("m_norm_g", [DEPTH, D]), ("qn_g", [DEPTH, 64]),
                    ("kn_g", [DEPTH, 64]), ("sinks", [DEPTH, 16]), ("w_oa", [DEPTH, D, D]), ("w_ob", [DEPTH, D, D]),
                    ("w_oc", [DEPTH, D, D]), ("b_gate", [DEPTH, 3, D]), ("w_out", [DEPTH, D, D]),
                    ("norm2_g", [DEPTH, D]), ("w_up", [DEPTH, D, 4096]), ("w_down", [DEPTH, 4096, D])]:
        W[nm] = din(nm, shp)
    O = {}
    O["y_p"] = dout("y_p", [SP, D])
    O["y_s"] = dout("y_s", [TS, D])
    for g in ["p", "s"]:
        O["conv_" + g] = dout("conv_" + g, [DEPTH, 3, D])
        O["lru_" + g] = dout("lru_" + g, [DEPTH, D])
        O["C_" + g] = dout("C_" + g, [DEPTH, 4, 256, 256])
        O["n_" + g] = dout("n_" + g, [DEPTH, 4, 256])
        O["m_" + g] = dout("m_" + g, [DEPTH, 4])
        O["k_" + g] = dout("k_" + g, [DEPTH, 128, 2, 64])
        O["v_" + g] = dout("v_" + g, [DEPTH, 128, 2, 64])

    cnt = [0]

    def sb(shape, dt=F32, nsub=1, name=None):
        cnt[0] += 1
        nm = (name or "t") + "_%d" % cnt[0]
        t = nc.alloc_sbuf_tensor(nm, list(shape), dt)
        return Buf(nm, t.ap(), nsub)

    banks = []
    for i in range(8):
        t = nc.alloc_psum_tensor("bank%d" % i, [128, 512], F32)
        banks.append(Buf("bank%d" % i, t.ap(), 1))
    mm_ring = [0]
    NMM = 2

    def ps_mm():
        b = banks[mm_ring[0] % 8]
        mm_ring[0] += 1
        return b

    def aux(role):
        return banks[NMM + role]

    ev = [0]

    def evac(fn_act, fn_dve, reads, writes):
        ev[0] += 1
        if ev[0] % 2 == 0:
            return S.act(fn_act, reads, writes)
        return S.dve(fn_dve, reads, writes)

    def copy_any(out_ap, in_ap, reads, writes, scale=None):
        if scale is None:
            return evac(lambda e: e.activation(out=out_ap, in_=in_ap, func=AF.Copy),
                        lambda e: e.tensor_copy(out=out_ap, in_=in_ap), reads, writes)
        return evac(lambda e: e.activation(out=out_ap, in_=in_ap, func=AF.Copy, scale=scale),
                    lambda e: e.tensor_scalar(out=out_ap, in0=in_ap, scalar1=scale, scalar2=None, op0=ALU.mult),
                    reads, writes)

    ident = sb([128, 128], F32, name="ident")
    identb = sb([128, 128], BF16, name="identb")
    ones_bf = sb([128, 128], BF16, name="onesbf")
    ones32 = sb([4, 128], F32, name="ones32")
    cmask = sb([64, 64], F32, name="cmask")
    hmask = sb([4, 4, 8], F32, name="hmask")
    maskrow = sb([4, 512], F32, name="maskrow")
    S.pool(lambda e: e.memset(ident.ap, 0.0), writes=[ident])
    S.pool(lambda e: e.affine_select(out=ident.ap, in_=ident.ap, pattern=[[-1, 128]], compare_op=ALU.not_equal,
                                     fill=1.0, base=0, channel_multiplier=1), reads=[ident], writes=[ident])
    S.dve(lambda e: e.tensor_copy(out=identb.ap, in_=ident.ap), reads=[ident], writes=[identb])
    S.dve(lambda e: e.memset(ones_bf.ap, 1.0), writes=[ones_bf])
    S.dve(lambda e: e.memset(ones32.ap, 1.0), writes=[ones32])
    S.pool(lambda e: e.memset(cmask.ap, 1.0), writes=[cmask])
    S.pool(lambda e: e.affine_select(out=cmask.ap, in_=cmask.ap, pattern=[[1, 64]], compare_op=ALU.is_ge,
                                     fill=0.0, base=0, channel_multiplier=-1), reads=[cmask], writes=[cmask])
    S.pool(lambda e: e.memset(hmask.ap, 1.0), writes=[hmask])
    S.pool(lambda e: e.affine_select(out=hmask.ap, in_=hmask.ap, pattern=[[1, 4], [0, 8]], compare_op=ALU.is_equal,
                                     fill=0.0, base=0, channel_multiplier=-1), reads=[hmask], writes=[hmask])
    S.dve(lambda e: e.memset(maskrow.ap, 1.0), writes=[maskrow])
    S.dve(lambda e: e.memset(maskrow.ap.rearrange("p (c l) -> p c l", l=64)[:, :, 0:1], 0.0), writes=[maskrow])

    VEC = ["norm1_g", "norm2_g", "conv_b", "lru_ba", "lru_bx", "lru_lam", "cw0", "cw1", "cw2", "cw3", "bg0", "bg1", "bg2"]
    NV = len(VEC)
    colv = [sb([128, NV * 8], F32, name="colv") for _ in range(DEPTH)]
    nsp8 = [sb([128, 8], F32, name="nsp8") for _ in range(DEPTH)]
    nsp4 = [sb([128, 8], F32, name="nsp4") for _ in range(DEPTH)]
    hb = [sb([128, 16], F32, name="hb") for _ in range(DEPTH)]
    wa_bf = [sb([128, 8, 128], BF16, name="wa") for _ in range(DEPTH)]
    wx_bf = [sb([128, 8, 128], BF16, name="wx") for _ in range(DEPTH)]
    mg_row1 = sb([64, D], F32, name="mgrow")
    mg_row = [mg_row1 for _ in range(DEPTH)]
    qg_row = [sb([128, 64], F32, name="qgrow") for _ in range(DEPTH)]
    kg_row = [sb([64, 64], F32, name="kgrow") for _ in range(DEPTH)]
    esink = [sb([128, 16], F32, name="esink") for _ in range(DEPTH)]
    bi_col = [sb([4, 1], F32, name="bi") for _ in range(DEPTH)]
    nbf_col = [sb([4, 1], F32, name="nbf") for _ in range(DEPTH)]
    vstage = sb([128, 128], F32, name="vstage")

    def vcol(l, name, c):
        i = VEC.index(name)
        return colv[l].ap[:, i * 8 + c:i * 8 + c + 1]

    for l in range(DEPTH):
        srcs = {"norm1_g": W["norm1_g"][l], "norm2_g": W["norm2_g"][l], "conv_b": W["conv_b"][l],
                "lru_ba": W["lru_ba"][l], "lru_bx": W["lru_bx"][l], "lru_lam": W["lru_lam"][l]}
        for j in range(4):
            srcs["cw%d" % j] = W["conv_w"][l, j]
        for j in range(3):
            srcs["bg%d" % j] = W["b_gate"][l, j]
        for i, nm in enumerate(VEC):
            src = srcs[nm].rearrange("(c p) -> c p", p=128)
            S.dma("sp", lambda e, i=i, src=src: e.dma_start(out=vstage.ap[i * 8:(i + 1) * 8, :], in_=src), vstage,
                  writes=[vstage])
        pb = aux(5)
        S.pe(lambda e, pb=pb: e.transpose(pb.ap[:, 0:NV * 8], vstage.ap[0:NV * 8, :], ident.ap[0:NV * 8, 0:NV * 8]),
             reads=[vstage, ident], writes=[pb])
        S.dve(lambda e, pb=pb, l=l: e.tensor_copy(out=colv[l].ap, in_=pb.ap[:, 0:NV * 8]), reads=[pb], writes=[colv[l]])
        lam = colv[l].ap[:, VEC.index("lru_lam") * 8:VEC.index("lru_lam") * 8 + 8]
        S.act(lambda e, l=l, lam=lam: e.activation(out=nsp8[l].ap, in_=lam, func=AF.Exp, scale=-1.0),
              reads=[colv[l]], writes=[nsp8[l]])
        S.act(lambda e, l=l: e.activation(out=nsp8[l].ap, in_=nsp8[l].ap, func=AF.Ln, bias=1.0),
              reads=[nsp8[l]], writes=[nsp8[l]])
        S.dve(lambda e, l=l: e.tensor_scalar(out=nsp4[l].ap, in0=nsp8[l].ap, scalar1=-4.0, scalar2=None, op0=ALU.mult),
              reads=[nsp8[l]], writes=[nsp4[l]])
        S.dve(lambda e, l=l: e.tensor_scalar(out=nsp8[l].ap, in0=nsp8[l].ap, scalar1=-8.0, scalar2=None, op0=ALU.mult),
              reads=[nsp8[l], nsp4[l]], writes=[nsp8[l]])
        _ib = VEC.index("lru_ba") * 8
        S.dve(lambda e, l=l, _ib=_ib: e.tensor_scalar(out=hb[l].ap, in0=colv[l].ap[:, _ib:_ib + 16], scalar1=0.5, scalar2=None,
                                                      op0=ALU.mult), reads=[colv[l]], writes=[hb[l]])
        S.dma("pool", lambda e, l=l: e.dma_start(out=wa_bf[l].ap, in_=W["lru_wa"][l].rearrange("n c d -> c n d")),
              wa_bf[l], writes=[wa_bf[l]])
        S.dma("pool", lambda e, l=l: e.dma_start(out=wx_bf[l].ap, in_=W["lru_wx"][l].rearrange("n c d -> c n d")),
              wx_bf[l], writes=[wx_bf[l]])
        S.dma("sp", lambda e, l=l: e.dma_start(out=qg_row[l].ap, in_=W["qn_g"][l].partition_broadcast(128)),
              qg_row[l], writes=[qg_row[l]])
        S.dma("sp", lambda e, l=l: e.dma_start(out=kg_row[l].ap, in_=W["kn_g"][l].partition_broadcast(64)),
              kg_row[l], writes=[kg_row[l]])
        S.dma("sp", lambda e, l=l: e.dma_start(out=esink[l].ap, in_=W["sinks"][l].partition_broadcast(128)),
              esink[l], writes=[esink[l]])
        S.act(lambda e, l=l: e.activation(out=esink[l].ap, in_=esink[l].ap, func=AF.Exp), reads=[esink[l]],
              writes=[esink[l]])
        S.dma("sp", lambda e, l=l: e.dma_start(out=bi_col[l].ap, in_=W["m_bi"][l].rearrange("(h o) -> h o", o=1)),
              bi_col[l], writes=[bi_col[l]])
        S.dma("sp", lambda e, l=l: e.dma_start(out=nbf_col[l].ap, in_=W["m_bf"][l].rearrange("(h o) -> h o", o=1)),
              nbf_col[l], writes=[nbf_col[l]])
        S.dve(lambda e, l=l: e.tensor_scalar(out=nbf_col[l].ap, in0=nbf_col[l].ap, scalar1=-1.0, scalar2=None,
                                             op0=ALU.mult), reads=[nbf_col[l]], writes=[nbf_col[l]])

    SCR = {}
    SCRB = {}
    for nm, R_, C_ in [("w_in", D, IN_COLS), ("w_oa", D, D), ("w_ob", D, D), ("w_oc", D, D), ("w_out", D, D),
                       ("w_up", D, 4096), ("w_down", 4096, D)]:
        SCR[nm] = nc.dram_tensor(nm + "_bf", [DEPTH, R_, C_], BF16, kind="Internal").ap()
    WIN_GROUPS = [(0, 2048), (2048, 4096), (O_G, O_G + 1024), (4096, O_AQ), (O_G + 1024, O_G + 2048), (O_AQ, O_G),
                  (O_G + 2048, IN_COLS)]
    for l in range(DEPTH):
        for gi_, (g0, g1) in enumerate(WIN_GROUPS):
            b_ = Buf("w_in_bf%d_%d" % (l, gi_), None, 2)
            SCRB[("w_in", l, gi_)] = b_
            for rb in range(2):
                S.dma("pool", lambda e, l=l, rb=rb, g0=g0, g1=g1: e.dma_start(out=SCR["w_in"][l, rb * 512:(rb + 1) * 512, g0:g1],
                                                                              in_=W["w_in"][l, rb * 512:(rb + 1) * 512, g0:g1]),
                      b_, writes=[b_[rb]])
        for nm in ["w_oa", "w_ob", "w_oc", "w_out", "w_up", "w_down"]:
            R_ = W[nm].shape[1]
            nblk = R_ // 256
            b_ = Buf("%s_bf%d" % (nm, l), None, nblk)
            SCRB[(nm, l)] = b_
            for rb in range(nblk):
                S.dma("pool", lambda e, nm=nm, l=l, rb=rb: e.dma_start(out=SCR[nm][l, rb * 256:(rb + 1) * 256, :],
                                                                       in_=W[nm][l, rb * 256:(rb + 1) * 256, :]),
                      b_, writes=[b_[rb]])

    NCHM = max(T // 64, 1)
    hist = [sb([128, 3, 8], F32, name="hist") for _ in range(DEPTH)]
    hst = [sb([128, 8], F32, name="hst") for _ in range(DEPTH)]
    Cst = [sb([128, 4, 2, 256], F32, nsub=4, name="Cst") for _ in range(DEPTH)]
    nst = [sb([128, 4, 2], F32, name="nst") for _ in range(DEPTH)]
    mst = [sb([4, 1], F32, name="mst") for _ in range(DEPTH)]
    NSLOT = 2 + NCHM
    KTwin = [sb([64, 2, NSLOT * 64], BF16, name="KTwin") for _ in range(DEPTH)]
    vaug = [sb([64, NSLOT, 2, 192], BF16, name="vaug") for _ in range(DEPTH)]

    xT = sb([128, 8, T], F32, nsub=8, name="xT")
    uT = sb([128, 8, T], BF16, nsub=8, name="uT")
    NSLAB = 4
    slabs = [sb([128, 4096], BF16, name="slab") for _ in range(NSLAB)]
    slab_i = [0]
    TB = min(T, 128)
    xin = [sb([128, D], F32, name="xin") for _ in range(2)]
    sqb = [sb([128, T], BF16, name="sqb") for _ in range(2)]
    rstd = sb([128, T], F32, name="rstd")
    xa_w = [sb([128, 3 + T], F32, name="xaw") for _ in range(2)]
    xc3 = [sb([128, T], F32, name="xc") for _ in range(3)]
    xcb = [sb([128, T], BF16, name="xcb") for _ in range(2)]
    rr = [sb([128, T], F32, name="rr") for _ in range(2)]
    ii = [sb([128, T], F32, name="ii") for _ in range(2)]
    aa = [sb([128, T], F32, name="aa") for _ in range(2)]
    sq1 = [sb([128, T], F32, name="sq1") for _ in range(2)]
    hh = [sb([128, T], F32, name="hh") for _ in range(2)]
    gel3 = [sb([128, T], F32, name="gel") for _ in range(3)]
    gel = gel3
    hgT = sb([128, 8, T], BF16, nsub=8, name="hgT")
    sg = [gel[0], gel[1]]
    tmpm = [hh[0], hh[1]]
    mix = sb([128, 8, T], F32, nsub=8, name="mix")
    qT = sb([128, 8, T], BF16, nsub=8, name="qT")
    kT = sb([128, 8, T], BF16, nsub=8, name="kT")
    g64 = [sb([64, 16, T], BF16, nsub=16, name="g64") for _ in range(4)]
    PER = 16 // NCHM

    def cview(g, vw4=False):
        flat = g.ap.rearrange("p h t -> p (h t)")
        if vw4:
            return ChunkView(g, flat.rearrange("p (c h e) -> p c h e", c=NCHM, h=4), PER)
        return ChunkView(g, flat.rearrange("p (c f) -> p c f", c=NCHM), PER)

    ktok = cview(g64[0])
    vw = cview(g64[1], True)
    sgm = cview(g64[2])
    hmtok = cview(g64[3])
    hmT = sb([128, 8, T], BF16, nsub=8, name="hmT")
    _gsrc = [rr[0], rr[1], ii[0], ii[1], aa[0], aa[1], sq1[0], sq1[1]]
    grow = {nm: view(_gsrc[i], _gsrc[i].ap[0:4, :]) for i, nm in enumerate(["ig", "sp", "b", "a", "ea", "cl", "iwt", "tmp"])}
    gsm = {nm: sb([4, NCHM], F32, name="gs_" + nm) for nm in ["amax", "d0", "M", "mnew", "mprev", "iw"]}
    rhsm = sb([4, 4, NCHM], F32, name="rhsm")
    iw_rep = sb([128, 4 * NCHM], F32, name="iwrep")
    colq = sb([64, NCHM, 12], F32, name="colq")
    eab = sb([64, NCHM, 4], BF16, name="eab")
    Cnb = [sb([128, 2, 256], BF16, name="Cnb") for _ in range(2)]
    nb = [sb([128, 2], BF16, name="nb") for _ in range(4)]
    nb4 = nb
    smask = [sb([64, 4, 64], BF16, name="smask") for _ in range(2)]
    den_s = [sb([64, 4], F32, name="dens") for _ in range(2)]
    den_t = [sb([64, 4], F32, name="dent") for _ in range(2)]
    hn = [view(xin[i], xin[i].ap[0:64, :].rearrange("p (h d) -> p h d", h=4)) for i in range(2)]
    ssm = [sb([64, 4], F32, name="ssm") for _ in range(2)]
    NBLK = (T + TB - 1) // TB
    ctab_k = sb([64, NCHM, 32], F32, name="ctabk")
    stab_k = sb([64, NCHM, 32], F32, name="stabk")
    ctab_q = sb([128, NBLK, 32], F32, name="ctabq")
    stab_q = sb([128, NBLK, 32], F32, name="stabq")
    QT_all = g64[0]
    OT2 = hmT
    qsq = sb([128, 512], F32, name="qsq")
    qss = [sb([128, 8], F32, name="qss") for _ in range(2)]
    qn = [sb([128, 8, 64], F32, name="qn") for _ in range(2)]
    qt1 = sb([128, 8, 32], F32, name="qt1")
    qt2 = sb([128, 8, 32], F32, name="qt2")
    qr = [sb([128, 8, 64], BF16, name="qr") for _ in range(2)]

    def _as_cnb(b_):
        return view(b_, b_.ap.rearrange("p a b -> p (a b)").bitcast(BF16).rearrange("p (c e) -> p c e", c=2))

    Cnb4 = [Cnb[0], Cnb[1], _as_cnb(qt1), _as_cnb(qt2)]
    kss = [sb([64, 2], F32, name="kss") for _ in range(2)]
    ksq = sb([64, 128], F32, name="ksq")
    kn = [sb([64, 2, 64], F32, name="kn") for _ in range(2)]
    kt1 = sb([64, 2, 32], F32, name="kt1")
    kt2 = sb([64, 2, 32], F32, name="kt2")
    kr = [sb([64, 128], F32, name="kr") for _ in range(2)]
    krb = [sb([64, 128], BF16, name="krb") for _ in range(2)]
    vf = [sb([64, 128], F32, name="vf") for _ in range(2)]
    pT = [sb([64, 512], BF16, name="pT") for _ in range(6)]
    pT_i = [0]
    dsum = [sb([128, 512], F32, name="dsum") for _ in range(2)]
    hsq = view(dsum[0], dsum[0].ap[0:64, :].bitcast(BF16).rearrange("p (h d) -> p h d", h=4))
    hid_parts = [hgT, qT, kT, hmT]
    rl = [rr[0], rr[1]]
    yout = xin
    kcs = view(qsq, qsq.ap[0:64, 0:256].rearrange("p (s f) -> p s f", s=2))
    kcb = sb([64, 2, 128], BF16, name="kcb")
    tailk = vf[0]

    def pipeline(gens, newest_first):
        gens = list(gens)
        active = []
        i = 0
        while i < len(gens) or active:
            if i < len(gens):
                active.append(gens[i])
                i += 1
            order = list(reversed(active)) if newest_first else list(active)
            for g in order:
                try:
                    next(g)
                except StopIteration:
                    active.remove(g)

    slab_live = [False] * NSLAB

    def release(sl):
        slab_live[slabs.index(sl)] = False

    def load_slab(src_ap, view, srcbuf, hold=False):
        for _ in range(NSLAB + 1):
            i_ = slab_i[0] % NSLAB
            slab_i[0] += 1
            if not slab_live[i_]:
                break
        else:
            raise RuntimeError("all slabs live")
        sl = slabs[i_]
        if hold:
            slab_live[i_] = True
        dst = view(sl.ap)
        S.dma("sp", lambda e: e.dma_start(out=dst, in_=src_ap), sl, reads=[srcbuf], writes=[sl])
        return sl, dst

    def slab_k8(l, wname, c0, ncols, hold=False):
        src = SCR[wname][l][:, c0:c0 + ncols].rearrange("(kc p) n -> p kc n", p=128)
        if wname == "w_in":
            gi_ = [i for i, (g0, g1) in enumerate(WIN_GROUPS) if g0 <= c0 and c0 + ncols <= g1]
            assert len(gi_) == 1, (c0, ncols)
            sb_ = SCRB[("w_in", l, gi_[0])]
        else:
            sb_ = SCRB[(wname, l)]
        return load_slab(src, lambda a: a[:, 0:8 * ncols].rearrange("p (k n) -> p k n", k=8), sb_, hold=hold)

    def qk_proj_gen(l, Tt, L):
        NCH = Tt // L
        for half in range(2):
            sl, v = slab_k8(l, "w_in", O_MQ + half * 512, 512, hold=True)
            for c4 in range(4):
                c = half * 4 + c4
                pb = ps_mm()
                fm_proj(pb, sl, v, c4 * 128, uT, Tt)
                S.dve(lambda e, pb=pb, c=c: e.tensor_copy(out=qT.ap[:, c, 0:Tt], in_=pb.ap[:, 0:Tt]), reads=[pb], writes=[qT[c]])
                yield
            release(sl)
        for half in range(2):
            sl, v = slab_k8(l, "w_in", O_MK + half * 512, 512, hold=True)
            for c4 in range(4):
                c = half * 4 + c4
                pb = ps_mm()
                fm_proj(pb, sl, v, c4 * 128, uT, Tt)
                S.dve(lambda e, pb=pb, c=c: e.tensor_scalar(out=kT.ap[:, c, 0:Tt], in0=pb.ap[:, 0:Tt], scalar1=1.0 / 16.0,
                                                            scalar2=None, op0=ALU.mult), reads=[pb], writes=[kT[c]])
                yield
            for ch in range(NCH):
                pb = ps_mm()
                for k in range(8):
                    S.pe(lambda e, k=k, pb=pb, ch=ch, v=v: e.matmul(pb.ap[0:L, :], uT.ap[:, k, ch * L:(ch + 1) * L], v[:, k, :],
                                                                    start=(k == 0), stop=(k == 7)),
                         reads=[sl, uT[k]], writes=[pb])
                S.dve(lambda e, pb=pb, ch=ch, half=half: e.tensor_scalar(out=ktok.ap[0:L, ch, half * 512:(half + 1) * 512],
                                                                         in0=pb.ap[0:L, :], scalar1=1.0 / 16.0, scalar2=None,
                                                                         op0=ALU.mult), reads=[pb], writes=[ktok[ch]])
                yield
            release(sl)

    def fm_proj(pb, sl, sview, col, act, Tt, KC=8):
        for k in range(KC):
            S.pe(lambda e, k=k: e.matmul(pb.ap[:, 0:Tt], sview[:, k, col:col + 128], act.ap[:, k, 0:Tt],
                                         start=(k == 0), stop=(k == KC - 1)),
                 reads=[sl, act[k]], writes=[pb])

    def norm_to_u(l, gname, Tt):
        pb = aux(0)
        for c in range(8):
            q = sqb[c % 2]
            S.act(lambda e, c=c, q=q: e.activation(out=q.ap[:, 0:Tt], in_=xT.ap[:, c, 0:Tt], func=AF.Square),
                  reads=[xT[c]], writes=[q])
            S.pe(lambda e, c=c, q=q: e.matmul(pb.ap[:, 0:Tt], ones_bf.ap, q.ap[:, 0:Tt], start=(c == 0), stop=(c == 7)),
                 reads=[q, ones_bf], writes=[pb])
        S.act(lambda e: e.activation(out=rstd.ap[:, 0:Tt], in_=pb.ap[:, 0:Tt], func=AF.Ln, scale=1.0 / D, bias=EPS),
              reads=[pb], writes=[rstd])
        S.act(lambda e: e.activation(out=rstd.ap[:, 0:Tt], in_=rstd.ap[:, 0:Tt], func=AF.Exp, scale=-0.5), reads=[rstd], writes=[rstd])
        for c in range(8):
            S.dve(lambda e, c=c: e.scalar_tensor_tensor(out=uT.ap[:, c, 0:Tt], in0=xT.ap[:, c, 0:Tt],
                                                        scalar=vcol(l, gname, c), in1=rstd.ap[:, 0:Tt],
                                                        op0=ALU.mult, op1=ALU.mult),
                  reads=[xT[c], rstd, colv[l]], writes=[uT[c]])

    def merge_branch(l, br, featT, Tt, K64=False):
        wname = ["w_oa", "w_ob", "w_oc"][br]
        for half in range(2):
            slg, vg = slab_k8(l, "w_in", O_G + br * 1024 + half * 512, 512)
            if not K64:
                slo, vo = slab_k8(l, wname, half * 512, 512)
            for c4 in range(4):
                c = half * 4 + c4
                pg = ps_mm()
                fm_proj(pg, slg, vg, c4 * 128, uT, Tt)
                s_ = sg[c % 2]
                S.act(lambda e, pg=pg, s_=s_, c=c: e.activation(out=s_.ap[:, 0:Tt], in_=pg.ap[:, 0:Tt], func=AF.Sigmoid,
                                                                bias=vcol(l, "bg%d" % br, c)),
                      reads=[pg, colv[l]], writes=[s_])
                py = ps_mm()
                if not K64:
                    fm_proj(py, slo, vo, c4 * 128, featT, Tt)
                else:
                    if c4 % 2 == 0:
                        src = SCR["w_oc"][l][:, c * 128:c * 128 + 256].rearrange("(h d) n -> d h n", d=64)
                        slo, vo = load_slab(src, lambda a: a[0:64, :].rearrange("p (h n) -> p h n", h=16), SCRB[("w_oc", l)])
                    off = (c4 % 2) * 128
                    for h in range(16):
                        S.pe(lambda e, h=h, py=py, vo=vo, off=off: e.matmul(py.ap[:, 0:Tt], vo[:, h, off:off + 128],
                                                                            featT.ap[:, h, 0:Tt], start=(h == 0),
                                                                            stop=(h == 15)),
                             reads=[slo, featT[h]], writes=[py])
                if br == 0:
                    S.dve(lambda e, py=py, s_=s_, c=c: e.tensor_tensor(out=mix.ap[:, c, 0:Tt], in0=py.ap[:, 0:Tt],
                                                                       in1=s_.ap[:, 0:Tt], op=ALU.mult),
                          reads=[py, s_], writes=[mix[c]])
                else:
                    t_ = tmpm[c % 2]
                    S.dve(lambda e, py=py, s_=s_, t_=t_: e.tensor_tensor(out=t_.ap[:, 0:Tt], in0=py.ap[:, 0:Tt],
                                                                         in1=s_.ap[:, 0:Tt], op=ALU.mult),
                          reads=[py, s_], writes=[t_])
                    S.pool(lambda e, t_=t_, c=c: e.tensor_tensor(out=mix.ap[:, c, 0:Tt], in0=mix.ap[:, c, 0:Tt],
                                                                 in1=t_.ap[:, 0:Tt], op=ALU.add),
                           reads=[t_, mix[c]], writes=[mix[c]])

    def lru_phase(l, Tt):
        sl_ = {}

        def body(c):
            half, c4 = divmod(c, 4)
            if c4 == 0:
                if "a" in sl_:
                    release(sl_["a"][0])
                    release(sl_["g"][0])
                sl_["a"] = slab_k8(l, "w_in", O_XA + half * 512, 512, hold=True)
                sl_["g"] = slab_k8(l, "w_in", O_GA + half * 512, 512, hold=True)
            sla, va = sl_["a"]
            slg, vg = sl_["g"]
            j = c % 2
            j3 = c % 3
            pa = ps_mm()
            fm_proj(pa, sla, va, c4 * 128, uT, Tt)
            xw = xa_w[j]
            S.dve(lambda e: e.tensor_copy(out=xw.ap[:, 0:3], in_=hist[l].ap[:, :, c]), reads=[hist[l]], writes=[xw])
            S.act(lambda e: e.activation(out=xw.ap[:, 3:3 + Tt], in_=pa.ap[:, 0:Tt], func=AF.Copy), reads=[pa], writes=[xw])
            S.dve(lambda e: e.tensor_copy(out=hist[l].ap[:, :, c], in_=xw.ap[:, Tt:Tt + 3]), reads=[xw], writes=[hist[l]])
            x_ = xc3[j3]
            S.dve(lambda e: e.tensor_scalar(out=x_.ap[:, 0:Tt], in0=xw.ap[:, 0:Tt], scalar1=vcol(l, "cw0", c),
                                            scalar2=vcol(l, "conv_b", c), op0=ALU.mult, op1=ALU.add),
                  reads=[xw, colv[l]], writes=[x_])
            for jj in range(1, 4):
                S.dve(lambda e, jj=jj: e.scalar_tensor_tensor(out=x_.ap[:, 0:Tt], in0=xw.ap[:, jj:jj + Tt],
                                                              scalar=vcol(l, "cw%d" % jj, c), in1=x_.ap[:, 0:Tt],
                                                              op0=ALU.mult, op1=ALU.add), reads=[xw, x_, colv[l]], writes=[x_])
            xb_ = xcb[j]
            S.dve(lambda e: e.tensor_copy(out=xb_.ap[:, 0:Tt], in_=x_.ap[:, 0:Tt]), reads=[x_], writes=[xb_])
            pg = ps_mm()
            fm_proj(pg, slg, vg, c4 * 128, uT, Tt)
            g_ = gel3[j3]
            S.act(lambda e: e.activation(out=g_.ap[:, 0:Tt], in_=pg.ap[:, 0:Tt], func=AF.Gelu_apprx_tanh), reads=[pg], writes=[g_])
            yield
            pr = aux(1)
            S.pe(lambda e: e.matmul(pr.ap[:, 0:Tt], wa_bf[l].ap[:, c, :], xb_.ap[:, 0:Tt], start=True, stop=True),
                 reads=[wa_bf[l], xb_], writes=[pr])
            pi = aux(2)
            S.pe(lambda e: e.matmul(pi.ap[:, 0:Tt], wx_bf[l].ap[:, c, :], xb_.ap[:, 0:Tt], start=True, stop=True),
                 reads=[wx_bf[l], xb_], writes=[pi])
            r_, i_, a_, q_, h_ = rr[j], ii[j], aa[j], sq1[j], hh[j]
            S.act(lambda e: e.activation(out=r_.ap[:, 0:Tt], in_=pr.ap[:, 0:Tt], func=AF.Tanh, scale=0.5, bias=hb[l].ap[:, c:c + 1]),
                  reads=[pr, hb[l]], writes=[r_])
            S.act(lambda e: e.activation(out=i_.ap[:, 0:Tt], in_=pi.ap[:, 0:Tt], func=AF.Tanh, scale=0.5,
                                         bias=hb[l].ap[:, 8 + c:9 + c]), reads=[pi, hb[l]], writes=[i_])
            yield
            S.act(lambda e: e.activation(out=a_.ap[:, 0:Tt], in_=r_.ap[:, 0:Tt], func=AF.Exp, scale=nsp4[l].ap[:, c:c + 1],
                                         bias=nsp4[l].ap[:, c:c + 1]), reads=[r_, nsp4[l]], writes=[a_])
            S.act(lambda e: e.activation(out=q_.ap[:, 0:Tt], in_=r_.ap[:, 0:Tt], func=AF.Exp, scale=nsp8[l].ap[:, c:c + 1],
                                         bias=nsp8[l].ap[:, c:c + 1]), reads=[r_, nsp8[l]], writes=[q_])
            S.act(lambda e: e.activation(out=q_.ap[:, 0:Tt], in_=q_.ap[:, 0:Tt], func=AF.Ln, scale=-1.0, bias=1.0),
                  reads=[q_], writes=[q_])
            S.act(lambda e: e.activation(out=q_.ap[:, 0:Tt], in_=q_.ap[:, 0:Tt], func=AF.Exp, scale=0.5), reads=[q_], writes=[q_])
            S.dve(lambda e: e.scalar_tensor_tensor(out=i_.ap[:, 0:Tt], in0=i_.ap[:, 0:Tt], scalar=1.0, in1=x_.ap[:, 0:Tt],
                                                   op0=ALU.add, op1=ALU.mult), reads=[i_, x_], writes=[i_])
            S.dve(lambda e: e.scalar_tensor_tensor(out=i_.ap[:, 0:Tt], in0=i_.ap[:, 0:Tt], scalar=0.5, in1=q_.ap[:, 0:Tt],
                                                   op0=ALU.mult, op1=ALU.mult), reads=[i_, q_], writes=[i_])
            S.dve(lambda e: e.tensor_tensor_scan(out=h_.ap[:, 0:Tt], data0=a_.ap[:, 0:Tt], data1=i_.ap[:, 0:Tt],
                                                 initial=hst[l].ap[:, c:c + 1], op0=ALU.mult, op1=ALU.add),
                  reads=[a_, i_, hst[l]], writes=[h_])
            S.pool(lambda e: e.tensor_copy(out=hst[l].ap[:, c:c + 1], in_=h_.ap[:, Tt - 1:Tt]), reads=[h_], writes=[hst[l]])
            S.pool(lambda e: e.tensor_tensor(out=hgT.ap[:, c, 0:Tt], in0=h_.ap[:, 0:Tt], in1=g_.ap[:, 0:Tt], op=ALU.mult),
                   reads=[g_, h_], writes=[hgT[c]])
            yield

        gens = [body(c) for c in range(8)]
        filler = qk_proj_gen(l, Tt, FL[0])
        for rnd in range(8 + 2):
            sa = rnd if rnd < 8 else None
            sb1 = rnd - 1 if 0 <= rnd - 1 < 8 else None
            sb2 = rnd - 2 if 0 <= rnd - 2 < 8 else None
            order = [sa, sb1, sb2] if rnd % 2 == 0 else [sb2, sa, sb1]
            for gi in order:
                if gi is not None:
                    next(gens[gi])
            for _ in range(3):
                next(filler, None)
        release(sl_["a"][0])
        release(sl_["g"][0])
        for _ in filler:
            pass
        merge_branch(l, 0, hgT, Tt)

    def mlstm_phase(l, Tt, L):
        NCH = Tt // L
        S.dma("sp", lambda e: e.dma_start(out=mg_row1.ap, in_=W["m_norm_g"][l].partition_broadcast(64)), mg_row1,
              writes=[mg_row1])
        slg, vg = slab_k8(l, "w_in", O_MI, 8)
        pi = aux(0)
        pf = aux(1)
        for k in range(8):
            S.pe(lambda e, k=k: e.matmul(pi.ap[0:4, 0:Tt], vg[:, k, 0:4], uT.ap[:, k, 0:Tt], start=(k == 0), stop=(k == 7)),
                 reads=[slg, uT[k]], writes=[pi])
        for k in range(8):
            S.pe(lambda e, k=k: e.matmul(pf.ap[0:4, 0:Tt], vg[:, k, 4:8], uT.ap[:, k, 0:Tt], start=(k == 0), stop=(k == 7)),
                 reads=[slg, uT[k]], writes=[pf])
        G = {k: v.ap[:, 0:Tt] for k, v in grow.items()}
        Gs = {k: v.ap[:, 0:NCH] for k, v in gsm.items()}
        S.act(lambda e: e.activation(out=G["ig"], in_=pi.ap[0:4, 0:Tt], func=AF.Identity, bias=bi_col[l].ap),
              reads=[pi, bi_col[l]], writes=[grow["ig"]])
        S.act(lambda e: e.activation(out=G["sp"], in_=pf.ap[0:4, 0:Tt], func=AF.Exp, scale=-1.0, bias=nbf_col[l].ap),
              reads=[pf, nbf_col[l]], writes=[grow["sp"]])
        S.act(lambda e: e.activation(out=G["sp"], in_=G["sp"], func=AF.Ln, bias=1.0), reads=[grow["sp"]], writes=[grow["sp"]])
        S.dve(lambda e: e.tensor_tensor_scan(out=G["b"], data0=maskrow.ap[:, 0:Tt], data1=G["sp"], initial=0.0,
                                             op0=ALU.mult, op1=ALU.subtract), reads=[maskrow, grow["sp"]], writes=[grow["b"]])
        S.dve(lambda e: e.tensor_tensor(out=G["a"], in0=G["ig"], in1=G["b"], op=ALU.subtract),
              reads=[grow["ig"], grow["b"]], writes=[grow["a"]])
        a3 = G["a"].rearrange("p (c l) -> p c l", l=L)
        b3 = G["b"].rearrange("p (c l) -> p c l", l=L)
        S.dve(lambda e: e.tensor_reduce(out=Gs["amax"], in_=a3, axis=AX.X, op=ALU.max), reads=[grow["a"]], writes=[gsm["amax"]])
        S.dve(lambda e: e.memset(Gs["d0"][:, 0:1], 0.0), writes=[gsm["d0"]])
        if NCH > 1:
            S.dve(lambda e: e.tensor_copy(out=Gs["d0"][:, 1:NCH], in_=b3[:, 0:NCH - 1, L - 1]), reads=[grow["b"]],
                  writes=[gsm["d0"]])
        S.dve(lambda e: e.tensor_tensor_scan(out=Gs["M"], data0=Gs["d0"], data1=Gs["amax"], initial=mst[l].ap,
                                             op0=ALU.add, op1=ALU.max), reads=[gsm["d0"], gsm["amax"], mst[l]],
              writes=[gsm["M"]])
        S.dve(lambda e: e.tensor_tensor(out=Gs["mnew"], in0=b3[:, :, L - 1], in1=Gs["M"], op=ALU.add),
              reads=[grow["b"], gsm["M"]], writes=[gsm["mnew"]])
        S.dve(lambda e: e.tensor_copy(out=Gs["mprev"][:, 0:1], in_=mst[l].ap), reads=[mst[l]], writes=[gsm["mprev"]])
        if NCH > 1:
            S.dve(lambda e: e.tensor_copy(out=Gs["mprev"][:, 1:NCH], in_=Gs["mnew"][:, 0:NCH - 1]), reads=[gsm["mnew"]],
                  writes=[gsm["mprev"]])
        S.dve(lambda e: e.tensor_copy(out=mst[l].ap, in_=Gs["mnew"][:, NCH - 1:NCH]), reads=[gsm["mnew"], gsm["mprev"]],
              writes=[mst[l]])
        S.dve(lambda e: e.tensor_tensor(out=Gs["iw"], in0=Gs["mprev"], in1=Gs["M"], op=ALU.subtract),
              reads=[gsm["mprev"], gsm["M"]], writes=[gsm["iw"]])
        S.act(lambda e: e.activation(out=Gs["iw"], in_=Gs["iw"], func=AF.Exp), reads=[gsm["iw"]], writes=[gsm["iw"]])
        Mb = Gs["M"].unsqueeze(2).to_broadcast([4, NCH, L])
        ea3 = G["ea"].rearrange("p (c l) -> p c l", l=L)
        cl3 = G["cl"].rearrange("p (c l) -> p c l", l=L)
        iw3 = G["iwt"].rearrange("p (c l) -> p c l", l=L)
        S.dve(lambda e: e.tensor_tensor(out=ea3, in0=a3, in1=Mb, op=ALU.subtract), reads=[grow["a"], gsm["M"]],
              writes=[grow["ea"]])
        S.act(lambda e: e.activation(out=G["ea"], in_=G["ea"], func=AF.Exp), reads=[grow["ea"]], writes=[grow["ea"]])
        S.dve(lambda e: e.tensor_tensor(out=cl3, in0=b3, in1=Mb, op=ALU.add), reads=[grow["b"], gsm["M"]],
              writes=[grow["cl"]])
        S.act(lambda e: e.activation(out=G["cl"], in_=G["cl"], func=AF.Exp, scale=-1.0), reads=[grow["cl"]],
              writes=[grow["cl"]])
        S.dve(lambda e: e.tensor_copy(out=iw3, in_=Gs["iw"].unsqueeze(2).to_broadcast([4, NCH, L])), reads=[gsm["iw"]],
              writes=[grow["iwt"]])
        G2 = min(2, NCH)
        TG = G2 * L
        sgt2 = [qsq, dsum[1]]
        for half in range(2):
            sl, v = slab_k8(l, "w_in", O_MO + half * 512, 512)
            for gp in range(NCH // G2):
                pb = ps_mm()
                for k in range(8):
                    S.pe(lambda e, k=k, pb=pb, gp=gp, v=v: e.matmul(pb.ap[0:TG, :], uT.ap[:, k, gp * TG:(gp + 1) * TG], v[:, k, :],
                                                                    start=(k == 0), stop=(k == 7)),
                         reads=[sl, uT[k]], writes=[pb])
                for gi in range(G2):
                    ch = gp * G2 + gi
                    t_ = sgt2[gi]
                    S.act(lambda e, pb=pb, gi=gi, t_=t_: e.activation(out=t_.ap[0:L, :], in_=pb.ap[gi * L:(gi + 1) * L, :], func=AF.Exp,
                                                                      scale=-1.0), reads=[pb], writes=[t_])
                    S.act(lambda e, t_=t_: e.activation(out=t_.ap[0:L, :], in_=t_.ap[0:L, :], func=AF.Ln, bias=1.0), reads=[t_], writes=[t_])
                    S.act(lambda e, t_=t_: e.activation(out=t_.ap[0:L, :], in_=t_.ap[0:L, :], func=AF.Exp, scale=-1.0), reads=[t_],
                          writes=[t_])
                    S.pool(lambda e, ch=ch, half=half, t_=t_: e.tensor_tensor(out=sgm.ap[0:L, ch, half * 512:(half + 1) * 512],
                                                                              in0=t_.ap[0:L, :],
                                                                              in1=mg_row[l].ap[0:L, half * 512:(half + 1) * 512],
                                                                              op=ALU.mult),
                           reads=[t_, mg_row[l]], writes=[sgm[ch]])
        pc = aux(2)
        pcv = pc.ap[0:L, 0:NCH * 12].rearrange("p (c q) -> p c q", q=12)
        for ch in range(NCH):
            for qi, nm in enumerate(["ea", "cl", "iwt"]):
                S.pe(lambda e, ch=ch, qi=qi, nm=nm: e.transpose(pcv[:, ch, qi * 4:qi * 4 + 4], G[nm][:, ch * L:(ch + 1) * L],
                                                                 ident.ap[0:4, 0:4]),
                     reads=[grow[nm], ident], writes=[pc])
        S.dve(lambda e: e.tensor_copy(out=colq.ap[0:L, 0:NCH, :], in_=pcv), reads=[pc], writes=[colq])
        S.act(lambda e: e.activation(out=eab.ap[0:L, 0:NCH, :], in_=colq.ap[0:L, 0:NCH, 0:4], func=AF.Copy), reads=[colq],
              writes=[eab])
        S.dve(lambda e: e.tensor_tensor(out=rhsm.ap[:, :, 0:NCH], in0=Gs["iw"].unsqueeze(1).to_broadcast([4, 4, NCH]),
                                        in1=hmask.ap[:, :, 0:NCH], op=ALU.mult), reads=[gsm["iw"], hmask], writes=[rhsm])
        pw = aux(3)
        for h2 in range(4):
            S.pe(lambda e, h2=h2: e.matmul(pw.ap[:, h2 * NCH:(h2 + 1) * NCH], ones32.ap, rhsm.ap[:, h2, 0:NCH],
                                           start=True, stop=True), reads=[ones32, rhsm], writes=[pw])
        S.dve(lambda e: e.tensor_copy(out=iw_rep.ap[:, 0:4 * NCH], in_=pw.ap[:, 0:4 * NCH]), reads=[pw], writes=[iw_rep])

        for half in range(2):
            sl, v = slab_k8(l, "w_in", O_MV + half * 512, 512)
            for gp in range(NCH // G2):
                pb = ps_mm()
                for k in range(8):
                    S.pe(lambda e, k=k, pb=pb, gp=gp, v=v: e.matmul(pb.ap[0:TG, :], uT.ap[:, k, gp * TG:(gp + 1) * TG], v[:, k, :],
                                                                    start=(k == 0), stop=(k == 7)),
                         reads=[sl, uT[k]], writes=[pb])
                for gi in range(G2):
                    ch = gp * G2 + gi
                    copy_any(vw.ap[0:L, ch, half * 2:half * 2 + 2, :],
                             pb.ap[gi * L:(gi + 1) * L, :].rearrange("p (h e) -> p h e", h=2), [pb], [vw[ch]])
        for ch in range(NCH):
            S.dve(lambda e, ch=ch: e.tensor_tensor(out=vw.ap[0:L, ch], in0=vw.ap[0:L, ch],
                                                   in1=colq.ap[0:L, ch, 0:4].unsqueeze(2).to_broadcast([L, 4, 256]), op=ALU.mult),
                  reads=[vw[ch], colq], writes=[vw[ch]])
        def state_copies(h, chn):
            iwn = iw_rep.ap[:, h * NCH + chn:h * NCH + chn + 1]
            S.act(lambda e: e.activation(out=Cnb4[h].ap, in_=Cst[l].ap[:, h, :, :], func=AF.Copy, scale=iwn),
                  reads=[Cst[l][h], iw_rep], writes=[Cnb4[h]])
            S.act(lambda e: e.activation(out=nb4[h].ap, in_=nst[l].ap[:, h, :], func=AF.Copy, scale=iwn),
                  reads=[nst[l], iw_rep], writes=[nb4[h]])

        for h in range(4):
            state_copies(h, 0)

        def cbody(ch):
            cs = slice(ch * L, (ch + 1) * L)
            j = ch % 2
            ps_s = aux(0)
            sv = ps_s.ap[0:L, 0:4 * L].rearrange("p (h t) -> p h t", h=4)
            for h in range(4):
                for dc in range(2):
                    S.pe(lambda e, h=h, dc=dc, sv=sv, cs=cs: e.matmul(sv[:, h, :], kT.ap[:, 2 * h + dc, cs],
                                                                      qT.ap[:, 2 * h + dc, cs], start=(dc == 0), stop=(dc == 1)),
                         reads=[kT[2 * h + dc], qT[2 * h + dc]], writes=[ps_s])
            sm = smask[j]
            S.dve(lambda e, sm=sm, sv=sv: e.tensor_tensor(out=sm.ap[0:L, :, 0:L], in0=sv,
                                                          in1=cmask.ap[0:L, 0:L].unsqueeze(1).to_broadcast([L, 4, L]),
                                                          op=ALU.mult), reads=[ps_s, cmask], writes=[sm])
            yield
            po = [aux(1), aux(2)]
            pd = aux(3)
            for h in range(4):
                cb, nb_ = Cnb4[h], nb4[h]
                iwc = iw_rep.ap[:, h * NCH + ch:h * NCH + ch + 1]
                pov = po[h // 2].ap[0:L, (h % 2) * 256:(h % 2 + 1) * 256]
                S.pe(lambda e, pov=pov, sm=sm, h=h, ch=ch: e.matmul(pov, sm.ap[0:L, h, 0:L], vw.ap[0:L, ch, h, :],
                                                                    start=True, stop=False),
                     reads=[sm, vw[ch]], writes=[po[h // 2]])
                for dc in range(2):
                    S.pe(lambda e, pov=pov, h=h, dc=dc, cs=cs, cb=cb: e.matmul(pov, qT.ap[:, 2 * h + dc, cs], cb.ap[:, dc, :],
                                                                               start=False, stop=(dc == 1)),
                         reads=[qT[2 * h + dc], cb], writes=[po[h // 2]])
                pdv = pd.ap[0:L, h:h + 1]
                S.pe(lambda e, pdv=pdv, sm=sm, h=h, ch=ch: e.matmul(pdv, sm.ap[0:L, h, 0:L], eab.ap[0:L, ch, h:h + 1],
                                                                    start=True, stop=False),
                     reads=[sm, eab], writes=[pd])
                for dc in range(2):
                    S.pe(lambda e, pdv=pdv, h=h, dc=dc, cs=cs, nb_=nb_: e.matmul(pdv, qT.ap[:, 2 * h + dc, cs],
                                                                                 nb_.ap[:, dc:dc + 1], start=False,
                                                                                 stop=(dc == 1)),
                         reads=[qT[2 * h + dc], nb_], writes=[pd])
                pdl = aux(4)
                for dc in range(2):
                    S.pe(lambda e, pdl=pdl, h=h, dc=dc, ch=ch: e.matmul(pdl.ap[:, dc * 256:(dc + 1) * 256],
                                                                        ktok.ap[0:L, ch, h * 256 + dc * 128:h * 256 + dc * 128 + 128],
                                                                        vw.ap[0:L, ch, h, :], start=True, stop=True),
                         reads=[ktok[ch], vw[ch]], writes=[pdl])
                    S.pe(lambda e, h=h, dc=dc, ch=ch: e.matmul(banks[h % 2].ap[:, dc:dc + 1],
                                                                      ktok.ap[0:L, ch, h * 256 + dc * 128:h * 256 + dc * 128 + 128],
                                                                      eab.ap[0:L, ch, h:h + 1], start=True, stop=True),
                         reads=[ktok[ch], eab], writes=[banks[h % 2]])
                S.dve(lambda e, pdl=pdl, h=h, iwc=iwc: e.scalar_tensor_tensor(
                    out=Cst[l].ap[:, h, :, :].rearrange("p a b -> p (a b)"), in0=Cst[l].ap[:, h, :, :].rearrange("p a b -> p (a b)"),
                    scalar=iwc, in1=pdl.ap[:, 0:512], op0=ALU.mult, op1=ALU.add),
                      reads=[Cst[l][h], pdl, iw_rep], writes=[Cst[l][h]])
                S.dve(lambda e, pd=pd, h=h, iwc=iwc: e.scalar_tensor_tensor(
                    out=nst[l].ap[:, h, :], in0=nst[l].ap[:, h, :], scalar=iwc, in1=banks[h % 2].ap[:, 0:2],
                    op0=ALU.mult, op1=ALU.add), reads=[nst[l], banks[h % 2], iw_rep], writes=[nst[l]])
                if ch + 1 < NCH:
                    state_copies(h, ch + 1)
            yield
            ds_, dt_, hn_, ss_ = den_s[j], den_t[j], hn[j], ssm[j]
            S.act(lambda e, ds_=ds_, pd=pd: e.activation(out=ds_.ap[0:L, :], in_=pd.ap[0:L, 0:4], func=AF.Copy), reads=[pd],
                  writes=[ds_])
            S.dve(lambda e, ds_=ds_, dt_=dt_: e.scalar_tensor_tensor(out=dt_.ap[0:L, :], in0=ds_.ap[0:L, :], scalar=-1.0,
                                                                     in1=ds_.ap[0:L, :], op0=ALU.mult, op1=ALU.max),
                  reads=[ds_], writes=[dt_])
            S.dve(lambda e, dt_=dt_, ch=ch: e.tensor_tensor(out=dt_.ap[0:L, :], in0=dt_.ap[0:L, :], in1=colq.ap[0:L, ch, 4:8],
                                                            op=ALU.max), reads=[dt_, colq], writes=[dt_])
            S.dve(lambda e, dt_=dt_: e.reciprocal(out=dt_.ap[0:L, :], in_=dt_.ap[0:L, :]), reads=[dt_], writes=[dt_])
            for hp in range(2):
                S.dve(lambda e, hp=hp, hn_=hn_, dt_=dt_: e.tensor_tensor(
                    out=hn_.ap[0:L, 2 * hp:2 * hp + 2, :], in0=po[hp].ap[0:L, :].rearrange("p (h d) -> p h d", h=2),
                    in1=dt_.ap[0:L, 2 * hp:2 * hp + 2].unsqueeze(2).to_broadcast([L, 2, 256]), op=ALU.mult),
                      reads=[po[hp], dt_], writes=[hn_])
            S.act(lambda e, hn_=hn_: e.activation(out=hsq.ap[0:L], in_=hn_.ap[0:L], func=AF.Square), reads=[hn_], writes=[hsq])
            S.dve(lambda e, ss_=ss_: e.tensor_reduce(out=ss_.ap[0:L, :], in_=hsq.ap[0:L], axis=AX.X, op=ALU.add), reads=[hsq],
                  writes=[ss_])
            S.act(lambda e, ss_=ss_: e.activation(out=ss_.ap[0:L, :], in_=ss_.ap[0:L, :], func=AF.Sqrt, scale=1.0 / 256, bias=EPS),
                  reads=[ss_], writes=[ss_])
            S.dve(lambda e, ss_=ss_: e.reciprocal(out=ss_.ap[0:L, :], in_=ss_.ap[0:L, :]), reads=[ss_], writes=[ss_])
            S.pool(lambda e, hn_=hn_, ss_=ss_: e.tensor_tensor(out=hn_.ap[0:L], in0=hn_.ap[0:L],
                                                               in1=ss_.ap[0:L, :].unsqueeze(2).to_broadcast([L, 4, 256]),
                                                               op=ALU.mult), reads=[hn_, ss_], writes=[hn_])
            S.pool(lambda e, hn_=hn_, ch=ch: e.tensor_tensor(out=hmtok.ap[0:L, ch, :], in0=hn_.ap[0:L].rearrange("p h d -> p (h d)"),
                                                             in1=sgm.ap[0:L, ch, :], op=ALU.mult),
                   reads=[hn_, sgm[ch]], writes=[hmtok[ch]])
            yield
            pt = aux(5)
            ptv = pt.ap.bitcast(BF16)[:, 0:8 * L].rearrange("p (c t) -> p c t", c=8)
            for c in range(8):
                S.pe(lambda e, c=c, ptv=ptv, ch=ch: e.transpose(ptv[:, c, :], hmtok.ap[0:L, ch, c * 128:(c + 1) * 128],
                                                                 identb.ap[0:L, 0:L]),
                     reads=[hmtok[ch], identb], writes=[pt])
            copy_any(hmT.ap[:, :, cs], ptv, [pt], [hmT])

        pipeline([cbody(ch) for ch in range(NCH)], newest_first=False)
        merge_branch(l, 1, hmT, Tt)

    def attn_phase(l, Tt, L, first_tile, is_sample, last_tile, grp):
        NCH = Tt // L
        Tb = min(Tt, 128)
        NB = Tt // Tb
        slkv, vkv = slab_k8(l, "w_in", O_AK, 256)

        def kbody(ch):
            j = ch % 2
            slot = 2 + ch
            pb = ps_mm()
            for k in range(8):
                S.pe(lambda e, k=k, pb=pb, ch=ch: e.matmul(pb.ap[0:L, 0:256], uT.ap[:, k, ch * L:(ch + 1) * L], vkv[:, k, :],
                                                           start=(k == 0), stop=(k == 7)), reads=[slkv, uT[k]], writes=[pb])
            vf_ = vf[j]
            S.act(lambda e, pb=pb, vf_=vf_: e.activation(out=vf_.ap[0:L, :], in_=pb.ap[0:L, 128:256], func=AF.Copy), reads=[pb],
                  writes=[vf_])
            S.dve(lambda e, vf_=vf_, slot=slot: e.tensor_copy(out=vaug[l].ap[0:L, slot, :, 0:64],
                                                              in_=vf_.ap[0:L, :].rearrange("p (k d) -> p k d", k=2)),
                  reads=[vf_], writes=[vaug[l]])
            S.dve(lambda e, vf_=vf_, slot=slot: e.tensor_copy(out=vaug[l].ap[0:L, slot, :, 128:192],
                                                              in_=vf_.ap[0:L, :].rearrange("p (k d) -> p k d", k=2)),
                  reads=[vf_], writes=[vaug[l]])
            ks_, kn_, kr_, krb_ = kss[j], kn[j], kr[j], krb[j]
            S.act(lambda e, pb=pb: e.activation(out=ksq.ap[0:L, :], in_=pb.ap[0:L, 0:128], func=AF.Square), reads=[pb], writes=[ksq])
            S.dve(lambda e, ks_=ks_: e.tensor_reduce(out=ks_.ap[0:L, :], in_=ksq.ap[0:L, :].rearrange("p (k d) -> p k d", k=2),
                                                     axis=AX.X, op=ALU.add), reads=[ksq], writes=[ks_])
            S.act(lambda e, ks_=ks_: e.activation(out=ks_.ap[0:L, :], in_=ks_.ap[0:L, :], func=AF.Sqrt, scale=1.0 / 64, bias=EPS),
                  reads=[ks_], writes=[ks_])
            S.dve(lambda e, ks_=ks_: e.reciprocal(out=ks_.ap[0:L, :], in_=ks_.ap[0:L, :]), reads=[ks_], writes=[ks_])
            for kv in range(2):
                S.dve(lambda e, kv=kv, pb=pb, ks_=ks_, kn_=kn_: e.scalar_tensor_tensor(
                    out=kn_.ap[0:L, kv, :], in0=pb.ap[0:L, kv * 64:(kv + 1) * 64], scalar=ks_.ap[0:L, kv:kv + 1],
                    in1=kg_row[l].ap[0:L, :], op0=ALU.mult, op1=ALU.mult), reads=[pb, ks_, kg_row[l]], writes=[kn_])
            cosb = ctab_k.ap[0:L, ch, :].unsqueeze(1).to_broadcast([L, 2, 32])
            sinb = stab_k.ap[0:L, ch, :].unsqueeze(1).to_broadcast([L, 2, 32])
            k1 = kn_.ap[0:L, :, 0:32]
            k2 = kn_.ap[0:L, :, 32:64]
            krv = kr_.ap[0:L, :].rearrange("p (k d) -> p k d", k=2)
            S.dve(lambda e, k1=k1, cosb=cosb: e.tensor_tensor(out=kt1.ap[0:L], in0=k1, in1=cosb, op=ALU.mult),
                  reads=[kn_, ctab_k], writes=[kt1])
            S.dve(lambda e, k2=k2, sinb=sinb: e.tensor_tensor(out=kt2.ap[0:L], in0=k2, in1=sinb, op=ALU.mult),
                  reads=[kn_, stab_k], writes=[kt2])
            S.dve(lambda e, krv=krv: e.tensor_tensor(out=krv[:, :, 0:32], in0=kt1.ap[0:L], in1=kt2.ap[0:L], op=ALU.subtract),
                  reads=[kt1, kt2], writes=[kr_])
            S.dve(lambda e, k2=k2, cosb=cosb: e.tensor_tensor(out=kt1.ap[0:L], in0=k2, in1=cosb, op=ALU.mult),
                  reads=[kn_, ctab_k], writes=[kt1])
            S.dve(lambda e, k1=k1, sinb=sinb: e.tensor_tensor(out=kt2.ap[0:L], in0=k1, in1=sinb, op=ALU.mult),
                  reads=[kn_, stab_k], writes=[kt2])
            S.dve(lambda e, krv=krv: e.tensor_tensor(out=krv[:, :, 32:64], in0=kt1.ap[0:L], in1=kt2.ap[0:L], op=ALU.add),
                  reads=[kt1, kt2], writes=[kr_])
            S.act(lambda e, kr_=kr_, krb_=krb_: e.activation(out=krb_.ap[0:L, :], in_=kr_.ap[0:L, :], func=AF.Copy),
                  reads=[kr_], writes=[krb_])
            yield
            pk = aux(0)
            pkv = pk.ap.bitcast(BF16)[0:64, 0:2 * L].rearrange("p (k t) -> p k t", k=2)
            for kv in range(2):
                S.pe(lambda e, kv=kv, pkv=pkv, krb_=krb_: e.transpose(pkv[:, kv, :], krb_.ap[0:L, kv * 64:(kv + 1) * 64],
                                                                      identb.ap[0:L, 0:L]), reads=[krb_, identb], writes=[pk])
            S.dve(lambda e, pkv=pkv, slot=slot: e.tensor_copy(out=KTwin[l].ap[:, :, slot * 64:slot * 64 + L], in_=pkv),
                  reads=[pk], writes=[KTwin[l]])
            if is_sample:
                S.dma("sp", lambda e, kr_=kr_: e.dma_start(out=O["k_s"][l, 128 - L:128].rearrange("r k d -> r (k d)"),
                                                           in_=kr_.ap[0:L, :]), kr_, reads=[kr_])
                S.dma("sp", lambda e, vf_=vf_: e.dma_start(out=O["v_s"][l, 128 - L:128].rearrange("r k d -> r (k d)"),
                                                           in_=vf_.ap[0:L, :]), vf_, reads=[vf_])
            elif last_tile and ch >= NCH - 2:
                r0 = (ch - (NCH - 2)) * 64
                S.dma("sp", lambda e, kr_=kr_, r0=r0: e.dma_start(out=O["k_p"][l, r0:r0 + 64].rearrange("r k d -> r (k d)"),
                                                                  in_=kr_.ap[0:L, :]), kr_, reads=[kr_])
                S.dma("sp", lambda e, vf_=vf_, r0=r0: e.dma_start(out=O["v_p"][l, r0:r0 + 64].rearrange("r k d -> r (k d)"),
                                                                  in_=vf_.ap[0:L, :]), vf_, reads=[vf_])

        pipeline([kbody(ch) for ch in range(NCH)], newest_first=True)
        qsl = {}

        def qbody(half, b):
            if True:
                if b == 0:
                    qsl["q"] = slab_k8(l, "w_in", O_AQ + half * 512, 512)
                slq, vq = qsl["q"]
                j = (half * NB + b) % 2
                pb = ps_mm()
                for k in range(8):
                    S.pe(lambda e, k=k, pb=pb, b=b, vq=vq: e.matmul(pb.ap[0:Tb, :], uT.ap[:, k, b * Tb:(b + 1) * Tb], vq[:, k, :],
                                                                    start=(k == 0), stop=(k == 7)), reads=[slq, uT[k]], writes=[pb])
                qs_, qn_, qr_ = qss[j], qn[j], qr[j]
                S.act(lambda e, pb=pb: e.activation(out=qsq.ap[0:Tb, :], in_=pb.ap[0:Tb, :], func=AF.Square), reads=[pb], writes=[qsq])
                S.dve(lambda e, qs_=qs_: e.tensor_reduce(out=qs_.ap[0:Tb, :], in_=qsq.ap[0:Tb, :].rearrange("p (h d) -> p h d", h=8),
                                                         axis=AX.X, op=ALU.add), reads=[qsq], writes=[qs_])
                S.act(lambda e, qs_=qs_: e.activation(out=qs_.ap[0:Tb, :], in_=qs_.ap[0:Tb, :], func=AF.Sqrt, scale=1.0 / 64, bias=EPS),
                      reads=[qs_], writes=[qs_])
                S.dve(lambda e, qs_=qs_: e.reciprocal(out=qs_.ap[0:Tb, :], in_=qs_.ap[0:Tb, :]), reads=[qs_], writes=[qs_])
                S.dve(lambda e, pb=pb, qs_=qs_, qn_=qn_: e.tensor_tensor(
                    out=qn_.ap[0:Tb], in0=pb.ap[0:Tb, :].rearrange("p (h d) -> p h d", h=8),
                    in1=qs_.ap[0:Tb, :].unsqueeze(2).to_broadcast([Tb, 8, 64]), op=ALU.mult), reads=[pb, qs_], writes=[qn_])
                S.dve(lambda e, qn_=qn_: e.tensor_tensor(out=qn_.ap[0:Tb], in0=qn_.ap[0:Tb],
                                                         in1=qg_row[l].ap[0:Tb, :].unsqueeze(1).to_broadcast([Tb, 8, 64]),
                                                         op=ALU.mult), reads=[qn_, qg_row[l]], writes=[qn_])
                cosb = ctab_q.ap[0:Tb, b, :].unsqueeze(1).to_broadcast([Tb, 8, 32])
                sinb = stab_q.ap[0:Tb, b, :].unsqueeze(1).to_broadcast([Tb, 8, 32])
                q1 = qn_.ap[0:Tb, :, 0:32]
                q2 = qn_.ap[0:Tb, :, 32:64]
                S.dve(lambda e, q1=q1, cosb=cosb: e.tensor_tensor(out=qt1.ap[0:Tb], in0=q1, in1=cosb, op=ALU.mult),
                      reads=[qn_, ctab_q], writes=[qt1])
                S.dve(lambda e, q2=q2, sinb=sinb: e.tensor_tensor(out=qt2.ap[0:Tb], in0=q2, in1=sinb, op=ALU.mult),
                      reads=[qn_, stab_q], writes=[qt2])
                S.dve(lambda e, qr_=qr_: e.tensor_tensor(out=qr_.ap[0:Tb, :, 0:32], in0=qt1.ap[0:Tb], in1=qt2.ap[0:Tb],
                                                         op=ALU.subtract), reads=[qt1, qt2], writes=[qr_])
                S.dve(lambda e, q2=q2, cosb=cosb: e.tensor_tensor(out=qt1.ap[0:Tb], in0=q2, in1=cosb, op=ALU.mult),
                      reads=[qn_, ctab_q], writes=[qt1])
                S.dve(lambda e, q1=q1, sinb=sinb: e.tensor_tensor(out=qt2.ap[0:Tb], in0=q1, in1=sinb, op=ALU.mult),
                      reads=[qn_, stab_q], writes=[qt2])
                S.dve(lambda e, qr_=qr_: e.tensor_tensor(out=qr_.ap[0:Tb, :, 32:64], in0=qt1.ap[0:Tb], in1=qt2.ap[0:Tb],
                                                         op=ALU.add), reads=[qt1, qt2], writes=[qr_])
                yield
                pq = aux(1)
                pqv = pq.ap.bitcast(BF16)[0:64, 0:8 * Tb].rearrange("p (h t) -> p h t", h=8)
                for h in range(8):
                    S.pe(lambda e, h=h, pqv=pqv, qr_=qr_: e.transpose(pqv[:, h, :], qr_.ap[0:Tb, h, :], identb.ap[0:Tb, 0:Tb]),
                         reads=[qr_, identb], writes=[pq])
                copy_any(QT_all.ap[:, half * 8:(half + 1) * 8, b * Tb:(b + 1) * Tb], pqv, [pq],
                         [QT_all[half * 8 + h] for h in range(8)])

        pipeline([qbody(half, b) for half in range(2) for b in range(NB)], newest_first=True)

        def abody(ch, kv):
            slots = [(ch, 64), (ch + 1, 64), (ch + 2, L)]
            if first_tile and not is_sample:
                slots = [(s, n) for (s, n) in slots if s >= 2]
            qs = slice(ch * L, (ch + 1) * L)
            if True:
                ppv = aux(kv)
                pts = []
                for si_, (s, nk) in enumerate(slots):
                    pss = aux(2 + si_)
                    S.pe(lambda e, pss=pss, s=s, nk=nk, kv=kv, qs=qs: e.matmul(
                        pss.ap[0:nk, 0:8 * L].rearrange("p (g t) -> p g t", g=8), KTwin[l].ap[:, kv, s * 64:s * 64 + nk],
                        QT_all.ap[:, kv * 8:(kv + 1) * 8, qs], start=True, stop=True),
                         reads=[KTwin[l]] + [QT_all[kv * 8 + g] for g in range(8)], writes=[pss])
                    p_ = pT[pT_i[0] % 6]
                    pT_i[0] += 1
                    S.act(lambda e, p_=p_, pss=pss, nk=nk: e.activation(out=p_.ap[0:nk, 0:8 * L], in_=pss.ap[0:nk, 0:8 * L],
                                                                        func=AF.Exp, scale=0.125), reads=[pss], writes=[p_])
                    pts.append((p_, s, nk))
                yield
                H4 = 4 * L
                for par in range(2):
                    for i, (p_, s, nk) in enumerate(pts):
                        pv4 = p_.ap[0:nk, 0:8 * L].rearrange("p (g two t) -> p g two t", two=2, t=L)
                        S.pe(lambda e, pv4=pv4, s=s, nk=nk, i=i, par=par: e.matmul(
                            ppv.ap[:, par * H4:(par + 1) * H4].rearrange("p (g t) -> p g t", g=4),
                            vaug[l].ap[0:nk, s, kv, par * 64:par * 64 + 128], pv4[:, :, par, :],
                            start=(i == 0), stop=(i == len(pts) - 1)), reads=[vaug[l], p_], writes=[ppv])
                ds_ = dsum[kv]
                es4 = esink[l].ap[:, kv * 8:(kv + 1) * 8].rearrange("p (g two) -> p g two", two=2)
                S.dve(lambda e: e.tensor_tensor(out=ds_.ap[64:128, 0:H4].rearrange("p (g t) -> p g t", g=4),
                                                in0=ppv.ap[64:128, 0:H4].rearrange("p (g t) -> p g t", g=4),
                                                in1=es4[64:128, :, 0].unsqueeze(2).to_broadcast([64, 4, L]), op=ALU.add),
                      reads=[ppv, esink[l]], writes=[ds_])
                S.act(lambda e: e.activation(out=ds_.ap[0:64, 0:H4], in_=ds_.ap[64:128, 0:H4], func=AF.Ln), reads=[ds_], writes=[ds_])
                S.act(lambda e: e.activation(out=ds_.ap[0:64, 0:H4], in_=ds_.ap[0:64, 0:H4], func=AF.Exp, scale=-1.0), reads=[ds_],
                      writes=[ds_])
                S.dve(lambda e: e.tensor_tensor(out=OT2.ap[0:64, kv * 4:(kv + 1) * 4, qs],
                                                in0=ppv.ap[0:64, 0:H4].rearrange("p (g t) -> p g t", g=4),
                                                in1=ds_.ap[0:64, 0:H4].rearrange("p (g t) -> p g t", g=4), op=ALU.mult),
                      reads=[ppv, ds_], writes=[OT2[kv * 4 + g] for g in range(4)])
                S.dve(lambda e: e.tensor_tensor(out=ds_.ap[0:64, H4:2 * H4].rearrange("p (g t) -> p g t", g=4),
                                                in0=ppv.ap[0:64, H4:2 * H4].rearrange("p (g t) -> p g t", g=4),
                                                in1=es4[0:64, :, 1].unsqueeze(2).to_broadcast([64, 4, L]), op=ALU.add),
                      reads=[ppv, esink[l]], writes=[ds_])
                S.act(lambda e: e.activation(out=ds_.ap[64:128, H4:2 * H4], in_=ds_.ap[0:64, H4:2 * H4], func=AF.Ln), reads=[ds_],
                      writes=[ds_])
                S.act(lambda e: e.activation(out=ds_.ap[64:128, H4:2 * H4], in_=ds_.ap[64:128, H4:2 * H4], func=AF.Exp, scale=-1.0),
                      reads=[ds_], writes=[ds_])
                S.dve(lambda e: e.tensor_tensor(out=OT2.ap[64:128, kv * 4:(kv + 1) * 4, qs],
                                                in0=ppv.ap[64:128, H4:2 * H4].rearrange("p (g t) -> p g t", g=4),
                                                in1=ds_.ap[64:128, H4:2 * H4].rearrange("p (g t) -> p g t", g=4), op=ALU.mult),
                      reads=[ppv, ds_], writes=[OT2[kv * 4 + g] for g in range(4)])

        pipeline([abody(ch, kv) for ch in range(NCH) for kv in range(2)], newest_first=True)
        if not is_sample and not last_tile:
            for i in range(2):
                S.dve(lambda e, i=i: e.tensor_copy(out=KTwin[l].ap[:, :, i * 64:(i + 1) * 64],
                                                   in_=KTwin[l].ap[:, :, (NCH + i) * 64:(NCH + i + 1) * 64]),
                      reads=[KTwin[l]], writes=[KTwin[l]])
                S.pool(lambda e, i=i: e.tensor_copy(out=vaug[l].ap[:, i, :, 0:64], in_=vaug[l].ap[:, NCH + i, :, 0:64]),
                       reads=[vaug[l]], writes=[vaug[l]])
                S.pool(lambda e, i=i: e.tensor_copy(out=vaug[l].ap[:, i, :, 128:192], in_=vaug[l].ap[:, NCH + i, :, 128:192]),
                       reads=[vaug[l]], writes=[vaug[l]])
        merge_branch(l, 2, OT2, Tt)

    def out_and_mlp(l, Tt):
        for c in range(8):
            S.act(lambda e, c=c: e.activation(out=uT.ap[:, c, 0:Tt], in_=mix.ap[:, c, 0:Tt], func=AF.Copy), reads=[mix[c]],
                  writes=[uT[c]])
        for half in range(2):
            sl, v = slab_k8(l, "w_out", half * 512, 512)
            for c4 in range(4):
                c = half * 4 + c4
                pb = ps_mm()
                fm_proj(pb, sl, v, c4 * 128, uT, Tt)
                S.dve(lambda e, pb=pb, c=c: e.tensor_tensor(out=xT.ap[:, c, 0:Tt], in0=pb.ap[:, 0:Tt], in1=xT.ap[:, c, 0:Tt],
                                                            op=ALU.add), reads=[pb, xT[c]], writes=[xT[c]])
        norm_to_u(l, "norm2_g", Tt)
        for s8 in range(8):
            sl, v = slab_k8(l, "w_up", s8 * 512, 512)
            for c4 in range(4):
                hc = s8 * 4 + c4
                pb = ps_mm()
                fm_proj(pb, sl, v, c4 * 128, uT, Tt)
                r_ = rl[hc % 2]
                S.act(lambda e, pb=pb, r_=r_: e.activation(out=r_.ap[:, 0:Tt], in_=pb.ap[:, 0:Tt], func=AF.Relu), reads=[pb],
                      writes=[r_])
                hp_ = hid_parts[hc // 8]
                S.dve(lambda e, r_=r_, hc=hc, hp_=hp_: e.tensor_tensor(out=hp_.ap[:, hc % 8, 0:Tt], in0=r_.ap[:, 0:Tt],
                                                                       in1=r_.ap[:, 0:Tt], op=ALU.mult),
                      reads=[r_], writes=[hp_[hc % 8]])
        for c in range(8):
            src = SCR["w_down"][l][:, c * 128:(c + 1) * 128].rearrange("(kc p) n -> p kc n", p=128)
            sl, v = load_slab(src, lambda a: a.rearrange("p (k n) -> p k n", k=32), SCRB[("w_down", l)])
            pb = ps_mm()
            for k in range(32):
                hp_ = hid_parts[k // 8]
                S.pe(lambda e, k=k, pb=pb, v=v, hp_=hp_: e.matmul(pb.ap[:, 0:Tt], v[:, k, :], hp_.ap[:, k % 8, 0:Tt],
                                                                  start=(k == 0), stop=(k == 31)),
                     reads=[sl, hp_[k % 8]], writes=[pb])
            S.dve(lambda e, pb=pb, c=c: e.tensor_tensor(out=xT.ap[:, c, 0:Tt], in0=pb.ap[:, 0:Tt], in1=xT.ap[:, c, 0:Tt],
                                                        op=ALU.add), reads=[pb, xT[c]], writes=[xT[c]])

    def load_x(src, Tt):
        Tb = min(Tt, 128)
        for b in range(Tt // Tb):
            xi = xin[b % 2]
            S.dma("sp", lambda e, xi=xi, b=b: e.dma_start(out=xi.ap[0:Tb, :], in_=src[b * Tb:(b + 1) * Tb, :]), xi, writes=[xi])
            for g4 in range(2):
                pb = aux(g4)
                for c4 in range(4):
                    c = g4 * 4 + c4
                    S.pe(lambda e, pb=pb, c=c, c4=c4, xi=xi: e.transpose(pb.ap[:, c4 * Tb:(c4 + 1) * Tb],
                                                                         xi.ap[0:Tb, c * 128:(c + 1) * 128], ident.ap[0:Tb, 0:Tb]),
                         reads=[xi, ident], writes=[pb])
                copy_any(xT.ap[:, g4 * 4:(g4 + 1) * 4, b * Tb:(b + 1) * Tb],
                         pb.ap[:, 0:4 * Tb].rearrange("p (c t) -> p c t", c=4), [pb], [xT[g4 * 4 + i] for i in range(4)])

    def store_y(dst, Tt):
        Tb = min(Tt, 128)
        for b in range(Tt // Tb):
            yo = yout[b % 2]
            for g4 in range(2):
                pb = aux(g4)
                for c4 in range(4):
                    c = g4 * 4 + c4
                    S.pe(lambda e, pb=pb, c=c, c4=c4, b=b: e.transpose(pb.ap[0:Tb, c4 * 128:(c4 + 1) * 128],
                                                                       xT.ap[:, c, b * Tb:(b + 1) * Tb], ident.ap),
                         reads=[xT[c], ident], writes=[pb])
                copy_any(yo.ap[0:Tb, g4 * 512:(g4 + 1) * 512], pb.ap[0:Tb, :], [pb], [yo])
            S.dma("sp", lambda e, yo=yo, b=b: e.dma_start(out=dst[b * Tb:(b + 1) * Tb, :], in_=yo.ap[0:Tb, :]), yo, reads=[yo])

    def load_rope(pos0, Tt, L):
        NCH = Tt // L
        Tb = min(Tt, 128)
        NB = Tt // Tb
        for (tab, src) in [(ctab_k, rope_c), (stab_k, rope_s)]:
            S.dma("sp", lambda e, tab=tab, src=src: e.dma_start(
                out=tab.ap[0:L, 0:NCH, :], in_=src[pos0:pos0 + Tt, :].rearrange("(c l) f -> l c f", l=L)), tab, writes=[tab])
        for (tab, src) in [(ctab_q, rope_c), (stab_q, rope_s)]:
            S.dma("sp", lambda e, tab=tab, src=src: e.dma_start(
                out=tab.ap[0:Tb, 0:NB, :], in_=src[pos0:pos0 + Tt, :].rearrange("(c l) f -> l c f", l=Tb)), tab, writes=[tab])

    def cols_to_rows_store(src_ap, n, dsts, rd):
        pb = aux(5)
        S.pe(lambda e: e.transpose(pb.ap[0:n, 0:128], src_ap, ident.ap), reads=rd + [ident], writes=[pb])
        S.dve(lambda e: e.tensor_copy(out=vstage.ap[0:n, :], in_=pb.ap[0:n, 0:128]), reads=[pb], writes=[vstage])
        for (r0, r1, d) in dsts:
            S.dma("sp", lambda e, r0=r0, r1=r1, d=d: e.dma_start(out=d, in_=vstage.ap[r0:r1, :]), vstage, reads=[vstage])

    def rows_load_to_cols(srcs, n, dst_ap, wr):
        for (r0, r1, s_) in srcs:
            S.dma("sp", lambda e, r0=r0, r1=r1, s_=s_: e.dma_start(out=vstage.ap[r0:r1, :], in_=s_), vstage, writes=[vstage])
        pb = aux(5)
        S.pe(lambda e: e.transpose(pb.ap[:, 0:n], vstage.ap[0:n, :], ident.ap[0:n, 0:n]), reads=[vstage, ident], writes=[pb])
        S.dve(lambda e: e.tensor_copy(out=dst_ap, in_=pb.ap[:, 0:n]), reads=[pb], writes=wr)

    def init_states_zero(l):
        S.dve(lambda e: e.memset(hist[l].ap, 0.0), writes=[hist[l]])
        S.dve(lambda e: e.memset(hst[l].ap, 0.0), writes=[hst[l]])
        S.pool(lambda e: e.memset(Cst[l].ap, 0.0), writes=[Cst[l]])
        S.dve(lambda e: e.memset(nst[l].ap, 0.0), writes=[nst[l]])
        S.dve(lambda e: e.memset(mst[l].ap, 0.0), writes=[mst[l]])
        S.pool(lambda e: e.memset(vaug[l].ap, 1.0), writes=[vaug[l]])
        S.pool(lambda e: e.memset(KTwin[l].ap, 0.0), writes=[KTwin[l]])

    def init_states_sample(l):
        for (r0, r1, s_) in [(0, 24, st_conv[l].rearrange("j (c p) -> (j c) p", p=128)),
                             (24, 32, st_lru[l].rearrange("(c p) -> c p", p=128)),
                             (32, 40, st_n[l].rearrange("h (c p) -> (h c) p", p=128))]:
            S.dma("sp", lambda e, r0=r0, r1=r1, s_=s_: e.dma_start(out=vstage.ap[r0:r1, :], in_=s_), vstage, writes=[vstage])
        pb = aux(5)
        S.pe(lambda e: e.transpose(pb.ap[:, 0:40], vstage.ap[0:40, :], ident.ap[0:40, 0:40]), reads=[vstage, ident], writes=[pb])
        S.dve(lambda e: e.tensor_copy(out=hist[l].ap.rearrange("p j c -> p (j c)"), in_=pb.ap[:, 0:24]), reads=[pb], writes=[hist[l]])
        S.dve(lambda e: e.tensor_copy(out=hst[l].ap, in_=pb.ap[:, 24:32]), reads=[pb], writes=[hst[l]])
        S.dve(lambda e: e.tensor_copy(out=nst[l].ap.rearrange("p h c -> p (h c)"), in_=pb.ap[:, 32:40]), reads=[pb], writes=[nst[l]])
        S.dma("sp", lambda e: e.dma_start(out=Cst[l].ap, in_=st_C[l].rearrange("h (c p) e -> p h c e", p=128)), Cst[l],
              writes=[Cst[l]])
        S.dma("sp", lambda e: e.dma_start(out=mst[l].ap, in_=st_m[l].rearrange("(h o) -> h o", o=1)), mst[l], writes=[mst[l]])
        S.pool(lambda e: e.memset(vaug[l].ap, 1.0), writes=[vaug[l]])
        for s2 in range(2):
            S.dma("pool", lambda e, s2=s2: e.dma_start(out=vaug[l].ap[:, s2, :, 0:64], in_=c_v[l, s2 * 64:(s2 + 1) * 64]),
                  vaug[l], writes=[vaug[l]])
        S.dve(lambda e: e.tensor_copy(out=vaug[l].ap[:, 0:2, :, 128:192], in_=vaug[l].ap[:, 0:2, :, 0:64]), reads=[vaug[l]],
              writes=[vaug[l]])
        S.dma("sp", lambda e: e.dma_start(out=kcs.ap, in_=c_k[l].rearrange("(s r) k d -> r s (k d)", s=2)), kcs, writes=[kcs])
        S.dve(lambda e: e.tensor_copy(out=kcb.ap, in_=kcs.ap), reads=[kcs], writes=[kcb])
        pk = aux(4)
        pkv = pk.ap.bitcast(BF16)[0:64, 0:256].rearrange("p (k t) -> p k t", k=2)
        for s in range(2):
            for kv in range(2):
                S.pe(lambda e, s=s, kv=kv: e.transpose(pkv[:, kv, s * 64:(s + 1) * 64], kcb.ap[:, s, kv * 64:(kv + 1) * 64],
                                                       identb.ap[0:64, 0:64]), reads=[kcb, identb], writes=[pk])
        S.dve(lambda e: e.tensor_copy(out=KTwin[l].ap[:, :, 0:128], in_=pkv), reads=[pk], writes=[KTwin[l]])
        for (o_, c_) in [(O["k_s"], c_k), (O["v_s"], c_v)]:
            S.dma("sp", lambda e, o_=o_, c_=c_: e.dma_start(out=tailk.ap[0:64, :], in_=c_[l, TS:TS + 64].rearrange("r k d -> r (k d)")),
                  tailk, writes=[tailk])
            S.dma("sp", lambda e, o_=o_: e.dma_start(out=o_[l, 0:64].rearrange("r k d -> r (k d)"), in_=tailk.ap[0:64, :]),
                  tailk, reads=[tailk])
            n2 = 128 - TS - 64
            S.dma("sp", lambda e, o_=o_, c_=c_: e.dma_start(out=tailk.ap[0:n2, :],
                                                            in_=c_[l, TS + 64:128].rearrange("r k d -> r (k d)")),
                  tailk, writes=[tailk])
            S.dma("sp", lambda e, o_=o_: e.dma_start(out=o_[l, 64:64 + n2].rearrange("r k d -> r (k d)"), in_=tailk.ap[0:n2, :]),
                  tailk, reads=[tailk])

    def store_states(l, g):
        cols_to_rows_store(hist[l].ap.rearrange("p j c -> p (j c)"), 24,
                           [(0, 24, O["conv_" + g][l].rearrange("j (c p) -> (j c) p", p=128))], [hist[l]])
        cols_to_rows_store(hst[l].ap, 8, [(0, 8, O["lru_" + g][l].rearrange("(c p) -> c p", p=128))], [hst[l]])
        cols_to_rows_store(nst[l].ap.rearrange("p h c -> p (h c)"), 8,
                           [(0, 8, O["n_" + g][l].rearrange("h (c p) -> (h c) p", p=128))], [nst[l]])
        S.dma("sp", lambda e: e.dma_start(out=O["C_" + g][l].rearrange("h (c p) e -> p h c e", p=128), in_=Cst[l].ap), Cst[l],
              reads=[Cst[l]])
        S.dma("sp", lambda e: e.dma_start(out=O["m_" + g][l].rearrange("(h o) -> h o", o=1), in_=mst[l].ap), mst[l],
              reads=[mst[l]])

    FL = [64]

    def mark(name):
        PHASES.append((name, len(S.ops["pe"])))

    def run_tile(src, dst, pos_idx, Tt, L, first_tile, last_tile, is_sample, g):
        mark("load")
        FL[0] = L
        load_rope(pos_idx, Tt, L)
        load_x(src, Tt)
        for l in range(DEPTH):
            mark("norm1")
            norm_to_u(l, "norm1_g", Tt)
            mark("lru")
            lru_phase(l, Tt)
            mark("mlstm")
            mlstm_phase(l, Tt, L)
            mark("attn")
            attn_phase(l, Tt, L, first_tile, is_sample, last_tile, g)
            mark("mlp")
            out_and_mlp(l, Tt)
            if last_tile:
                store_states(l, g)
        mark("store")
        store_y(dst, Tt)

    for l in range(DEPTH):
        init_states_zero(l)
    for t in range(NT):
        run_tile(x_p[t * T:(t + 1) * T, :], O["y_p"][t * T:(t + 1) * T, :], t * T, T, 64, t == 0, t == NT - 1, False, "p")
    if SAMPLE:
        for l in range(DEPTH):
            init_states_sample(l)
        run_tile(x_s, O["y_s"], SP, TS, TS, True, True, True, "s")
    stats = S.emit()
    return nc, stats


PHASES = []
CFG = dict(NT=16, T=256, DEPTH=2)
_cache = {}


def rope_tables(npos_list):
    half = 32
    inv = (10000.0 ** (-np.arange(half, dtype=np.float32) / half)).astype(np.float32)
    pos = np.asarray(npos_list, dtype=np.float32)
    ang = pos[:, None] * inv[None, :]
    return np.cos(ang).astype(np.float32), np.sin(ang).astype(np.float32)


def run(inputs, NT, T, DEPTH, n_cores, past_len=PAST_LEN):
    key = (NT, T, DEPTH)
    if key not in _cache:
        _cache[key] = build(NT, T, DEPTH, True)
    nc, stats = _cache[key]
    SPp = NT * T
    TS = 16
    pos = list(range(SPp)) + [past_len + i for i in range(TS)]
    rc, rs = rope_tables(pos)
    wnames = ["norm1_g", "w_in", "conv_w", "conv_b", "lru_wa", "lru_ba", "lru_wx", "lru_bx", "lru_lam", "m_bi", "m_bf",
              "m_norm_g", "qn_g", "kn_g", "sinks", "w_oa", "w_ob", "w_oc", "b_gate", "w_out", "norm2_g", "w_up", "w_down"]
    f = lambda a: np.ascontiguousarray(np.asarray(a, dtype=np.float32))
    wd = {k: f(inputs[k])[:DEPTH] for k in wnames}
    in_maps = []
    for b in range(n_cores):
        m = dict(wd)
        m["x_p"] = f(inputs["x_prompt"][b, :SPp])
        m["x_s"] = f(inputs["x_sample"][b])
        m["st_conv"] = f(inputs["state_conv"][:DEPTH, b])
        m["st_lru"] = f(inputs["state_lru"][:DEPTH, b])
        m["st_C"] = f(inputs["state_mlstm_C"][:DEPTH, b])
        m["st_n"] = f(inputs["state_mlstm_n"][:DEPTH, b])
        m["st_m"] = f(inputs["state_mlstm_m"][:DEPTH, b])
        m["c_k"] = f(inputs["cache_k"][:DEPTH, b])
        m["c_v"] = f(inputs["cache_v"][:DEPTH, b])
        m["rope_c"] = rc
        m["rope_s"] = rs
        in_maps.append(m)
    res = run_bass_kernel_spmd(nc, in_maps, core_ids=list(range(n_cores)))
    R = res.results
    outs = []
    outs.append(np.stack([np.asarray(R[b]["y_p"]) for b in range(n_cores)]))
    outs.append(np.stack([np.asarray(R[b]["y_s"]) for b in range(n_cores)]))
    for g in ["p", "s"]:
        for nm in ["conv_", "lru_", "C_", "n_", "m_", "k_", "v_"]:
            outs.append(np.stack([np.asarray(R[b][nm + g]) for b in range(n_cores)], axis=1))
    return tuple(o.astype(np.float32) for o in outs)


def kernel(**inputs):
    return run(inputs, CFG["NT"], CFG["T"], CFG["DEPTH"], 8)
```

```python
import numpy as np
from contextlib import ExitStack
import concourse.bass as bass
import concourse.mybir as mybir
from concourse.bass_utils import run_bass_kernel_spmd

F32 = mybir.dt.float32
BF16 = mybir.dt.bfloat16
AF = mybir.ActivationFunctionType
ALU = mybir.AluOpType
AX = mybir.AxisListType

D = 1024
EPS = 1e-6
IN_COLS = 10504
O_XA, O_GA, O_MQ, O_MK, O_MV, O_MO, O_MI, O_MF, O_AQ, O_AK, O_AV, O_G = (
    0, 1024, 2048, 3072, 4096, 5120, 6144, 6148, 6152, 7176, 7304, 7432)
PAST_LEN = 2048


class Sub:
    __slots__ = ("writer", "readers", "dma_readers")

    def __init__(self):
        self.writer = None
        self.readers = {}
        self.dma_readers = {}


class Buf:
    def __init__(self, name, ap, nsub=1):
        self.name = name
        self.ap = ap
        self.subs = [Sub() for _ in range(nsub)]
        self.dma_sem = None
        self.dma_count = 0

    def __getitem__(self, i):
        return (self, i)


def view(buf, ap):
    v = Buf(buf.name + "_v", ap, 0)
    v.subs = buf.subs
    return v


class ChunkView:
    def __init__(self, buf, ap, per):
        self.buf = buf
        self.ap = ap
        self.per = per
        self.subs = buf.subs

    def __getitem__(self, ch):
        return (self.buf, range(ch * self.per, (ch + 1) * self.per))


class Op:
    __slots__ = ("eng", "fn", "deps", "dma_waits", "signal", "is_dma", "buf", "count", "idx")

    def __init__(self, eng, fn):
        self.eng = eng
        self.fn = fn
        self.deps = []
        self.dma_waits = []
        self.signal = False
        self.is_dma = False
        self.buf = None
        self.count = None


def _subs(refs):
    out = []
    for r in refs:
        if isinstance(r, (Buf, ChunkView)):
            out.extend(r.subs)
        else:
            b, i = r
            if isinstance(i, (list, tuple, range)):
                out.extend(b.subs[j] for j in i)
            else:
                out.append(b.subs[i])
    return out


class Sched:
    ENGS = ["pe", "act", "dve", "pool", "sp"]

    def __init__(self, nc):
        self.nc = nc
        self.ops = {e: [] for e in self.ENGS}
        self.dma_bufs = []

    def add(self, eng, fn, reads=(), writes=(), dma_buf=None):
        op = Op(eng, fn)
        op.idx = len(self.ops[eng])
        need = {}
        dneed = {}

        def dep_on(d):
            if d is op:
                return
            if d.is_dma:
                dneed[id(d.buf)] = d.buf
            else:
                if d.eng == "pe" and eng == "pe":
                    return
                cur = need.get(d.eng)
                if cur is None or d.idx > cur.idx:
                    need[d.eng] = d

        rs = _subs(reads)
        ws = _subs(writes)
        for s in rs:
            if s.writer is not None:
                dep_on(s.writer)
        for s in ws:
            if s.writer is not None:
                dep_on(s.writer)
            for r in s.readers.values():
                dep_on(r)
            for b in s.dma_readers.values():
                dneed[id(b)] = b
        for d in need.values():
            op.deps.append(d)
            d.signal = True
        for b in dneed.values():
            op.dma_waits.append((b, b.dma_count))
        if dma_buf is not None:
            op.is_dma = True
            op.buf = dma_buf
            if dma_buf.dma_sem is None:
                dma_buf.dma_sem = "pending"
                self.dma_bufs.append(dma_buf)
            dma_buf.dma_count += 16
        for s in rs:
            if op.is_dma:
                s.dma_readers[id(dma_buf)] = dma_buf
            else:
                s.readers[eng] = op
        for s in ws:
            s.writer = op
            s.readers = {}
            s.dma_readers = {}
        self.ops[eng].append(op)
        return op

    def pe(self, fn, reads=(), writes=()):
        return self.add("pe", fn, reads, writes)

    def act(self, fn, reads=(), writes=()):
        return self.add("act", fn, reads, writes)

    def dve(self, fn, reads=(), writes=()):
        return self.add("dve", fn, reads, writes)

    def pool(self, fn, reads=(), writes=()):
        return self.add("pool", fn, reads, writes)

    def dma(self, eng, fn, buf, reads=(), writes=()):
        return self.add(eng, fn, reads, writes, dma_buf=buf)

    def emit(self):
        nc = self.nc
        with ExitStack() as st:
            esem = {}
            for e in ["pe", "act", "dve", "pool"]:
                esem[e] = st.enter_context(nc.semaphore("es_" + e))
            for b in self.dma_bufs:
                b.dma_sem = st.enter_context(nc.semaphore("ds_" + b.name))
            for e in ["pe", "act", "dve", "pool"]:
                c = 0
                for op in self.ops[e]:
                    if op.is_dma:
                        continue
                    if op.signal:
                        c += 1
                        op.count = c
            stats = {}
            block = st.enter_context(nc.Block())

            def run(e, engobj):
                waited = {}
                nw = 0
                for op in self.ops[e]:
                    for d in op.deps:
                        if waited.get(d.eng, 0) >= d.count:
                            continue
                        waited[d.eng] = d.count
                        engobj.wait_ge(esem[d.eng], d.count)
                        nw += 1
                    for (b, v) in op.dma_waits:
                        if waited.get(id(b), 0) >= v:
                            continue
                        waited[id(b)] = v
                        engobj.wait_ge(b.dma_sem, v)
                        nw += 1
                    inst = op.fn(engobj)
                    if op.is_dma:
                        inst.then_inc(op.buf.dma_sem, 16)
                    elif op.signal:
                        inst.then_inc(esem[e], 1)
                if e == "sp":
                    for ee in ["pe", "act", "dve", "pool"]:
                        last = 0
                        for op in self.ops[ee]:
                            if op.count:
                                last = op.count
                        if last:
                            engobj.wait_ge(esem[ee], last)
                    for b in self.dma_bufs:
                        engobj.wait_ge(b.dma_sem, b.dma_count)
                stats[e] = (len(self.ops[e]), nw)

            @block.tensor
            def _(eng):
                run("pe", eng)

            @block.scalar
            def _(eng):
                run("act", eng)

            @block.vector
            def _(eng):
                run("dve", eng)

            @block.gpsimd
            def _(eng):
                run("pool", eng)

            @block.sync
            def _(eng):
                run("sp", eng)

            return stats


def build(NT, T, DEPTH, SAMPLE, TS=16):
    nc = bass.Bass("TRN2", target_bir_lowering=False)
    S = Sched(nc)
    SP = NT * T
    NPOS = SP + TS

    def din(name, shape):
        return nc.dram_tensor(name, list(shape), F32, kind="ExternalInput").ap()

    def dout(name, shape):
        return nc.dram_tensor(name, list(shape), F32, kind="ExternalOutput").ap()

    x_p = din("x_p", [SP, D])
    x_s = din("x_s", [TS, D])
    st_conv = din("st_conv", [DEPTH, 3, D])
    st_lru = din("st_lru", [DEPTH, D])
    st_C = din("st_C", [DEPTH, 4, 256, 256])
    st_n = din("st_n", [DEPTH, 4, 256])
    st_m = din("st_m", [DEPTH, 4])
    c_k = din("c_k", [DEPTH, 128, 2, 64])
    c_v = din("c_v", [DEPTH, 128, 2, 64])
    rope_c = din("rope_c", [NPOS, 32])
    rope_s = din("rope_s", [NPOS, 32])
    W = {}
    for nm, shp in [("norm1_g", [DEPTH, D]), ("w_in", [DEPTH, D, IN_COLS]), ("conv_w", [DEPTH, 4, D]),
                    ("conv_b", [DEPTH, D]), ("lru_wa", [DEPTH, 8, 128, 128]), ("lru_ba", [DEPTH, D]),
                    ("lru_wx", [DEPTH, 8, 128, 128]), ("lru_bx", [DEPTH, D]), ("lru_lam", [DEPTH, D]),
                    ("m_bi", [DEPTH, 4]), ("m_bf", [DEPTH, 4]), ("m_norm_g", [DEPTH, D]), ("qn_g", [DEPTH, 64]),
                    ("kn_g", [DEPTH, 64]), ("sinks", [DEPTH, 16]), ("w_oa", [DEPTH, D, D]), ("w_ob", [DEPTH, D, D]),
                    ("w_oc", [DEPTH, D, D]), ("b_gate", [DEPTH, 3, D]), ("w_out", [DEPTH, D, D]),
                    ("norm2_g", [DEPTH, D]), ("w_up", [DEPTH, D, 4096]), ("w_down", [DEPTH, 4096, D])]:
        W[nm] = din(nm, shp)
    O = {}
    O["y_p"] = dout("y_p", [SP, D])
    O["y_s"] = dout("y_s", [TS, D])
    for g in ["p", "s"]:
        O["conv_" + g] = dout("conv_" + g, [DEPTH, 3, D])
        O["lru_" + g] = dout("lru_" + g, [DEPTH, D])
        O["C_" + g] = dout("C_" + g, [DEPTH, 4, 256, 256])
        O["n_" + g] = dout("n_" + g, [DEPTH, 4, 256])
        O["m_" + g] = dout("m_" + g, [DEPTH, 4])
        O["k_" + g] = dout("k_" + g, [DEPTH, 128, 2, 64])
        O["v_" + g] = dout("v_" + g, [DEPTH, 128, 2, 64])

    cnt = [0]

    def sb(shape, dt=F32, nsub=1, name=None):
        cnt[0] += 1
        nm = (name or "t") + "_%d" % cnt[0]
        t = nc.alloc_sbuf_tensor(nm, list(shape), dt)
        return Buf(nm, t.ap(), nsub)

    banks = []
    for i in range(8):
        t = nc.alloc_psum_tensor("bank%d" % i, [128, 512], F32)
        banks.append(Buf("bank%d" % i, t.ap(), 1))
    mm_ring = [0]
    NMM = 2

    def ps_mm():
        b = banks[mm_ring[0] % 8]
        mm_ring[0] += 1
        return b

    def aux(role):
        return banks[NMM + role]

    ev = [0]

    def evac(fn_act, fn_dve, reads, writes):
        ev[0] += 1
        if ev[0] % 2 == 0:
            return S.act(fn_act, reads, writes)
        return S.dve(fn_dve, reads, writes)

    def copy_any(out_ap, in_ap, reads, writes, scale=None):
        if scale is None:
            return evac(lambda e: e.activation(out=out_ap, in_=in_ap, func=AF.Copy),
                        lambda e: e.tensor_copy(out=out_ap, in_=in_ap), reads, writes)
        return evac(lambda e: e.activation(out=out_ap, in_=in_ap, func=AF.Copy, scale=scale),
                    lambda e: e.tensor_scalar(out=out_ap, in0=in_ap, scalar1=scale, scalar2=None, op0=ALU.mult),
                    reads, writes)

    ident = sb([128, 128], F32, name="ident")
    identb = sb([128, 128], BF16, name="identb")
    ones_bf = sb([128, 128], BF16, name="onesbf")
    ones32 = sb([4, 128], F32, name="ones32")
    cmask = sb([64, 64], F32, name="cmask")
    hmask = sb([4, 4, 8], F32, name="hmask")
    maskrow = sb([4, 512], F32, name="maskrow")
    S.pool(lambda e: e.memset(ident.ap, 0.0), writes=[ident])
    S.pool(lambda e: e.affine_select(out=ident.ap, in_=ident.ap, pattern=[[-1, 128]], compare_op=ALU.not_equal,
                                     fill=1.0, base=0, channel_multiplier=1), reads=[ident], writes=[ident])
    S.dve(lambda e: e.tensor_copy(out=identb.ap, in_=ident.ap), reads=[ident], writes=[identb])
    S.dve(lambda e: e.memset(ones_bf.ap, 1.0), writes=[ones_bf])
    S.dve(lambda e: e.memset(ones32.ap, 1.0), writes=[ones32])
    S.pool(lambda e: e.memset(cmask.ap, 1.0), writes=[cmask])
    S.pool(lambda e: e.affine_select(out=cmask.ap, in_=cmask.ap, pattern=[[1, 64]], compare_op=ALU.is_ge,
                                     fill=0.0, base=0, channel_multiplier=-1), reads=[cmask], writes=[cmask])
    S.pool(lambda e: e.memset(hmask.ap, 1.0), writes=[hmask])
    S.pool(lambda e: e.affine_select(out=hmask.ap, in_=hmask.ap, pattern=[[1, 4], [0, 8]], compare_op=ALU.is_equal,
                                     fill=0.0, base=0, channel_multiplier=-1), reads=[hmask], writes=[hmask])
    S.dve(lambda e: e.memset(maskrow.ap, 1.0), writes=[maskrow])
    S.dve(lambda e: e.memset(maskrow.ap.rearrange("p (c l) -> p c l", l=64)[:, :, 0:1], 0.0), writes=[maskrow])

    VEC = ["norm1_g", "norm2_g", "conv_b", "lru_ba", "lru_bx", "lru_lam", "cw0", "cw1", "cw2", "cw3", "bg0", "bg1", "bg2"]
    NV = len(VEC)
    colv = [sb([128, NV * 8], F32, name="colv") for _ in range(DEPTH)]
    nsp8 = [sb([128, 8], F32, name="nsp8") for _ in range(DEPTH)]
    nsp4 = [sb([128, 8], F32, name="nsp4") for _ in range(DEPTH)]
    hb = [sb([128, 16], F32, name="hb") for _ in range(DEPTH)]
    wa_bf = [sb([128, 8, 128], BF16, name="wa") for _ in range(DEPTH)]
    wx_bf = [sb([128, 8, 128], BF16, name="wx") for _ in range(DEPTH)]
    mg_row1 = sb([64, D], F32, name="mgrow")
    mg_row = [mg_row1 for _ in range(DEPTH)]
    qg_row = [sb([128, 64], F32, name="qgrow") for _ in range(DEPTH)]
    kg_row = [sb([64, 64], F32, name="kgrow") for _ in range(DEPTH)]
    esink = [sb([128, 16], F32, name="esink") for _ in range(DEPTH)]
    bi_col = [sb([4, 1], F32, name="bi") for _ in range(DEPTH)]
    nbf_col = [sb([4, 1], F32, name="nbf") for _ in range(DEPTH)]
    vstage = sb([128, 128], F32, name="vstage")

    def vcol(l, name, c):
        i = VEC.index(name)
        return colv[l].ap[:, i * 8 + c:i * 8 + c + 1]

    for l in range(DEPTH):
        srcs = {"norm1_g": W["norm1_g"][l], "norm2_g": W["norm2_g"][l], "conv_b": W["conv_b"][l],
                "lru_ba": W["lru_ba"][l], "lru_bx": W["lru_bx"][l], "lru_lam": W["lru_lam"][l]}
        for j in range(4):
            srcs["cw%d" % j] = W["conv_w"][l, j]
        for j in range(3):
            srcs["bg%d" % j] = W["b_gate"][l, j]
        for i, nm in enumerate(VEC):
            src = srcs[nm].rearrange("(c p) -> c p", p=128)
            S.dma("sp", lambda e, i=i, src=src: e.dma_start(out=vstage.ap[i * 8:(i + 1) * 8, :], in_=src), vstage,
                  writes=[vstage])
        pb = aux(5)
        S.pe(lambda e, pb=pb: e.transpose(pb.ap[:, 0:NV * 8], vstage.ap[0:NV * 8, :], ident.ap[0:NV * 8, 0:NV * 8]),
             reads=[vstage, ident], writes=[pb])
        S.dve(lambda e, pb=pb, l=l: e.tensor_copy(out=colv[l].ap, in_=pb.ap[:, 0:NV * 8]), reads=[pb], writes=[colv[l]])
        lam = colv[l].ap[:, VEC.index("lru_lam") * 8:VEC.index("lru_lam") * 8 + 8]
        S.act(lambda e, l=l, lam=lam: e.activation(out=nsp8[l].ap, in_=lam, func=AF.Exp, scale=-1.0),
              reads=[colv[l]], writes=[nsp8[l]])
        S.act(lambda e, l=l: e.activation(out=nsp8[l].ap, in_=nsp8[l].ap, func=AF.Ln, bias=1.0),
              reads=[nsp8[l]], writes=[nsp8[l]])
        S.dve(lambda e, l=l: e.tensor_scalar(out=nsp4[l].ap, in0=nsp8[l].ap, scalar1=-4.0, scalar2=None, op0=ALU.mult),
              reads=[nsp8[l]], writes=[nsp4[l]])
        S.dve(lambda e, l=l: e.tensor_scalar(out=nsp8[l].ap, in0=nsp8[l].ap, scalar1=-8.0, scalar2=None, op0=ALU.mult),
              reads=[nsp8[l], nsp4[l]], writes=[nsp8[l]])
        _ib = VEC.index("lru_ba") * 8
        S.dve(lambda e, l=l, _ib=_ib: e.tensor_scalar(out=hb[l].ap, in0=colv[l].ap[:, _ib:_ib + 16], scalar1=0.5, scalar2=None,
                                                      op0=ALU.mult), reads=[colv[l]], writes=[hb[l]])
        S.dma("pool", lambda e, l=l: e.dma_start(out=wa_bf[l].ap, in_=W["lru_wa"][l].rearrange("n c d -> c n d")),
              wa_bf[l], writes=[wa_bf[l]])
        S.dma("pool", lambda e, l=l: e.dma_start(out=wx_bf[l].ap, in_=W["lru_wx"][l].rearrange("n c d -> c n d")),
              wx_bf[l], writes=[wx_bf[l]])
        S.dma("sp", lambda e, l=l: e.dma_start(out=qg_row[l].ap, in_=W["qn_g"][l].partition_broadcast(128)),
              qg_row[l], writes=[qg_row[l]])
        S.dma("sp", lambda e, l=l: e.dma_start(out=kg_row[l].ap, in_=W["kn_g"][l].partition_broadcast(64)),
              kg_row[l], writes=[kg_row[l]])
        S.dma("sp", lambda e, l=l: e.dma_start(out=esink[l].ap, in_=W["sinks"][l].partition_broadcast(128)),
              esink[l], writes=[esink[l]])
        S.act(lambda e, l=l: e.activation(out=esink[l].ap, in_=esink[l].ap, func=AF.Exp), reads=[esink[l]],
              writes=[esink[l]])
        S.dma("sp", lambda e, l=l: e.dma_start(out=bi_col[l].ap, in_=W["m_bi"][l].rearrange("(h o) -> h o", o=1)),
              bi_col[l], writes=[bi_col[l]])
        S.dma("sp", lambda e, l=l: e.dma_start(out=nbf_col[l].ap, in_=W["m_bf"][l].rearrange("(h o) -> h o", o=1)),
              nbf_col[l], writes=[nbf_col[l]])
        S.dve(lambda e, l=l: e.tensor_scalar(out=nbf_col[l].ap, in0=nbf_col[l].ap, scalar1=-1.0, scalar2=None,
                                             op0=ALU.mult), reads=[nbf_col[l]], writes=[nbf_col[l]])

    SCR = {}
    SCRB = {}
    for nm, R_, C_ in [("w_in", D, IN_COLS), ("w_oa", D, D), ("w_ob", D, D), ("w_oc", D, D), ("w_out", D, D),
                       ("w_up", D, 4096), ("w_down", 4096, D)]:
        SCR[nm] = nc.dram_tensor(nm + "_bf", [DEPTH, R_, C_], BF16, kind="Internal").ap()
    WIN_GROUPS = [(0, 2048), (2048, 4096), (O_G, O_G + 1024), (4096, O_AQ), (O_G + 1024, O_G + 2048), (O_AQ, O_G),
                  (O_G + 2048, IN_COLS)]
    for l in range(DEPTH):
        for gi_, (g0, g1) in enumerate(WIN_GROUPS):
            b_ = Buf("w_in_bf%d_%d" % (l, gi_), None, 2)
            SCRB[("w_in", l, gi_)] = b_
            for rb in range(2):
                S.dma("pool", lambda e, l=l, rb=rb, g0=g0, g1=g1: e.dma_start(out=SCR["w_in"][l, rb * 512:(rb + 1) * 512, g0:g1],
                                                                              in_=W["w_in"][l, rb * 512:(rb + 1) * 512, g0:g1]),
                      b_, writes=[b_[rb]])
        for nm in ["w_oa", "w_ob", "w_oc", "w_out", "w_up", "w_down"]:
            R_ = W[nm].shape[1]
            nblk = R_ // 256
            b_ = Buf("%s_bf%d" % (nm, l), None, nblk)
            SCRB[(nm, l)] = b_
            for rb in range(nblk):
                S.dma("pool", lambda e, nm=nm, l=l, rb=rb: e.dma_start(out=SCR[nm][l, rb * 256:(rb + 1) * 256, :],
                                                                       in_=W[nm][l, rb * 256:(rb + 1) * 256, :]),
                      b_, writes=[b_[rb]])

    NCHM = max(T // 64, 1)
    hist = [sb([128, 3, 8], F32, name="hist") for _ in range(DEPTH)]
    hst = [sb([128, 8], F32, name="hst") for _ in range(DEPTH)]
    Cst = [sb([128, 4, 2, 256], F32, nsub=4, name="Cst") for _ in range(DEPTH)]
    nst = [sb([128, 4, 2], F32, name="nst") for _ in range(DEPTH)]
    mst = [sb([4, 1], F32, name="mst") for _ in range(DEPTH)]
    NSLOT = 2 + NCHM
    KTwin = [sb([64, 2, NSLOT * 64], BF16, name="KTwin") for _ in range(DEPTH)]
    vaug = [sb([64, NSLOT, 2, 192], BF16, name="vaug") for _ in range(DEPTH)]

    xT = sb([128, 8, T], F32, nsub=8, name="xT")
    uT = sb([128, 8, T], BF16, nsub=8, name="uT")
    NSLAB = 4
    slabs = [sb([128, 4096], BF16, name="slab") for _ in range(NSLAB)]
    slab_i = [0]
    TB = min(T, 128)
    xin = [sb([128, D], F32, name="xin") for _ in range(2)]
    sqb = [sb([128, T], BF16, name="sqb") for _ in range(2)]
    rstd = sb([128, T], F32, name="rstd")
    xa_w = [sb([128, 3 + T], F32, name="xaw") for _ in range(2)]
    xc3 = [sb([128, T], F32, name="xc") for _ in range(3)]
    xcb = [sb([128, T], BF16, name="xcb") for _ in range(2)]
    rr = [sb([128, T], F32, name="rr") for _ in range(2)]
    ii = [sb([128, T], F32, name="ii") for _ in range(2)]
    aa = [sb([128, T], F32, name="aa") for _ in range(2)]
    sq1 = [sb([128, T], F32, name="sq1") for _ in range(2)]
    hh = [sb([128, T], F32, name="hh") for _ in range(2)]
    gel3 = [sb([128, T], F32, name="gel") for _ in range(3)]
    gel = gel3
    hgT = sb([128, 8, T], BF16, nsub=8, name="hgT")
    sg = [gel[0], gel[1]]
    tmpm = [hh[0], hh[1]]
    mix = sb([128, 8, T], F32, nsub=8, name="mix")
    qT = sb([128, 8, T], BF16, nsub=8, name="qT")
    kT = sb([128, 8, T], BF16, nsub=8, name="kT")
    g64 = [sb([64, 16, T], BF16, nsub=16, name="g64") for _ in range(4)]
    PER = 16 // NCHM

    def cview(g, vw4=False):
        flat = g.ap.rearrange("p h t -> p (h t)")
        if vw4:
            return ChunkView(g, flat.rearrange("p (c h e) -> p c h e", c=NCHM, h=4), PER)
        return ChunkView(g, flat.rearrange("p (c f) -> p c f", c=NCHM), PER)

    ktok = cview(g64[0])
    vw = cview(g64[1], True)
    sgm = cview(g64[2])
    hmtok = cview(g64[3])
    hmT = sb([128, 8, T], BF16, nsub=8, name="hmT")
    _gsrc = [rr[0], rr[1], ii[0], ii[1], aa[0], aa[1], sq1[0], sq1[1]]
    grow = {nm: view(_gsrc[i], _gsrc[i].ap[0:4, :]) for i, nm in enumerate(["ig", "sp", "b", "a", "ea", "cl", "iwt", "tmp"])}
    gsm = {nm: sb([4, NCHM], F32, name="gs_" + nm) for nm in ["amax", "d0", "M", "mnew", "mprev", "iw"]}
    rhsm = sb([4, 4, NCHM], F32, name="rhsm")
    iw_rep = sb([128, 4 * NCHM], F32, name="iwrep")
    colq = sb([64, NCHM, 12], F32, name="colq")
    eab = sb([64, NCHM, 4], BF16, name="eab")
    Cnb = [sb([128, 2, 256], BF16, name="Cnb") for _ in range(2)]
    nb = [sb([128, 2], BF16, name="nb") for _ in range(4)]
    nb4 = nb
    smask = [sb([64, 4, 64], BF16, name="smask") for _ in range(2)]
    den_s = [sb([64, 4], F32, name="dens") for _ in range(2)]
    den_t = [sb([64, 4], F32, name="dent") for _ in range(2)]
    hn = [view(xin[i], xin[i].ap[0:64, :].rearrange("p (h d) -> p h d", h=4)) for i in range(2)]
    ssm = [sb([64, 4], F32, name="ssm") for _ in range(2)]
    NBLK = (T + TB - 1) // TB
    ctab_k = sb([64, NCHM, 32], F32, name="ctabk")
    stab_k = sb([64, NCHM, 32], F32, name="stabk")
    ctab_q = sb([128, NBLK, 32], F32, name="ctabq")
    stab_q = sb([128, NBLK, 32], F32, name="stabq")
    QT_all = g64[0]
    OT2 = hmT
    qsq = sb([128, 512], F32, name="qsq")
    qss = [sb([128, 8], F32, name="qss") for _ in range(2)]
    qn = [sb([128, 8, 64], F32, name="qn") for _ in range(2)]
    qt1 = sb([128, 8, 32], F32, name="qt1")
    qt2 = sb([128, 8, 32], F32, name="qt2")
    qr = [sb([128, 8, 64], BF16, name="qr") for _ in range(2)]

    def _as_cnb(b_):
        return view(b_, b_.ap.rearrange("p a b -> p (a b)").bitcast(BF16).rearrange("p (c e) -> p c e", c=2))

    Cnb4 = [Cnb[0], Cnb[1], _as_cnb(qt1), _as_cnb(qt2)]
    kss = [sb([64, 2], F32, name="kss") for _ in range(2)]
    ksq = sb([64, 128], F32, name="ksq")
    kn = [sb([64, 2, 64], F32, name="kn") for _ in range(2)]
    kt1 = sb([64, 2, 32], F32, name="kt1")
    kt2 = sb([64, 2, 32], F32, name="kt2")
    kr = [sb([64, 128], F32, name="kr") for _ in range(2)]
    krb = [sb([64, 128], BF16, name="krb") for _ in range(2)]
    vf = [sb([64, 128], F32, name="vf") for _ in range(2)]
    pT = [sb([64, 512], BF16, name="pT") for _ in range(6)]
    pT_i = [0]
    dsum = [sb([128, 512], F32, name="dsum") for _ in range(2)]
    hsq = view(dsum[0], dsum[0].ap[0:64, :].bitcast(BF16).rearrange("p (h d) -> p h d", h=4))
    hid_parts = [hgT, qT, kT, hmT]
    rl = [rr[0], rr[1]]
    yout = xin
    kcs = view(qsq, qsq.ap[0:64, 0:256].rearrange("p (s f) -> p s f", s=2))
    kcb = sb([64, 2, 128], BF16, name="kcb")
    tailk = vf[0]

    def pipeline(gens, newest_first):
        gens = list(gens)
        active = []
        i = 0
        while i < len(gens) or active:
            if i < len(gens):
                active.append(gens[i])
                i += 1
            order = list(reversed(active)) if newest_first else list(active)
            for g in order:
                try:
                    next(g)
                except StopIteration:
                    active.remove(g)

    slab_live = [False] * NSLAB

    def release(sl):
        slab_live[slabs.index(sl)] = False

    def load_slab(src_ap, view, srcbuf, hold=False):
        for _ in range(NSLAB + 1):
            i_ = slab_i[0] % NSLAB
            slab_i[0] += 1
            if not slab_live[i_]:
                break
        else:
            raise RuntimeError("all slabs live")
        sl = slabs[i_]
        if hold:
            slab_live[i_] = True
        dst = view(sl.ap)
        S.dma("sp", lambda e: e.dma_start(out=dst, in_=src_ap), sl, reads=[srcbuf], writes=[sl])
        return sl, dst

    def slab_k8(l, wname, c0, ncols, hold=False):
        src = SCR[wname][l][:, c0:c0 + ncols].rearrange("(kc p) n -> p kc n", p=128)
        if wname == "w_in":
            gi_ = [i for i, (g0, g1) in enumerate(WIN_GROUPS) if g0 <= c0 and c0 + ncols <= g1]
            assert len(gi_) == 1, (c0, ncols)
            sb_ = SCRB[("w_in", l, gi_[0])]
        else:
            sb_ = SCRB[(wname, l)]
        return load_slab(src, lambda a: a[:, 0:8 * ncols].rearrange("p (k n) -> p k n", k=8), sb_, hold=hold)

    def qk_proj_gen(l, Tt, L):
        NCH = Tt // L
        for half in range(2):
            sl, v = slab_k8(l, "w_in", O_MQ + half * 512, 512, hold=True)
            for c4 in range(4):
                c = half * 4 + c4
                pb = ps_mm()
                fm_proj(pb, sl, v, c4 * 128, uT, Tt)
                S.dve(lambda e, pb=pb, c=c: e.tensor_copy(out=qT.ap[:, c, 0:Tt], in_=pb.ap[:, 0:Tt]), reads=[pb], writes=[qT[c]])
                yield
            release(sl)
        for half in range(2):
            sl, v = slab_k8(l, "w_in", O_MK + half * 512, 512, hold=True)
            for c4 in range(4):
                c = half * 4 + c4
                pb = ps_mm()
                fm_proj(pb, sl, v, c4 * 128, uT, Tt)
                S.dve(lambda e, pb=pb, c=c: e.tensor_scalar(out=kT.ap[:, c, 0:Tt], in0=pb.ap[:, 0:Tt], scalar1=1.0 / 16.0,
                                                            scalar2=None, op0=ALU.mult), reads=[pb], writes=[kT[c]])
                yield
            for ch in range(NCH):
                pb = ps_mm()
                for k in range(8):
                    S.pe(lambda e, k=k, pb=pb, ch=ch, v=v: e.matmul(pb.ap[0:L, :], uT.ap[:, k, ch * L:(ch + 1) * L], v[:, k, :],
                                                                    start=(k == 0), stop=(k == 7)),
                         reads=[sl, uT[k]], writes=[pb])
                S.dve(lambda e, pb=pb, ch=ch, half=half: e.tensor_scalar(out=ktok.ap[0:L, ch, half * 512:(half + 1) * 512],
                                                                         in0=pb.ap[0:L, :], scalar1=1.0 / 16.0, scalar2=None,
                                                                         op0=ALU.mult), reads=[pb], writes=[ktok[ch]])
                yield
            release(sl)

    def fm_proj(pb, sl, sview, col, act, Tt, KC=8):
        for k in range(KC):
            S.pe(lambda e, k=k: e.matmul(pb.ap[:, 0:Tt], sview[:, k, col:col + 128], act.ap[:, k, 0:Tt],
                                         start=(k == 0), stop=(k == KC - 1)),
                 reads=[sl, act[k]], writes=[pb])

    def norm_to_u(l, gname, Tt):
        pb = aux(0)
        for c in range(8):
            q = sqb[c % 2]
            S.act(lambda e, c=c, q=q: e.activation(out=q.ap[:, 0:Tt], in_=xT.ap[:, c, 0:Tt], func=AF.Square),
                  reads=[xT[c]], writes=[q])
            S.pe(lambda e, c=c, q=q: e.matmul(pb.ap[:, 0:Tt], ones_bf.ap, q.ap[:, 0:Tt], start=(c == 0), stop=(c == 7)),
                 reads=[q, ones_bf], writes=[pb])
        S.act(lambda e: e.activation(out=rstd.ap[:, 0:Tt], in_=pb.ap[:, 0:Tt], func=AF.Ln, scale=1.0 / D, bias=EPS),
              reads=[pb], writes=[rstd])
        S.act(lambda e: e.activation(out=rstd.ap[:, 0:Tt], in_=rstd.ap[:, 0:Tt], func=AF.Exp, scale=-0.5), reads=[rstd], writes=[rstd])
        for c in range(8):
            S.dve(lambda e, c=c: e.scalar_tensor_tensor(out=uT.ap[:, c, 0:Tt], in0=xT.ap[:, c, 0:Tt],
                                                        scalar=vcol(l, gname, c), in1=rstd.ap[:, 0:Tt],
                                                        op0=ALU.mult, op1=ALU.mult),
                  reads=[xT[c], rstd, colv[l]], writes=[uT[c]])

    def merge_branch(l, br, featT, Tt, K64=False):
        wname = ["w_oa", "w_ob", "w_oc"][br]
        for half in range(2):
            slg, vg = slab_k8(l, "w_in", O_G + br * 1024 + half * 512, 512)
            if not K64:
                slo, vo = slab_k8(l, wname, half * 512, 512)
            for c4 in range(4):
                c = half * 4 + c4
                pg = ps_mm()
                fm_proj(pg, slg, vg, c4 * 128, uT, Tt)
                s_ = sg[c % 2]
                S.act(lambda e, pg=pg, s_=s_, c=c: e.activation(out=s_.ap[:, 0:Tt], in_=pg.ap[:, 0:Tt], func=AF.Sigmoid,
                                                                bias=vcol(l, "bg%d" % br, c)),
                      reads=[pg, colv[l]], writes=[s_])
                py = ps_mm()
                if not K64:
                    fm_proj(py, slo, vo, c4 * 128, featT, Tt)
                else:
                    if c4 % 2 == 0:
                        src = SCR["w_oc"][l][:, c * 128:c * 128 + 256].rearrange("(h d) n -> d h n", d=64)
                        slo, vo = load_slab(src, lambda a: a[0:64, :].rearrange("p (h n) -> p h n", h=16), SCRB[("w_oc", l)])
                    off = (c4 % 2) * 128
                    for h in range(16):
                        S.pe(lambda e, h=h, py=py, vo=vo, off=off: e.matmul(py.ap[:, 0:Tt], vo[:, h, off:off + 128],
                                                                            featT.ap[:, h, 0:Tt], start=(h == 0),
                                                                            stop=(h == 15)),
                             reads=[slo, featT[h]], writes=[py])
                if br == 0:
                    S.dve(lambda e, py=py, s_=s_, c=c: e.tensor_tensor(out=mix.ap[:, c, 0:Tt], in0=py.ap[:, 0:Tt],
                                                                       in1=s_.ap[:, 0:Tt], op=ALU.mult),
                          reads=[py, s_], writes=[mix[c]])
                else:
                    t_ = tmpm[c % 2]
                    S.dve(lambda e, py=py, s_=s_, t_=t_: e.tensor_tensor(out=t_.ap[:, 0:Tt], in0=py.ap[:, 0:Tt],
                                                                         in1=s_.ap[:, 0:Tt], op=ALU.mult),
                          reads=[py, s_], writes=[t_])
                    S.pool(lambda e, t_=t_, c=c: e.tensor_tensor(out=mix.ap[:, c, 0:Tt], in0=mix.ap[:, c, 0:Tt],
                                                                 in1=t_.ap[:, 0:Tt], op=ALU.add),
                           reads=[t_, mix[c]], writes=[mix[c]])

    def lru_phase(l, Tt):
        sl_ = {}

        def body(c):
            half, c4 = divmod(c, 4)
            if c4 == 0:
                if "a" in sl_:
                    release(sl_["a"][0])
                    release(sl_["g"][0])
                sl_["a"] = slab_k8(l, "w_in", O_XA + half * 512, 512, hold=True)
                sl_["g"] = slab_k8(l, "w_in", O_GA + half * 512, 512, hold=True)
            sla, va = sl_["a"]
            slg, vg = sl_["g"]
            j = c % 2
            j3 = c % 3
            pa = ps_mm()
            fm_proj(pa, sla, va, c4 * 128, uT, Tt)
            xw = xa_w[j]
            S.dve(lambda e: e.tensor_copy(out=xw.ap[:, 0:3], in_=hist[l].ap[:, :, c]), reads=[hist[l]], writes=[xw])
            S.act(lambda e: e.activation(out=xw.ap[:, 3:3 + Tt], in_=pa.ap[:, 0:Tt], func=AF.Copy), reads=[pa], writes=[xw])
            S.dve(lambda e: e.tensor_copy(out=hist[l].ap[:, :, c], in_=xw.ap[:, Tt:Tt + 3]), reads=[xw], writes=[hist[l]])
            x_ = xc3[j3]
            S.dve(lambda e: e.tensor_scalar(out=x_.ap[:, 0:Tt], in0=xw.ap[:, 0:Tt], scalar1=vcol(l, "cw0", c),
                                            scalar2=vcol(l, "conv_b", c), op0=ALU.mult, op1=ALU.add),
                  reads=[xw, colv[l]], writes=[x_])
            for jj in range(1, 4):
                S.dve(lambda e, jj=jj: e.scalar_tensor_tensor(out=x_.ap[:, 0:Tt], in0=xw.ap[:, jj:jj + Tt],
                                                              scalar=vcol(l, "cw%d" % jj, c), in1=x_.ap[:, 0:Tt],
                                                              op0=ALU.mult, op1=ALU.add), reads=[xw, x_, colv[l]], writes=[x_])
            xb_ = xcb[j]
            S.dve(lambda e: e.tensor_copy(out=xb_.ap[:, 0:Tt], in_=x_.ap[:, 0:Tt]), reads=[x_], writes=[xb_])
            pg = ps_mm()
            fm_proj(pg, slg, vg, c4 * 128, uT, Tt)
            g_ = gel3[j3]
            S.act(lambda e: e.activation(out=g_.ap[:, 0:Tt], in_=pg.ap[:, 0:Tt], func=AF.Gelu_apprx_tanh), reads=[pg], writes=[g_])
            yield
            pr = aux(1)
            S.pe(lambda e: e.matmul(pr.ap[:, 0:Tt], wa_bf[l].ap[:, c, :], xb_.ap[:, 0:Tt], start=True, stop=True),
                 reads=[wa_bf[l], xb_], writes=[pr])
            pi = aux(2)
            S.pe(lambda e: e.matmul(pi.ap[:, 0:Tt], wx_bf[l].ap[:, c, :], xb_.ap[:, 0:Tt], start=True, stop=True),
                 reads=[wx_bf[l], xb_], writes=[pi])
            r_, i_, a_, q_, h_ = rr[j], ii[j], aa[j], sq1[j], hh[j]
            S.act(lambda e: e.activation(out=r_.ap[:, 0:Tt], in_=pr.ap[:, 0:Tt], func=AF.Tanh, scale=0.5, bias=hb[l].ap[:, c:c + 1]),
                  reads=[pr, hb[l]], writes=[r_])
            S.act(lambda e: e.activation(out=i_.ap[:, 0:Tt], in_=pi.ap[:, 0:Tt], func=AF.Tanh, scale=0.5,
                                         bias=hb[l].ap[:, 8 + c:9 + c]), reads=[pi, hb[l]], writes=[i_])
            yield
            S.act(lambda e: e.activation(out=a_.ap[:, 0:Tt], in_=r_.ap[:, 0:Tt], func=AF.Exp, scale=nsp4[l].ap[:, c:c + 1],
                                         bias=nsp4[l].ap[:, c:c + 1]), reads=[r_, nsp4[l]], writes=[a_])
            S.act(lambda e: e.activation(out=q_.ap[:, 0:Tt], in_=r_.ap[:, 0:Tt], func=AF.Exp, scale=nsp8[l].ap[:, c:c + 1],
                                         bias=nsp8[l].ap[:, c:c + 1]), reads=[r_, nsp8[l]], writes=[q_])
            S.act(lambda e: e.activation(out=q_.ap[:, 0:Tt], in_=q_.ap[:, 0:Tt], func=AF.Ln, scale=-1.0, bias=1.0),
                  reads=[q_], writes=[q_])
            S.act(lambda e: e.activation(out=q_.ap[:, 0:Tt], in_=q_.ap[:, 0:Tt], func=AF.Exp, scale=0.5), reads=[q_], writes=[q_])
            S.dve(lambda e: e.scalar_tensor_tensor(out=i_.ap[:, 0:Tt], in0=i_.ap[:, 0:Tt], scalar=1.0, in1=x_.ap[:, 0:Tt],
                                                   op0=ALU.add, op1=ALU.mult), reads=[i_, x_], writes=[i_])
            S.dve(lambda e: e.scalar_tensor_tensor(out=i_.ap[:, 0:Tt], in0=i_.ap[:, 0:Tt], scalar=0.5, in1=q_.ap[:, 0:Tt],
                                                   op0=ALU.mult, op1=ALU.mult), reads=[i_, q_], writes=[i_])
            S.dve(lambda e: e.tensor_tensor_scan(out=h_.ap[:, 0:Tt], data0=a_.ap[:, 0:Tt], data1=i_.ap[:, 0:Tt],
                                                 initial=hst[l].ap[:, c:c + 1], op0=ALU.mult, op1=ALU.add),
                  reads=[a_, i_, hst[l]], writes=[h_])
            S.pool(lambda e: e.tensor_copy(out=hst[l].ap[:, c:c + 1], in_=h_.ap[:, Tt - 1:Tt]), reads=[h_], writes=[hst[l]])
            S.pool(lambda e: e.tensor_tensor(out=hgT.ap[:, c, 0:Tt], in0=h_.ap[:, 0:Tt], in1=g_.ap[:, 0:Tt], op=ALU.mult),
                   reads=[g_, h_], writes=[hgT[c]])
            yield

        gens = [body(c) for c in range(8)]
        filler = qk_proj_gen(l, Tt, FL[0])
        for rnd in range(8 + 2):
            sa = rnd if rnd < 8 else None
            sb1 = rnd - 1 if 0 <= rnd - 1 < 8 else None
            sb2 = rnd - 2 if 0 <= rnd - 2 < 8 else None
            order = [sa, sb1, sb2] if rnd % 2 == 0 else [sb2, sa, sb1]
            for gi in order:
                if gi is not None:
                    next(gens[gi])
            for _ in range(3):
                next(filler, None)
        release(sl_["a"][0])
        release(sl_["g"][0])
        for _ in filler:
            pass
        merge_branch(l, 0, hgT, Tt)

    def mlstm_phase(l, Tt, L):
        NCH = Tt // L
        S.dma("sp", lambda e: e.dma_start(out=mg_row1.ap, in_=W["m_norm_g"][l].partition_broadcast(64)), mg_row1,
              writes=[mg_row1])
        slg, vg = slab_k8(l, "w_in", O_MI, 8)
        pi = aux(0)
        pf = aux(1)
        for k in range(8):
            S.pe(lambda e, k=k: e.matmul(pi.ap[0:4, 0:Tt], vg[:, k, 0:4], uT.ap[:, k, 0:Tt], start=(k == 0), stop=(k == 7)),
                 reads=[slg, uT[k]], writes=[pi])
        for k in range(8):
            S.pe(lambda e, k=k: e.matmul(pf.ap[0:4, 0:Tt], vg[:, k, 4:8], uT.ap[:, k, 0:Tt], start=(k == 0), stop=(k == 7)),
                 reads=[slg, uT[k]], writes=[pf])
        G = {k: v.ap[:, 0:Tt] for k, v in grow.items()}
        Gs = {k: v.ap[:, 0:NCH] for k, v in gsm.items()}
        S.act(lambda e: e.activation(out=G["ig"], in_=pi.ap[0:4, 0:Tt], func=AF.Identity, bias=bi_col[l].ap),
              reads=[pi, bi_col[l]], writes=[grow["ig"]])
        S.act(lambda e: e.activation(out=G["sp"], in_=pf.ap[0:4, 0:Tt], func=AF.Exp, scale=-1.0, bias=nbf_col[l].ap),
              reads=[pf, nbf_col[l]], writes=[grow["sp"]])
        S.act(lambda e: e.activation(out=G["sp"], in_=G["sp"], func=AF.Ln, bias=1.0), reads=[grow["sp"]], writes=[grow["sp"]])
        S.dve(lambda e: e.tensor_tensor_scan(out=G["b"], data0=maskrow.ap[:, 0:Tt], data1=G["sp"], initial=0.0,
                                             op0=ALU.mult, op1=ALU.subtract), reads=[maskrow, grow["sp"]], writes=[grow["b"]])
        S.dve(lambda e: e.tensor_tensor(out=G["a"], in0=G["ig"], in1=G["b"], op=ALU.subtract),
              reads=[grow["ig"], grow["b"]], writes=[grow["a"]])
        a3 = G["a"].rearrange("p (c l) -> p c l", l=L)
        b3 = G["b"].rearrange("p (c l) -> p c l", l=L)
        S.dve(lambda e: e.tensor_reduce(out=Gs["amax"], in_=a3, axis=AX.X, op=ALU.max), reads=[grow["a"]], writes=[gsm["amax"]])
        S.dve(lambda e: e.memset(Gs["d0"][:, 0:1], 0.0), writes=[gsm["d0"]])
        if NCH > 1:
            S.dve(lambda e: e.tensor_copy(out=Gs["d0"][:, 1:NCH], in_=b3[:, 0:NCH - 1, L - 1]), reads=[grow["b"]],
                  writes=[gsm["d0"]])
        S.dve(lambda e: e.tensor_tensor_scan(out=Gs["M"], data0=Gs["d0"], data1=Gs["amax"], initial=mst[l].ap,
                                             op0=ALU.add, op1=ALU.max), reads=[gsm["d0"], gsm["amax"], mst[l]],
              writes=[gsm["M"]])
        S.dve(lambda e: e.tensor_tensor(out=Gs["mnew"], in0=b3[:, :, L - 1], in1=Gs["M"], op=ALU.add),
              reads=[grow["b"], gsm["M"]], writes=[gsm["mnew"]])
        S.dve(lambda e: e.tensor_copy(out=Gs["mprev"][:, 0:1], in_=mst[l].ap), reads=[mst[l]], writes=[gsm["mprev"]])
        if NCH > 1:
            S.dve(lambda e: e.tensor_copy(out=Gs["mprev"][:, 1:NCH], in_=Gs["mnew"][:, 0:NCH - 1]), reads=[gsm["mnew"]],
                  writes=[gsm["mprev"]])
        S.dve(lambda e: e.tensor_copy(out=mst[l].ap, in_=Gs["mnew"][:, NCH - 1:NCH]), reads=[gsm["mnew"], gsm["mprev"]],
              writes=[mst[l]])
        S.dve(lambda e: e.tensor_tensor(out=Gs["iw"], in0=Gs["mprev"], in1=Gs["M"], op=ALU.subtract),
              reads=[gsm["mprev"], gsm["M"]], writes=[gsm["iw"]])
        S.act(lambda e: e.activation(out=Gs["iw"], in_=Gs["iw"], func=AF.Exp), reads=[gsm["iw"]], writes=[gsm["iw"]])
        Mb = Gs["M"].unsqueeze(2).to_broadcast([4, NCH, L])
        ea3 = G["ea"].rearrange("p (c l) -> p c l", l=L)
        cl3 = G["cl"].rearrange("p (c l) -> p c l", l=L)
        iw3 = G["iwt"].rearrange("p (c l) -> p c l", l=L)
        S.dve(lambda e: e.tensor_tensor(out=ea3, in0=a3, in1=Mb, op=ALU.subtract), reads=[grow["a"], gsm["M"]],
              writes=[grow["ea"]])
        S.act(lambda e: e.activation(out=G["ea"], in_=G["ea"], func=AF.Exp), reads=[grow["ea"]], writes=[grow["ea"]])
        S.dve(lambda e: e.tensor_tensor(out=cl3, in0=b3, in1=Mb, op=ALU.add), reads=[grow["b"], gsm["M"]],
              writes=[grow["cl"]])
        S.act(lambda e: e.activation(out=G["cl"], in_=G["cl"], func=AF.Exp, scale=-1.0), reads=[grow["cl"]],
              writes=[grow["cl"]])
        S.dve(lambda e: e.tensor_copy(out=iw3, in_=Gs["iw"].unsqueeze(2).to_broadcast([4, NCH, L])), reads=[gsm["iw"]],
              writes=[grow["iwt"]])
        G2 = min(2, NCH)
        TG = G2 * L
        sgt2 = [qsq, dsum[1]]
        for half in range(2):
            sl, v = slab_k8(l, "w_in", O_MO + half * 512, 512)
            for gp in range(NCH // G2):
                pb = ps_mm()
                for k in range(8):
                    S.pe(lambda e, k=k, pb=pb, gp=gp, v=v: e.matmul(pb.ap[0:TG, :], uT.ap[:, k, gp * TG:(gp + 1) * TG], v[:, k, :],
                                                                    start=(k == 0), stop=(k == 7)),
                         reads=[sl, uT[k]], writes=[pb])
                for gi in range(G2):
                    ch = gp * G2 + gi
                    t_ = sgt2[gi]
                    S.act(lambda e, pb=pb, gi=gi, t_=t_: e.activation(out=t_.ap[0:L, :], in_=pb.ap[gi * L:(gi + 1) * L, :], func=AF.Exp,
                                                                      scale=-1.0), reads=[pb], writes=[t_])
                    S.act(lambda e, t_=t_: e.activation(out=t_.ap[0:L, :], in_=t_.ap[0:L, :], func=AF.Ln, bias=1.0), reads=[t_], writes=[t_])
                    S.act(lambda e, t_=t_: e.activation(out=t_.ap[0:L, :], in_=t_.ap[0:L, :], func=AF.Exp, scale=-1.0), reads=[t_],
                          writes=[t_])
                    S.pool(lambda e, ch=ch, half=half, t_=t_: e.tensor_tensor(out=sgm.ap[0:L, ch, half * 512:(half + 1) * 512],
                                                                              in0=t_.ap[0:L, :],
                                                                              in1=mg_row[l].ap[0:L, half * 512:(half + 1) * 512],
                                                                              op=ALU.mult),
                           reads=[t_, mg_row[l]], writes=[sgm[ch]])
        pc = aux(2)
        pcv = pc.ap[0:L, 0:NCH * 12].rearrange("p (c q) -> p c q", q=12)
        for ch in range(NCH):
            for qi, nm in enumerate(["ea", "cl", "iwt"]):
                S.pe(lambda e, ch=ch, qi=qi, nm=nm: e.transpose(pcv[:, ch, qi * 4:qi * 4 + 4], G[nm][:, ch * L:(ch + 1) * L],
                                                                 ident.ap[0:4, 0:4]),
                     reads=[grow[nm], ident], writes=[pc])
        S.dve(lambda e: e.tensor_copy(out=colq.ap[0:L, 0:NCH, :], in_=pcv), reads=[pc], writes=[colq])
        S.act(lambda e: e.activation(out=eab.ap[0:L, 0:NCH, :], in_=colq.ap[0:L, 0:NCH, 0:4], func=AF.Copy), reads=[colq],
              writes=[eab])
        S.dve(lambda e: e.tensor_tensor(out=rhsm.ap[:, :, 0:NCH], in0=Gs["iw"].unsqueeze(1).to_broadcast([4, 4, NCH]),
                                        in1=hmask.ap[:, :, 0:NCH], op=ALU.mult), reads=[gsm["iw"], hmask], writes=[rhsm])
        pw = aux(3)
        for h2 in range(4):
            S.pe(lambda e, h2=h2: e.matmul(pw.ap[:, h2 * NCH:(h2 + 1) * NCH], ones32.ap, rhsm.ap[:, h2, 0:NCH],
                                           start=True, stop=True), reads=[ones32, rhsm], writes=[pw])
        S.dve(lambda e: e.tensor_copy(out=iw_rep.ap[:, 0:4 * NCH], in_=pw.ap[:, 0:4 * NCH]), reads=[pw], writes=[iw_rep])

        for half in range(2):
            sl, v = slab_k8(l, "w_in", O_MV + half * 512, 512)
            for gp in range(NCH // G2):
                pb = ps_mm()
                for k in range(8):
                    S.pe(lambda e, k=k, pb=pb, gp=gp, v=v: e.matmul(pb.ap[0:TG, :], uT.ap[:, k, gp * TG:(gp + 1) * TG], v[:, k, :],
                                                                    start=(k == 0), stop=(k == 7)),
                         reads=[sl, uT[k]], writes=[pb])
                for gi in range(G2):
                    ch = gp * G2 + gi
                    copy_any(vw.ap[0:L, ch, half * 2:half * 2 + 2, :],
                             pb.ap[gi * L:(gi + 1) * L, :].rearrange("p (h e) -> p h e", h=2), [pb], [vw[ch]])
        for ch in range(NCH):
            S.dve(lambda e, ch=ch: e.tensor_tensor(out=vw.ap[0:L, ch], in0=vw.ap[0:L, ch],
                                                   in1=colq.ap[0:L, ch, 0:4].unsqueeze(2).to_broadcast([L, 4, 256]), op=ALU.mult),
                  reads=[vw[ch], colq], writes=[vw[ch]])
        def state_copies(h, chn):
            iwn = iw_rep.ap[:, h * NCH + chn:h * NCH + chn + 1]
            S.act(lambda e: e.activation(out=Cnb4[h].ap, in_=Cst[l].ap[:, h, :, :], func=AF.Copy, scale=iwn),
                  reads=[Cst[l][h], iw_rep], writes=[Cnb4[h]])
            S.act(lambda e: e.activation(out=nb4[h].ap, in_=nst[l].ap[:, h, :], func=AF.Copy, scale=iwn),
                  reads=[nst[l], iw_rep], writes=[nb4[h]])

        for h in range(4):
            state_copies(h, 0)

        def cbody(ch):
            cs = slice(ch * L, (ch + 1) * L)
            j = ch % 2
            ps_s = aux(0)
            sv = ps_s.ap[0:L, 0:4 * L].rearrange("p (h t) -> p h t", h=4)
            for h in range(4):
                for dc in range(2):
                    S.pe(lambda e, h=h, dc=dc, sv=sv, cs=cs: e.matmul(sv[:, h, :], kT.ap[:, 2 * h + dc, cs],
                                                                      qT.ap[:, 2 * h + dc, cs], start=(dc == 0), stop=(dc == 1)),
                         reads=[kT[2 * h + dc], qT[2 * h + dc]], writes=[ps_s])
            sm = smask[j]
            S.dve(lambda e, sm=sm, sv=sv: e.tensor_tensor(out=sm.ap[0:L, :, 0:L], in0=sv,
                                                          in1=cmask.ap[0:L, 0:L].unsqueeze(1).to_broadcast([L, 4, L]),
                                                          op=ALU.mult), reads=[ps_s, cmask], writes=[sm])
            yield
            po = [aux(1), aux(2)]
            pd = aux(3)
            for h in range(4):
                cb, nb_ = Cnb4[h], nb4[h]
                iwc = iw_rep.ap[:, h * NCH + ch:h * NCH + ch + 1]
                pov = po[h // 2].ap[0:L, (h % 2) * 256:(h % 2 + 1) * 256]
                S.pe(lambda e, pov=pov, sm=sm, h=h, ch=ch: e.matmul(pov, sm.ap[0:L, h, 0:L], vw.ap[0:L, ch, h, :],
                                                                    start=True, stop=False),
                     reads=[sm, vw[ch]], writes=[po[h // 2]])
                for dc in range(2):
                    S.pe(lambda e, pov=pov, h=h, dc=dc, cs=cs, cb=cb: e.matmul(pov, qT.ap[:, 2 * h + dc, cs], cb.ap[:, dc, :],
                                                                               start=False, stop=(dc == 1)),
                         reads=[qT[2 * h + dc], cb], writes=[po[h // 2]])
                pdv = pd.ap[0:L, h:h + 1]
                S.pe(lambda e, pdv=pdv, sm=sm, h=h, ch=ch: e.matmul(pdv, sm.ap[0:L, h, 0:L], eab.ap[0:L, ch, h:h + 1],
                                                                    start=True, stop=False),
                     reads=[sm, eab], writes=[pd])
                for dc in range(2):
                    S.pe(lambda e, pdv=pdv, h=h, dc=dc, cs=cs, nb_=nb_: e.matmul(pdv, qT.ap[:, 2 * h + dc, cs],
                                                                                 nb_.ap[:, dc:dc + 1], start=False,
                                                                                 stop=(dc == 1)),
                         reads=[qT[2 * h + dc], nb_], writes=[pd])
                pdl = aux(4)
                for dc in range(2):
                    S.pe(lambda e, pdl=pdl, h=h, dc=dc, ch=ch: e.matmul(pdl.ap[:, dc * 256:(dc + 1) * 256],
                                                                        ktok.ap[0:L, ch, h * 256 + dc * 128:h * 256 + dc * 128 + 128],
                                                                        vw.ap[0:L, ch, h, :], start=True, stop=True),
                         reads=[ktok[ch], vw[ch]], writes=[pdl])
                    S.pe(lambda e, h=h, dc=dc, ch=ch: e.matmul(banks[h % 2].ap[:, dc:dc + 1],
                                                                      ktok.ap[0:L, ch, h * 256 + dc * 128:h * 256 + dc * 128 + 128],
                                                                      eab.ap[0:L, ch, h:h + 1], start=True, stop=True),
                         reads=[ktok[ch], eab], writes=[banks[h % 2]])
                S.dve(lambda e, pdl=pdl, h=h, iwc=iwc: e.scalar_tensor_tensor(
                    out=Cst[l].ap[:, h, :, :].rearrange("p a b -> p (a b)"), in0=Cst[l].ap[:, h, :, :].rearrange("p a b -> p (a b)"),
                    scalar=iwc, in1=pdl.ap[:, 0:512], op0=ALU.mult, op1=ALU.add),
                      reads=[Cst[l][h], pdl, iw_rep], writes=[Cst[l][h]])
                S.dve(lambda e, pd=pd, h=h, iwc=iwc: e.scalar_tensor_tensor(
                    out=nst[l].ap[:, h, :], in0=nst[l].ap[:, h, :], scalar=iwc, in1=banks[h % 2].ap[:, 0:2],
                    op0=ALU.mult, op1=ALU.add), reads=[nst[l], banks[h % 2], iw_rep], writes=[nst[l]])
                if ch + 1 < NCH:
                    state_copies(h, ch + 1)
            yield
            ds_, dt_, hn_, ss_ = den_s[j], den_t[j], hn[j], ssm[j]
            S.act(lambda e, ds_=ds_, pd=pd: e.activation(out=ds_.ap[0:L, :], in_=pd.ap[0:L, 0:4], func=AF.Copy), reads=[pd],
                  writes=[ds_])
            S.dve(lambda e, ds_=ds_, dt_=dt_: e.scalar_tensor_tensor(out=dt_.ap[0:L, :], in0=ds_.ap[0:L, :], scalar=-1.0,
                                                                     in1=ds_.ap[0:L, :], op0=ALU.mult, op1=ALU.max),
                  reads=[ds_], writes=[dt_])
            S.dve(lambda e, dt_=dt_, ch=ch: e.tensor_tensor(out=dt_.ap[0:L, :], in0=dt_.ap[0:L, :], in1=colq.ap[0:L, ch, 4:8],
                                                            op=ALU.max), reads=[dt_, colq], writes=[dt_])
            S.dve(lambda e, dt_=dt_: e.reciprocal(out=dt_.ap[0:L, :], in_=dt_.ap[0:L, :]), reads=[dt_], writes=[dt_])
            for hp in range(2):
                S.dve(lambda e, hp=hp, hn_=hn_, dt_=dt_: e.tensor_tensor(
                    out=hn_.ap[0:L, 2 * hp:2 * hp + 2, :], in0=po[hp].ap[0:L, :].rearrange("p (h d) -> p h d", h=2),
                    in1=dt_.ap[0:L, 2 * hp:2 * hp + 2].unsqueeze(2).to_broadcast([L, 2, 256]), op=ALU.mult),
                      reads=[po[hp], dt_], writes=[hn_])
            S.act(lambda e, hn_=hn_: e.activation(out=hsq.ap[0:L], in_=hn_.ap[0:L], func=AF.Square), reads=[hn_], writes=[hsq])
            S.dve(lambda e, ss_=ss_: e.tensor_reduce(out=ss_.ap[0:L, :], in_=hsq.ap[0:L], axis=AX.X, op=ALU.add), reads=[hsq],
                  writes=[ss_])
            S.act(lambda e, ss_=ss_: e.activation(out=ss_.ap[0:L, :], in_=ss_.ap[0:L, :], func=AF.Sqrt, scale=1.0 / 256, bias=EPS),
                  reads=[ss_], writes=[ss_])
            S.dve(lambda e, ss_=ss_: e.reciprocal(out=ss_.ap[0:L, :], in_=ss_.ap[0:L, :]), reads=[ss_], writes=[ss_])
            S.dve(lambda e, hn_=hn_, ss_=ss_: e.tensor_tensor(out=hn_.ap[0:L], in0=hn_.ap[0:L],
                                                              in1=ss_.ap[0:L, :].unsqueeze(2).to_broadcast([L, 4, 256]),
                                                              op=ALU.mult), reads=[hn_, ss_], writes=[hn_])
            S.pool(lambda e, hn_=hn_, ch=ch: e.tensor_tensor(out=hmtok.ap[0:L, ch, :], in0=hn_.ap[0:L].rearrange("p h d -> p (h d)"),
                                                             in1=sgm.ap[0:L, ch, :], op=ALU.mult),
                   reads=[hn_, sgm[ch]], writes=[hmtok[ch]])
            yield
            pt = aux(5)
            ptv = pt.ap.bitcast(BF16)[:, 0:8 * L].rearrange("p (c t) -> p c t", c=8)
            for c in range(8):
                S.pe(lambda e, c=c, ptv=ptv, ch=ch: e.transpose(ptv[:, c, :], hmtok.ap[0:L, ch, c * 128:(c + 1) * 128],
                                                                 identb.ap[0:L, 0:L]),
                     reads=[hmtok[ch], identb], writes=[pt])
            copy_any(hmT.ap[:, :, cs], ptv, [pt], [hmT])

        pipeline([cbody(ch) for ch in range(NCH)], newest_first=False)
        merge_branch(l, 1, hmT, Tt)

    def attn_phase(l, Tt, L, first_tile, is_sample, last_tile, grp):
        NCH = Tt // L
        Tb = min(Tt, 128)
        NB = Tt // Tb
        slkv, vkv = slab_k8(l, "w_in", O_AK, 256)

        def kbody(ch):
            j = ch % 2
            slot = 2 + ch
            pb = ps_mm()
            for k in range(8):
                S.pe(lambda e, k=k, pb=pb, ch=ch: e.matmul(pb.ap[0:L, 0:256], uT.ap[:, k, ch * L:(ch + 1) * L], vkv[:, k, :],
                                                           start=(k == 0), stop=(k == 7)), reads=[slkv, uT[k]], writes=[pb])
            vf_ = vf[j]
            S.act(lambda e, pb=pb, vf_=vf_: e.activation(out=vf_.ap[0:L, :], in_=pb.ap[0:L, 128:256], func=AF.Copy), reads=[pb],
                  writes=[vf_])
            S.dve(lambda e, vf_=vf_, slot=slot: e.tensor_copy(out=vaug[l].ap[0:L, slot, :, 0:64],
                                                              in_=vf_.ap[0:L, :].rearrange("p (k d) -> p k d", k=2)),
                  reads=[vf_], writes=[vaug[l]])
            S.dve(lambda e, vf_=vf_, slot=slot: e.tensor_copy(out=vaug[l].ap[0:L, slot, :, 128:192],
                                                              in_=vf_.ap[0:L, :].rearrange("p (k d) -> p k d", k=2)),
                  reads=[vf_], writes=[vaug[l]])
            ks_, kn_, kr_, krb_ = kss[j], kn[j], kr[j], krb[j]
            S.act(lambda e, pb=pb: e.activation(out=ksq.ap[0:L, :], in_=pb.ap[0:L, 0:128], func=AF.Square), reads=[pb], writes=[ksq])
            S.dve(lambda e, ks_=ks_: e.tensor_reduce(out=ks_.ap[0:L, :], in_=ksq.ap[0:L, :].rearrange("p (k d) -> p k d", k=2),
                                                     axis=AX.X, op=ALU.add), reads=[ksq], writes=[ks_])
            S.act(lambda e, ks_=ks_: e.activation(out=ks_.ap[0:L, :], in_=ks_.ap[0:L, :], func=AF.Sqrt, scale=1.0 / 64, bias=EPS),
                  reads=[ks_], writes=[ks_])
            S.dve(lambda e, ks_=ks_: e.reciprocal(out=ks_.ap[0:L, :], in_=ks_.ap[0:L, :]), reads=[ks_], writes=[ks_])
            for kv in range(2):
                S.dve(lambda e, kv=kv, pb=pb, ks_=ks_, kn_=kn_: e.scalar_tensor_tensor(
                    out=kn_.ap[0:L, kv, :], in0=pb.ap[0:L, kv * 64:(kv + 1) * 64], scalar=ks_.ap[0:L, kv:kv + 1],
                    in1=kg_row[l].ap[0:L, :], op0=ALU.mult, op1=ALU.mult), reads=[pb, ks_, kg_row[l]], writes=[kn_])
            cosb = ctab_k.ap[0:L, ch, :].unsqueeze(1).to_broadcast([L, 2, 32])
            sinb = stab_k.ap[0:L, ch, :].unsqueeze(1).to_broadcast([L, 2, 32])
            k1 = kn_.ap[0:L, :, 0:32]
            k2 = kn_.ap[0:L, :, 32:64]
            krv = kr_.ap[0:L, :].rearrange("p (k d) -> p k d", k=2)
            S.dve(lambda e, k1=k1, cosb=cosb: e.tensor_tensor(out=kt1.ap[0:L], in0=k1, in1=cosb, op=ALU.mult),
                  reads=[kn_, ctab_k], writes=[kt1])
            S.dve(lambda e, k2=k2, sinb=sinb: e.tensor_tensor(out=kt2.ap[0:L], in0=k2, in1=sinb, op=ALU.mult),
                  reads=[kn_, stab_k], writes=[kt2])
            S.dve(lambda e, krv=krv: e.tensor_tensor(out=krv[:, :, 0:32], in0=kt1.ap[0:L], in1=kt2.ap[0:L], op=ALU.subtract),
                  reads=[kt1, kt2], writes=[kr_])
            S.dve(lambda e, k2=k2, cosb=cosb: e.tensor_tensor(out=kt1.ap[0:L], in0=k2, in1=cosb, op=ALU.mult),
                  reads=[kn_, ctab_k], writes=[kt1])
            S.dve(lambda e, k1=k1, sinb=sinb: e.tensor_tensor(out=kt2.ap[0:L], in0=k1, in1=sinb, op=ALU.mult),
                  reads=[kn_, stab_k], writes=[kt2])
            S.dve(lambda e, krv=krv: e.tensor_tensor(out=krv[:, :, 32:64], in0=kt1.ap[0:L], in1=kt2.ap[0:L], op=ALU.add),
                  reads=[kt1, kt2], writes=[kr_])
            S.act(lambda e, kr_=kr_, krb_=krb_: e.activation(out=krb_.ap[0:L, :], in_=kr_.ap[0:L, :], func=AF.Copy),
                  reads=[kr_], writes=[krb_])
            yield
            pk = aux(0)
            pkv = pk.ap.bitcast(BF16)[0:64, 0:2 * L].rearrange("p (k t) -> p k t", k=2)
            for kv in range(2):
                S.pe(lambda e, kv=kv, pkv=pkv, krb_=krb_: e.transpose(pkv[:, kv, :], krb_.ap[0:L, kv * 64:(kv + 1) * 64],
                                                                      identb.ap[0:L, 0:L]), reads=[krb_, identb], writes=[pk])
            S.dve(lambda e, pkv=pkv, slot=slot: e.tensor_copy(out=KTwin[l].ap[:, :, slot * 64:slot * 64 + L], in_=pkv),
                  reads=[pk], writes=[KTwin[l]])
            if is_sample:
                S.dma("sp", lambda e, kr_=kr_: e.dma_start(out=O["k_s"][l, 128 - L:128].rearrange("r k d -> r (k d)"),
                                                           in_=kr_.ap[0:L, :]), kr_, reads=[kr_])
                S.dma("sp", lambda e, vf_=vf_: e.dma_start(out=O["v_s"][l, 128 - L:128].rearrange("r k d -> r (k d)"),
                                                           in_=vf_.ap[0:L, :]), vf_, reads=[vf_])
            elif last_tile and ch >= NCH - 2:
                r0 = (ch - (NCH - 2)) * 64
                S.dma("sp", lambda e, kr_=kr_, r0=r0: e.dma_start(out=O["k_p"][l, r0:r0 + 64].rearrange("r k d -> r (k d)"),
                                                                  in_=kr_.ap[0:L, :]), kr_, reads=[kr_])
                S.dma("sp", lambda e, vf_=vf_, r0=r0: e.dma_start(out=O["v_p"][l, r0:r0 + 64].rearrange("r k d -> r (k d)"),
                                                                  in_=vf_.ap[0:L, :]), vf_, reads=[vf_])

        pipeline([kbody(ch) for ch in range(NCH)], newest_first=True)
        qsl = {}

        def qbody(half, b):
            if True:
                if b == 0:
                    qsl["q"] = slab_k8(l, "w_in", O_AQ + half * 512, 512)
                slq, vq = qsl["q"]
                j = (half * NB + b) % 2
                pb = ps_mm()
                for k in range(8):
                    S.pe(lambda e, k=k, pb=pb, b=b, vq=vq: e.matmul(pb.ap[0:Tb, :], uT.ap[:, k, b * Tb:(b + 1) * Tb], vq[:, k, :],
                                                                    start=(k == 0), stop=(k == 7)), reads=[slq, uT[k]], writes=[pb])
                qs_, qn_, qr_ = qss[j], qn[j], qr[j]
                S.act(lambda e, pb=pb: e.activation(out=qsq.ap[0:Tb, :], in_=pb.ap[0:Tb, :], func=AF.Square), reads=[pb], writes=[qsq])
                S.dve(lambda e, qs_=qs_: e.tensor_reduce(out=qs_.ap[0:Tb, :], in_=qsq.ap[0:Tb, :].rearrange("p (h d) -> p h d", h=8),
                                                         axis=AX.X, op=ALU.add), reads=[qsq], writes=[qs_])
                S.act(lambda e, qs_=qs_: e.activation(out=qs_.ap[0:Tb, :], in_=qs_.ap[0:Tb, :], func=AF.Sqrt, scale=1.0 / 64, bias=EPS),
                      reads=[qs_], writes=[qs_])
                S.dve(lambda e, qs_=qs_: e.reciprocal(out=qs_.ap[0:Tb, :], in_=qs_.ap[0:Tb, :]), reads=[qs_], writes=[qs_])
                S.dve(lambda e, pb=pb, qs_=qs_, qn_=qn_: e.tensor_tensor(
                    out=qn_.ap[0:Tb], in0=pb.ap[0:Tb, :].rearrange("p (h d) -> p h d", h=8),
                    in1=qs_.ap[0:Tb, :].unsqueeze(2).to_broadcast([Tb, 8, 64]), op=ALU.mult), reads=[pb, qs_], writes=[qn_])
                S.dve(lambda e, qn_=qn_: e.tensor_tensor(out=qn_.ap[0:Tb], in0=qn_.ap[0:Tb],
                                                         in1=qg_row[l].ap[0:Tb, :].unsqueeze(1).to_broadcast([Tb, 8, 64]),
                                                         op=ALU.mult), reads=[qn_, qg_row[l]], writes=[qn_])
                cosb = ctab_q.ap[0:Tb, b, :].unsqueeze(1).to_broadcast([Tb, 8, 32])
                sinb = stab_q.ap[0:Tb, b, :].unsqueeze(1).to_broadcast([Tb, 8, 32])
                q1 = qn_.ap[0:Tb, :, 0:32]
                q2 = qn_.ap[0:Tb, :, 32:64]
                S.dve(lambda e, q1=q1, cosb=cosb: e.tensor_tensor(out=qt1.ap[0:Tb], in0=q1, in1=cosb, op=ALU.mult),
                      reads=[qn_, ctab_q], writes=[qt1])
                S.dve(lambda e, q2=q2, sinb=sinb: e.tensor_tensor(out=qt2.ap[0:Tb], in0=q2, in1=sinb, op=ALU.mult),
                      reads=[qn_, stab_q], writes=[qt2])
                S.dve(lambda e, qr_=qr_: e.tensor_tensor(out=qr_.ap[0:Tb, :, 0:32], in0=qt1.ap[0:Tb], in1=qt2.ap[0:Tb],
                                                         op=ALU.subtract), reads=[qt1, qt2], writes=[qr_])
                S.dve(lambda e, q2=q2, cosb=cosb: e.tensor_tensor(out=qt1.ap[0:Tb], in0=q2, in1=cosb, op=ALU.mult),
                      reads=[qn_, ctab_q], writes=[qt1])
                S.dve(lambda e, q1=q1, sinb=sinb: e.tensor_tensor(out=qt2.ap[0:Tb], in0=q1, in1=sinb, op=ALU.mult),
                      reads=[qn_, stab_q], writes=[qt2])
                S.dve(lambda e, qr_=qr_: e.tensor_tensor(out=qr_.ap[0:Tb, :, 32:64], in0=qt1.ap[0:Tb], in1=qt2.ap[0:Tb],
                                                         op=ALU.add), reads=[qt1, qt2], writes=[qr_])
                yield
                pq = aux(1)
                pqv = pq.ap.bitcast(BF16)[0:64, 0:8 * Tb].rearrange("p (h t) -> p h t", h=8)
                for h in range(8):
                    S.pe(lambda e, h=h, pqv=pqv, qr_=qr_: e.transpose(pqv[:, h, :], qr_.ap[0:Tb, h, :], identb.ap[0:Tb, 0:Tb]),
                         reads=[qr_, identb], writes=[pq])
                copy_any(QT_all.ap[:, half * 8:(half + 1) * 8, b * Tb:(b + 1) * Tb], pqv, [pq],
                         [QT_all[half * 8 + h] for h in range(8)])

        pipeline([qbody(half, b) for half in range(2) for b in range(NB)], newest_first=True)

        def abody(ch, kv):
            slots = [(ch, 64), (ch + 1, 64), (ch + 2, L)]
            if first_tile and not is_sample:
                slots = [(s, n) for (s, n) in slots if s >= 2]
            qs = slice(ch * L, (ch + 1) * L)
            if True:
                ppv = aux(kv)
                pts = []
                for si_, (s, nk) in enumerate(slots):
                    pss = aux(2 + si_)
                    S.pe(lambda e, pss=pss, s=s, nk=nk, kv=kv, qs=qs: e.matmul(
                        pss.ap[0:nk, 0:8 * L].rearrange("p (g t) -> p g t", g=8), KTwin[l].ap[:, kv, s * 64:s * 64 + nk],
                        QT_all.ap[:, kv * 8:(kv + 1) * 8, qs], start=True, stop=True),
                         reads=[KTwin[l]] + [QT_all[kv * 8 + g] for g in range(8)], writes=[pss])
                    p_ = pT[pT_i[0] % 6]
                    pT_i[0] += 1
                    S.act(lambda e, p_=p_, pss=pss, nk=nk: e.activation(out=p_.ap[0:nk, 0:8 * L], in_=pss.ap[0:nk, 0:8 * L],
                                                                        func=AF.Exp, scale=0.125), reads=[pss], writes=[p_])
                    pts.append((p_, s, nk))
                yield
                H4 = 4 * L
                for par in range(2):
                    for i, (p_, s, nk) in enumerate(pts):
                        pv4 = p_.ap[0:nk, 0:8 * L].rearrange("p (g two t) -> p g two t", two=2, t=L)
                        S.pe(lambda e, pv4=pv4, s=s, nk=nk, i=i, par=par: e.matmul(
                            ppv.ap[:, par * H4:(par + 1) * H4].rearrange("p (g t) -> p g t", g=4),
                            vaug[l].ap[0:nk, s, kv, par * 64:par * 64 + 128], pv4[:, :, par, :],
                            start=(i == 0), stop=(i == len(pts) - 1)), reads=[vaug[l], p_], writes=[ppv])
                ds_ = dsum[kv]
                es4 = esink[l].ap[:, kv * 8:(kv + 1) * 8].rearrange("p (g two) -> p g two", two=2)
                S.dve(lambda e: e.tensor_tensor(out=ds_.ap[64:128, 0:H4].rearrange("p (g t) -> p g t", g=4),
                                                in0=ppv.ap[64:128, 0:H4].rearrange("p (g t) -> p g t", g=4),
                                                in1=es4[64:128, :, 0].unsqueeze(2).to_broadcast([64, 4, L]), op=ALU.add),
                      reads=[ppv, esink[l]], writes=[ds_])
                S.act(lambda e: e.activation(out=ds_.ap[0:64, 0:H4], in_=ds_.ap[64:128, 0:H4], func=AF.Ln), reads=[ds_], writes=[ds_])
                S.act(lambda e: e.activation(out=ds_.ap[0:64, 0:H4], in_=ds_.ap[0:64, 0:H4], func=AF.Exp, scale=-1.0), reads=[ds_],
                      writes=[ds_])
                S.dve(lambda e: e.tensor_tensor(out=OT2.ap[0:64, kv * 4:(kv + 1) * 4, qs],
                                                in0=ppv.ap[0:64, 0:H4].rearrange("p (g t) -> p g t", g=4),
                                                in1=ds_.ap[0:64, 0:H4].rearrange("p (g t) -> p g t", g=4), op=ALU.mult),
                      reads=[ppv, ds_], writes=[OT2[kv * 4 + g] for g in range(4)])
                S.dve(lambda e: e.tensor_tensor(out=ds_.ap[0:64, H4:2 * H4].rearrange("p (g t) -> p g t", g=4),
                                                in0=ppv.ap[0:64, H4:2 * H4].rearrange("p (g t) -> p g t", g=4),
                                                in1=es4[0:64, :, 1].unsqueeze(2).to_broadcast([64, 4, L]), op=ALU.add),
                      reads=[ppv, esink[l]], writes=[ds_])
                S.act(lambda e: e.activation(out=ds_.ap[64:128, H4:2 * H4], in_=ds_.ap[0:64, H4:2 * H4], func=AF.Ln), reads=[ds_],
                      writes=[ds_])
                S.act(lambda e: e.activation(out=ds_.ap[64:128, H4:2 * H4], in_=ds_.ap[64:128, H4:2 * H4], func=AF.Exp, scale=-1.0),
                      reads=[ds_], writes=[ds_])
                S.dve(lambda e: e.tensor_tensor(out=OT2.ap[64:128, kv * 4:(kv + 1) * 4, qs],
                                                in0=ppv.ap[64:128, H4:2 * H4].rearrange("p (g t) -> p g t", g=4),
                                                in1=ds_.ap[64:128, H4:2 * H4].rearrange("p (g t) -> p g t", g=4), op=ALU.mult),
                      reads=[ppv, ds_], writes=[OT2[kv * 4 + g] for g in range(4)])

        pipeline([abody(ch, kv) for ch in range(NCH) for kv in range(2)], newest_first=True)
        if not is_sample and not last_tile:
            for i in range(2):
                S.dve(lambda e, i=i: e.tensor_copy(out=KTwin[l].ap[:, :, i * 64:(i + 1) * 64],
                                                   in_=KTwin[l].ap[:, :, (NCH + i) * 64:(NCH + i + 1) * 64]),
                      reads=[KTwin[l]], writes=[KTwin[l]])
                S.pool(lambda e, i=i: e.tensor_copy(out=vaug[l].ap[:, i, :, 0:64], in_=vaug[l].ap[:, NCH + i, :, 0:64]),
                       reads=[vaug[l]], writes=[vaug[l]])
                S.pool(lambda e, i=i: e.tensor_copy(out=vaug[l].ap[:, i, :, 128:192], in_=vaug[l].ap[:, NCH + i, :, 128:192]),
                       reads=[vaug[l]], writes=[vaug[l]])
        merge_branch(l, 2, OT2, Tt)

    def out_and_mlp(l, Tt):
        for c in range(8):
            S.act(lambda e, c=c: e.activation(out=uT.ap[:, c, 0:Tt], in_=mix.ap[:, c, 0:Tt], func=AF.Copy), reads=[mix[c]],
                  writes=[uT[c]])
        for half in range(2):
            sl, v = slab_k8(l, "w_out", half * 512, 512)
            for c4 in range(4):
                c = half * 4 + c4
                pb = ps_mm()
                fm_proj(pb, sl, v, c4 * 128, uT, Tt)
                S.dve(lambda e, pb=pb, c=c: e.tensor_tensor(out=xT.ap[:, c, 0:Tt], in0=pb.ap[:, 0:Tt], in1=xT.ap[:, c, 0:Tt],
                                                            op=ALU.add), reads=[pb, xT[c]], writes=[xT[c]])
        norm_to_u(l, "norm2_g", Tt)
        for s8 in range(8):
            sl, v = slab_k8(l, "w_up", s8 * 512, 512)
            for c4 in range(4):
                hc = s8 * 4 + c4
                pb = ps_mm()
                fm_proj(pb, sl, v, c4 * 128, uT, Tt)
                r_ = rl[hc % 2]
                S.act(lambda e, pb=pb, r_=r_: e.activation(out=r_.ap[:, 0:Tt], in_=pb.ap[:, 0:Tt], func=AF.Relu), reads=[pb],
                      writes=[r_])
                hp_ = hid_parts[hc // 8]
                S.dve(lambda e, r_=r_, hc=hc, hp_=hp_: e.tensor_tensor(out=hp_.ap[:, hc % 8, 0:Tt], in0=r_.ap[:, 0:Tt],
                                                                       in1=r_.ap[:, 0:Tt], op=ALU.mult),
                      reads=[r_], writes=[hp_[hc % 8]])
        for c in range(8):
            src = SCR["w_down"][l][:, c * 128:(c + 1) * 128].rearrange("(kc p) n -> p kc n", p=128)
            sl, v = load_slab(src, lambda a: a.rearrange("p (k n) -> p k n", k=32), SCRB[("w_down", l)])
            pb = ps_mm()
            for k in range(32):
                hp_ = hid_parts[k // 8]
                S.pe(lambda e, k=k, pb=pb, v=v, hp_=hp_: e.matmul(pb.ap[:, 0:Tt], v[:, k, :], hp_.ap[:, k % 8, 0:Tt],
                                                                  start=(k == 0), stop=(k == 31)),
                     reads=[sl, hp_[k % 8]], writes=[pb])
            S.dve(lambda e, pb=pb, c=c: e.tensor_tensor(out=xT.ap[:, c, 0:Tt], in0=pb.ap[:, 0:Tt], in1=xT.ap[:, c, 0:Tt],
                                                        op=ALU.add), reads=[pb, xT[c]], writes=[xT[c]])

    def load_x(src, Tt):
        Tb = min(Tt, 128)
        for b in range(Tt // Tb):
            xi = xin[b % 2]
            S.dma("sp", lambda e, xi=xi, b=b: e.dma_start(out=xi.ap[0:Tb, :], in_=src[b * Tb:(b + 1) * Tb, :]), xi, writes=[xi])
            for g4 in range(2):
                pb = aux(g4)
                for c4 in range(4):
                    c = g4 * 4 + c4
                    S.pe(lambda e, pb=pb, c=c, c4=c4, xi=xi: e.transpose(pb.ap[:, c4 * Tb:(c4 + 1) * Tb],
                                                                         xi.ap[0:Tb, c * 128:(c + 1) * 128], ident.ap[0:Tb, 0:Tb]),
                         reads=[xi, ident], writes=[pb])
                copy_any(xT.ap[:, g4 * 4:(g4 + 1) * 4, b * Tb:(b + 1) * Tb],
                         pb.ap[:, 0:4 * Tb].rearrange("p (c t) -> p c t", c=4), [pb], [xT[g4 * 4 + i] for i in range(4)])

    def store_y(dst, Tt):
        Tb = min(Tt, 128)
        for b in range(Tt // Tb):
            yo = yout[b % 2]
            for g4 in range(2):
                pb = aux(g4)
                for c4 in range(4):
                    c = g4 * 4 + c4
                    S.pe(lambda e, pb=pb, c=c, c4=c4, b=b: e.transpose(pb.ap[0:Tb, c4 * 128:(c4 + 1) * 128],
                                                                       xT.ap[:, c, b * Tb:(b + 1) * Tb], ident.ap),
                         reads=[xT[c], ident], writes=[pb])
                copy_any(yo.ap[0:Tb, g4 * 512:(g4 + 1) * 512], pb.ap[0:Tb, :], [pb], [yo])
            S.dma("sp", lambda e, yo=yo, b=b: e.dma_start(out=dst[b * Tb:(b + 1) * Tb, :], in_=yo.ap[0:Tb, :]), yo, reads=[yo])

    def load_rope(pos0, Tt, L):
        NCH = Tt // L
        Tb = min(Tt, 128)
        NB = Tt // Tb
        for (tab, src) in [(ctab_k, rope_c), (stab_k, rope_s)]:
            S.dma("sp", lambda e, tab=tab, src=src: e.dma_start(
                out=tab.ap[0:L, 0:NCH, :], in_=src[pos0:pos0 + Tt, :].rearrange("(c l) f -> l c f", l=L)), tab, writes=[tab])
        for (tab, src) in [(ctab_q, rope_c), (stab_q, rope_s)]:
            S.dma("sp", lambda e, tab=tab, src=src: e.dma_start(
                out=tab.ap[0:Tb, 0:NB, :], in_=src[pos0:pos0 + Tt, :].rearrange("(c l) f -> l c f", l=Tb)), tab, writes=[tab])

    def cols_to_rows_store(src_ap, n, dsts, rd):
        pb = aux(5)
        S.pe(lambda e: e.transpose(pb.ap[0:n, 0:128], src_ap, ident.ap), reads=rd + [ident], writes=[pb])
        S.dve(lambda e: e.tensor_copy(out=vstage.ap[0:n, :], in_=pb.ap[0:n, 0:128]), reads=[pb], writes=[vstage])
        for (r0, r1, d) in dsts:
            S.dma("sp", lambda e, r0=r0, r1=r1, d=d: e.dma_start(out=d, in_=vstage.ap[r0:r1, :]), vstage, reads=[vstage])

    def rows_load_to_cols(srcs, n, dst_ap, wr):
        for (r0, r1, s_) in srcs:
            S.dma("sp", lambda e, r0=r0, r1=r1, s_=s_: e.dma_start(out=vstage.ap[r0:r1, :], in_=s_), vstage, writes=[vstage])
        pb = aux(5)
        S.pe(lambda e: e.transpose(pb.ap[:, 0:n], vstage.ap[0:n, :], ident.ap[0:n, 0:n]), reads=[vstage, ident], writes=[pb])
        S.dve(lambda e: e.tensor_copy(out=dst_ap, in_=pb.ap[:, 0:n]), reads=[pb], writes=wr)

    def init_states_zero(l):
        S.dve(lambda e: e.memset(hist[l].ap, 0.0), writes=[hist[l]])
        S.dve(lambda e: e.memset(hst[l].ap, 0.0), writes=[hst[l]])
        S.pool(lambda e: e.memset(Cst[l].ap, 0.0), writes=[Cst[l]])
        S.dve(lambda e: e.memset(nst[l].ap, 0.0), writes=[nst[l]])
        S.dve(lambda e: e.memset(mst[l].ap, 0.0), writes=[mst[l]])
        S.pool(lambda e: e.memset(vaug[l].ap, 1.0), writes=[vaug[l]])
        S.pool(lambda e: e.memset(KTwin[l].ap, 0.0), writes=[KTwin[l]])

    def init_states_sample(l):
        for (r0, r1, s_) in [(0, 24, st_conv[l].rearrange("j (c p) -> (j c) p", p=128)),
                             (24, 32, st_lru[l].rearrange("(c p) -> c p", p=128)),
                             (32, 40, st_n[l].rearrange("h (c p) -> (h c) p", p=128))]:
            S.dma("sp", lambda e, r0=r0, r1=r1, s_=s_: e.dma_start(out=vstage.ap[r0:r1, :], in_=s_), vstage, writes=[vstage])
        pb = aux(5)
        S.pe(lambda e: e.transpose(pb.ap[:, 0:40], vstage.ap[0:40, :], ident.ap[0:40, 0:40]), reads=[vstage, ident], writes=[pb])
        S.dve(lambda e: e.tensor_copy(out=hist[l].ap.rearrange("p j c -> p (j c)"), in_=pb.ap[:, 0:24]), reads=[pb], writes=[hist[l]])
        S.dve(lambda e: e.tensor_copy(out=hst[l].ap, in_=pb.ap[:, 24:32]), reads=[pb], writes=[hst[l]])
        S.dve(lambda e: e.tensor_copy(out=nst[l].ap.rearrange("p h c -> p (h c)"), in_=pb.ap[:, 32:40]), reads=[pb], writes=[nst[l]])
        S.dma("sp", lambda e: e.dma_start(out=Cst[l].ap, in_=st_C[l].rearrange("h (c p) e -> p h c e", p=128)), Cst[l],
              writes=[Cst[l]])
        S.dma("sp", lambda e: e.dma_start(out=mst[l].ap, in_=st_m[l].rearrange("(h o) -> h o", o=1)), mst[l], writes=[mst[l]])
        S.pool(lambda e: e.memset(vaug[l].ap, 1.0), writes=[vaug[l]])
        for s2 in range(2):
            S.dma("pool", lambda e, s2=s2: e.dma_start(out=vaug[l].ap[:, s2, :, 0:64], in_=c_v[l, s2 * 64:(s2 + 1) * 64]),
                  vaug[l], writes=[vaug[l]])
        S.dve(lambda e: e.tensor_copy(out=vaug[l].ap[:, 0:2, :, 128:192], in_=vaug[l].ap[:, 0:2, :, 0:64]), reads=[vaug[l]],
              writes=[vaug[l]])
        S.dma("sp", lambda e: e.dma_start(out=kcs.ap, in_=c_k[l].rearrange("(s r) k d -> r s (k d)", s=2)), kcs, writes=[kcs])
        S.dve(lambda e: e.tensor_copy(out=kcb.ap, in_=kcs.ap), reads=[kcs], writes=[kcb])
        pk = aux(4)
        pkv = pk.ap.bitcast(BF16)[0:64, 0:256].rearrange("p (k t) -> p k t", k=2)
        for s in range(2):
            for kv in range(2):
                S.pe(lambda e, s=s, kv=kv: e.transpose(pkv[:, kv, s * 64:(s + 1) * 64], kcb.ap[:, s, kv * 64:(kv + 1) * 64],
                                                       identb.ap[0:64, 0:64]), reads=[kcb, identb], writes=[pk])
        S.dve(lambda e: e.tensor_copy(out=KTwin[l].ap[:, :, 0:128], in_=pkv), reads=[pk], writes=[KTwin[l]])
        for (o_, c_) in [(O["k_s"], c_k), (O["v_s"], c_v)]:
            S.dma("sp", lambda e, o_=o_, c_=c_: e.dma_start(out=tailk.ap[0:64, :], in_=c_[l, TS:TS + 64].rearrange("r k d -> r (k d)")),
                  tailk, writes=[tailk])
            S.dma("sp", lambda e, o_=o_: e.dma_start(out=o_[l, 0:64].rearrange("r k d -> r (k d)"), in_=tailk.ap[0:64, :]),
                  tailk, reads=[tailk])
            n2 = 128 - TS - 64
            S.dma("sp", lambda e, o_=o_, c_=c_: e.dma_start(out=tailk.ap[0:n2, :],
                                                            in_=c_[l, TS + 64:128].rearrange("r k d -> r (k d)")),
                  tailk, writes=[tailk])
            S.dma("sp", lambda e, o_=o_: e.dma_start(out=o_[l, 64:64 + n2].rearrange("r k d -> r (k d)"), in_=tailk.ap[0:n2, :]),
                  tailk, reads=[tailk])

    def store_states(l, g):
        cols_to_rows_store(hist[l].ap.rearrange("p j c -> p (j c)"), 24,
                           [(0, 24, O["conv_" + g][l].rearrange("j (c p) -> (j c) p", p=128))], [hist[l]])
        cols_to_rows_store(hst[l].ap, 8, [(0, 8, O["lru_" + g][l].rearrange("(c p) -> c p", p=128))], [hst[l]])
        cols_to_rows_store(nst[l].ap.rearrange("p h c -> p (h c)"), 8,
                           [(0, 8, O["n_" + g][l].rearrange("h (c p) -> (h c) p", p=128))], [nst[l]])
        S.dma("sp", lambda e: e.dma_start(out=O["C_" + g][l].rearrange("h (c p) e -> p h c e", p=128), in_=Cst[l].ap), Cst[l],
              reads=[Cst[l]])
        S.dma("sp", lambda e: e.dma_start(out=O["m_" + g][l].rearrange("(h o) -> h o", o=1), in_=mst[l].ap), mst[l],
              reads=[mst[l]])

    FL = [64]

    def mark(name):
        PHASES.append((name, len(S.ops["pe"])))

    def run_tile(src, dst, pos_idx, Tt, L, first_tile, last_tile, is_sample, g):
        mark("load")
        FL[0] = L
        load_rope(pos_idx, Tt, L)
        load_x(src, Tt)
        for l in range(DEPTH):
            mark("norm1")
            norm_to_u(l, "norm1_g", Tt)
            mark("lru")
            lru_phase(l, Tt)
            mark("mlstm")
            mlstm_phase(l, Tt, L)
            mark("attn")
            attn_phase(l, Tt, L, first_tile, is_sample, last_tile, g)
            mark("mlp")
            out_and_mlp(l, Tt)
            if last_tile:
                store_states(l, g)
        mark("store")
        store_y(dst, Tt)

    for l in range(DEPTH):
        init_states_zero(l)
    for t in range(NT):
        run_tile(x_p[t * T:(t + 1) * T, :], O["y_p"][t * T:(t + 1) * T, :], t * T, T, 64, t == 0, t == NT - 1, False, "p")
    if SAMPLE:
        for l in range(DEPTH):
            init_states_sample(l)
        run_tile(x_s, O["y_s"], SP, TS, TS, True, True, True, "s")
    stats = S.emit()
    return nc, stats


PHASES = []
CFG = dict(NT=16, T=256, DEPTH=2)
_cache = {}


def rope_tables(npos_list):
    half = 32
    inv = (10000.0 ** (-np.arange(half, dtype=np.float32) / half)).astype(np.float32)
    pos = np.asarray(npos_list, dtype=np.float32)
    ang = pos[:, None] * inv[None, :]
    return np.cos(ang).astype(np.float32), np.sin(ang).astype(np.float32)


def run(inputs, NT, T, DEPTH, n_cores, past_len=PAST_LEN):
    key = (NT, T, DEPTH)
    if key not in _cache:
        _cache[key] = build(NT, T, DEPTH, True)
    nc, stats = _cache[key]
    SPp = NT * T
    TS = 16
    pos = list(range(SPp)) + [past_len + i for i in range(TS)]
    rc, rs = rope_tables(pos)
    wnames = ["norm1_g", "w_in", "conv_w", "conv_b", "lru_wa", "lru_ba", "lru_wx", "lru_bx", "lru_lam", "m_bi", "m_bf",
              "m_norm_g", "qn_g", "kn_g", "sinks", "w_oa", "w_ob", "w_oc", "b_gate", "w_out", "norm2_g", "w_up", "w_down"]
    f = lambda a: np.ascontiguousarray(np.asarray(a, dtype=np.float32))
    wd = {k: f(inputs[k])[:DEPTH] for k in wnames}
    in_maps = []
    for b in range(n_cores):
        m = dict(wd)
        m["x_p"] = f(inputs["x_prompt"][b, :SPp])
        m["x_s"] = f(inputs["x_sample"][b])
        m["st_conv"] = f(inputs["state_conv"][:DEPTH, b])
        m["st_lru"] = f(inputs["state_lru"][:DEPTH, b])
        m["st_C"] = f(inputs["state_mlstm_C"][:DEPTH, b])
        m["st_n"] = f(inputs["state_mlstm_n"][:DEPTH, b])
        m["st_m"] = f(inputs["state_mlstm_m"][:DEPTH, b])
        m["c_k"] = f(inputs["cache_k"][:DEPTH, b])
        m["c_v"] = f(inputs["cache_v"][:DEPTH, b])
        m["rope_c"] = rc
        m["rope_s"] = rs
        in_maps.append(m)
    res = run_bass_kernel_spmd(nc, in_maps, core_ids=list(range(n_cores)))
    R = res.results
    outs = []
    outs.append(np.stack([np.asarray(R[b]["y_p"]) for b in range(n_cores)]))
    outs.append(np.stack([np.asarray(R[b]["y_s"]) for b in range(n_cores)]))
    for g in ["p", "s"]:
        for nm in ["conv_", "lru_", "C_", "n_", "m_", "k_", "v_"]:
            outs.append(np.stack([np.asarray(R[b][nm + g]) for b in range(n_cores)], axis=1))
    return tuple(o.astype(np.float32) for o in outs)


def kernel(**inputs):
    return run(inputs, CFG["NT"], CFG["T"], CFG["DEPTH"], 8)
```
